# Optimizing a Trainium2 kernel written in Bass

```python
import math
import jax, jax.numpy as jnp
from jax import lax
import numpy as np

D_MODEL = 1024
BATCH = 4
SEQ = 4096
DEPTH = 4

HEAD_DIM = 64
BLOCK = 128
A_HEADS = 8
A_KV = 2
WINDOW = 128
B_HEADS = 8
B_KV = 2
GRID_W = 64
C_WIDTH = 512
C_GROUPS = 4
C_GROUP_DIM = C_WIDTH // C_GROUPS
CHUNK = 128
N_BRANCH = 3
BRANCH_WIDTH = 512
ROPE_THETA = 10000.0
MEM_LEN = 256
X_HEADS = 4
X_HEAD_DIM = 128
X_WIDTH = X_HEADS * X_HEAD_DIM
D_FF = 2816
CONV_W = 3
ALPHA = (2 * DEPTH) ** 0.25
BETA = (8 * DEPTH) ** -0.25
LN_EPS = 1e-5
RMS_EPS = 1e-6

A_Q_W = A_HEADS * HEAD_DIM
A_KV_W = A_KV * HEAD_DIM
B_Q_W = B_HEADS * HEAD_DIM
B_KV_W = B_KV * HEAD_DIM
IN_SIZES = (A_Q_W, A_KV_W, A_KV_W, B_Q_W, B_KV_W, B_KV_W, 2 * C_WIDTH, N_BRANCH * D_MODEL)
D_IN = sum(IN_SIZES)

kernel_name = "hybrid_gated_window_axial_gmlp_encoder"


def _split_points(sizes):
    pts, acc = [], 0
    for s in sizes[:-1]:
        acc += s
        pts.append(acc)
    return pts


def layer_norm(x, g, b):
    xf = x.astype(jnp.float32)
    mu = jnp.mean(xf, axis=-1, keepdims=True)
    var = jnp.mean(jnp.square(xf - mu), axis=-1, keepdims=True)
    y = (xf - mu) * lax.rsqrt(var + LN_EPS) * g.astype(jnp.float32) + b.astype(jnp.float32)
    return y.astype(x.dtype)


def rms_norm(x, g):
    xf = x.astype(jnp.float32)
    y = xf * lax.rsqrt(jnp.mean(jnp.square(xf), axis=-1, keepdims=True) + RMS_EPS)
    return (y * g.astype(jnp.float32)).astype(x.dtype)


def rope(x, pos, theta):
    d = x.shape[-1]
    half = d // 2
    inv = theta ** (-jnp.arange(half, dtype=jnp.float32) * (2.0 / d))
    ang = pos.astype(jnp.float32)[:, None] * inv[None, :]
    cos = jnp.cos(ang)[:, None, :]
    sin = jnp.sin(ang)[:, None, :]
    x1 = x[..., :half].astype(jnp.float32)
    x2 = x[..., half:].astype(jnp.float32)
    out = jnp.concatenate([x1 * cos - x2 * sin, x2 * cos + x1 * sin], axis=-1)
    return out.astype(x.dtype)


def window_attention(q, k, v, sink):
    B, S, H, D = q.shape
    KV = k.shape[2]
    G = H // KV
    nb = S // BLOCK
    qb = q.reshape(B, nb, BLOCK, KV, G, D)
    pad = ((0, 0), (BLOCK, BLOCK), (0, 0), (0, 0))
    kp = jnp.pad(k, pad).reshape(B, nb + 2, BLOCK, KV, D)
    vp = jnp.pad(v, pad).reshape(B, nb + 2, BLOCK, KV, D)
    kw = jnp.concatenate([kp[:, :-2], kp[:, 1:-1], kp[:, 2:]], axis=2)
    vw = jnp.concatenate([vp[:, :-2], vp[:, 1:-1], vp[:, 2:]], axis=2)
    s = jnp.einsum('bnqkgd,bnskd->bnkgqs', qb, kw).astype(jnp.float32) / math.sqrt(D)
    blk = jnp.arange(nb, dtype=jnp.int32)[:, None]
    qpos = blk * BLOCK + jnp.arange(BLOCK, dtype=jnp.int32)[None, :]
    kpos = (blk - 1) * BLOCK + jnp.arange(3 * BLOCK, dtype=jnp.int32)[None, :]
    dist = jnp.abs(qpos[:, :, None] - kpos[:, None, :])
    valid = (dist <= WINDOW) & (kpos[:, None, :] >= 0) & (kpos[:, None, :] < S)
    s = jnp.where(valid[None, :, None, None], s, -jnp.inf)
    sink_l = jnp.broadcast_to(sink.astype(jnp.float32).reshape(KV, G)[None, None, :, :, None, None],
                              s.shape[:-1] + (1,))
    p = jax.nn.softmax(jnp.concatenate([s, sink_l], axis=-1), axis=-1)[..., :-1]
    o = jnp.einsum('bnkgqs,bnskd->bnqkgd', p.astype(v.dtype), vw)
    return o.reshape(B, S, H * D)


def dense_block_attention(q, k, v):
    B, S, H, D = q.shape
    KV = k.shape[2]
    G = H // KV
    nb = S // BLOCK
    scale = 1.0 / math.sqrt(D)
    qb = q.reshape(B, nb, BLOCK, KV, G, D).transpose(1, 0, 2, 3, 4, 5)

    def one_block(qblk):
        s = jnp.einsum('bqkgd,bskd->bkgqs', qblk, k).astype(jnp.float32) * scale
        p = jax.nn.softmax(s, axis=-1)
        return jnp.einsum('bkgqs,bskd->bqkgd', p.astype(v.dtype), v)

    o = lax.map(one_block, qb)
    return o.transpose(1, 0, 2, 3, 4, 5).reshape(B, S, H * D)


def spatial_gating(u, v, ln_g, ln_b, w_s, b_s):
    B, S, _ = v.shape
    nc = S // CHUNK
    vc = layer_norm(v, ln_g, ln_b).reshape(B, nc, CHUNK, C_GROUPS, C_GROUP_DIM)
    mixed = jnp.einsum('gij,bnjgc->bnigc', w_s, vc) + b_s.T[None, None, :, :, None]
    return u * mixed.reshape(B, S, C_WIDTH)


def hybrid_mixer(x, pos, row, col, w_in, b_gate, a_sink, b_q_gain, b_k_gain,
                 c_ln_g, c_ln_b, c_ws, c_bs, w_branch, w_mix_out):
    B, S, _ = x.shape
    proj = x @ w_in
    aq, ak, av, bq, bk, bv, cz, gate = jnp.split(proj, _split_points(IN_SIZES), axis=-1)
    aq = rope(aq.reshape(B, S, A_HEADS, HEAD_DIM), pos, ROPE_THETA)
    ak = rope(ak.reshape(B, S, A_KV, HEAD_DIM), pos, ROPE_THETA)
    av = av.reshape(B, S, A_KV, HEAD_DIM)
    out_a = window_attention(aq, ak, av, a_sink)
    half = HEAD_DIM // 2
    bq = rms_norm(bq.reshape(B, S, B_HEADS, HEAD_DIM), b_q_gain)
    bk = rms_norm(bk.reshape(B, S, B_KV, HEAD_DIM), b_k_gain)
    bq = jnp.concatenate([rope(bq[..., :half], row, ROPE_THETA), rope(bq[..., half:], col, ROPE_THETA)], axis=-1)
    bk = jnp.concatenate([rope(bk[..., :half], row, ROPE_THETA), rope(bk[..., half:], col, ROPE_THETA)], axis=-1)
    bv = bv.reshape(B, S, B_KV, HEAD_DIM)
    out_b = dense_block_attention(bq, bk, bv)
    u, v = jnp.split(jax.nn.gelu(cz, approximate=False), 2, axis=-1)
    out_c = spatial_gating(u, v, c_ln_g, c_ln_b, c_ws, c_bs)
    branches = jnp.stack([out_a, out_b, out_c], axis=2)
    gates = jax.nn.sigmoid(gate + b_gate).reshape(B, S, N_BRANCH, D_MODEL)
    merged = (jnp.einsum('bsnc,ncd->bsnd', branches, w_branch) * gates).sum(axis=2)
    return merged @ w_mix_out


def memory_cross_attention(x, mem, wq, wkv, wo):
    B, S, _ = x.shape
    M = mem.shape[1]
    q = (x @ wq).reshape(B, S, X_HEADS, X_HEAD_DIM)
    k, v = jnp.split(mem @ wkv, 2, axis=-1)
    k = k.reshape(B, M, X_HEADS, X_HEAD_DIM)
    v = v.reshape(B, M, X_HEADS, X_HEAD_DIM)
    s = jnp.einsum('bqhd,bmhd->bhqm', q, k).astype(jnp.float32) / math.sqrt(X_HEAD_DIM)
    p = jax.nn.softmax(s, axis=-1)
    o = jnp.einsum('bhqm,bmhd->bqhd', p.astype(v.dtype), v)
    return o.reshape(B, S, X_WIDTH) @ wo


def conv_ffn(x, w_up, conv_k, conv_b, w_down):
    S = x.shape[1]
    h = x @ w_up
    r = CONV_W // 2
    hp = jnp.pad(h, ((0, 0), (r, r), (0, 0)))
    h = sum(hp[:, j:j + S] * conv_k[j] for j in range(CONV_W)) + conv_b
    a, b = jnp.split(h, 2, axis=-1)
    return (jax.nn.gelu(a, approximate=False) * b) @ w_down


def setup_inputs(seed: int = 0) -> dict:
    key = jax.random.key(seed)
    ks = jax.random.split(key, 26)

    def nrm(k, shape, scale):
        return jax.random.normal(k, shape, dtype=jnp.float32) * scale

    L = DEPTH
    return {
        "x": nrm(ks[0], (BATCH, SEQ, D_MODEL), 1.0),
        "mem": nrm(ks[1], (BATCH, MEM_LEN, D_MODEL), 1.0),
        "w_in": nrm(ks[2], (L, D_MODEL, D_IN), D_MODEL ** -0.5),
        "b_gate": nrm(ks[3], (L, N_BRANCH * D_MODEL), 0.1),
        "a_sink": nrm(ks[4], (L, A_HEADS), 1.0),
        "b_q_gain": 1.0 + nrm(ks[5], (L, HEAD_DIM), 0.02),
        "b_k_gain": 1.0 + nrm(ks[6], (L, HEAD_DIM), 0.02),
        "c_ln_g": 1.0 + nrm(ks[7], (L, C_WIDTH), 0.02),
        "c_ln_b": nrm(ks[8], (L, C_WIDTH), 0.02),
        "c_ws": nrm(ks[9], (L, C_GROUPS, CHUNK, CHUNK), CHUNK ** -0.5),
        "c_bs": 1.0 + nrm(ks[10], (L, C_GROUPS, CHUNK), 0.02),
        "w_branch": nrm(ks[11], (L, N_BRANCH, BRANCH_WIDTH, D_MODEL), BRANCH_WIDTH ** -0.5),
        "w_mix_out": nrm(ks[12], (L, D_MODEL, D_MODEL), BETA * D_MODEL ** -0.5),
        "ln1_g": 1.0 + nrm(ks[13], (L, D_MODEL), 0.02),
        "ln1_b": nrm(ks[14], (L, D_MODEL), 0.02),
        "x_wq": nrm(ks[15], (L, D_MODEL, X_WIDTH), D_MODEL ** -0.5),
        "x_wkv": nrm(ks[16], (L, D_MODEL, 2 * X_WIDTH), D_MODEL ** -0.5),
        "x_wo": nrm(ks[17], (L, X_WIDTH, D_MODEL), BETA * X_WIDTH ** -0.5),
        "ln2_g": 1.0 + nrm(ks[18], (L, D_MODEL), 0.02),
        "ln2_b": nrm(ks[19], (L, D_MODEL), 0.02),
        "f_w_up": nrm(ks[20], (L, D_MODEL, 2 * D_FF), D_MODEL ** -0.5),
        "f_conv_k": nrm(ks[21], (L, CONV_W, 2 * D_FF), CONV_W ** -0.5),
        "f_conv_b": nrm(ks[22], (L, 2 * D_FF), 0.02),
        "f_w_down": nrm(ks[23], (L, D_FF, D_MODEL), BETA * D_FF ** -0.5),
        "ln3_g": 1.0 + nrm(ks[24], (L, D_MODEL), 0.02),
        "ln3_b": nrm(ks[25], (L, D_MODEL), 0.02),
    }


def reference(x, mem, w_in, b_gate, a_sink, b_q_gain, b_k_gain, c_ln_g, c_ln_b, c_ws, c_bs,
              w_branch, w_mix_out, ln1_g, ln1_b, x_wq, x_wkv, x_wo, ln2_g, ln2_b,
              f_w_up, f_conv_k, f_conv_b, f_w_down, ln3_g, ln3_b):
    seq = x.shape[1]
    rows = seq // GRID_W
    pos = jnp.arange(seq, dtype=jnp.int32)
    row = jnp.repeat(jnp.arange(rows, dtype=jnp.int32), GRID_W)
    col = jnp.tile(jnp.arange(GRID_W, dtype=jnp.int32), rows)
    for l in range(DEPTH):
        h = hybrid_mixer(x, pos, row, col, w_in[l], b_gate[l], a_sink[l], b_q_gain[l], b_k_gain[l],
                         c_ln_g[l], c_ln_b[l], c_ws[l], c_bs[l], w_branch[l], w_mix_out[l])
        x = layer_norm(ALPHA * x + h, ln1_g[l], ln1_b[l])
        h = memory_cross_attention(x, mem, x_wq[l], x_wkv[l], x_wo[l])
        x = layer_norm(ALPHA * x + h, ln2_g[l], ln2_b[l])
        h = conv_ffn(x, f_w_up[l], f_conv_k[l], f_conv_b[l], f_w_down[l])
        x = layer_norm(ALPHA * x + h, ln3_g[l], ln3_b[l])
    return x
```

```python
import contextlib
import numpy as np
import ml_dtypes
import concourse.bass as bass
import concourse.mybir as mybir
from concourse.bass_utils import run_bass_kernel_spmd

F32 = mybir.dt.float32
BF16 = mybir.dt.bfloat16
AF = mybir.ActivationFunctionType
ALU = mybir.AluOpType
ENGS = ('pe', 'dve', 'act', 'pool', 'sp')

DEPTH = 4
ALPHA = (2 * DEPTH) ** 0.25
LN_EPS = 1e-5
RMS_EPS = 1e-6
NTOK = 2048
DEBUG_STAGE = 9
DEBUG_SUB = 9
DEBUG_R = 9
DEBUG_M = 9
GS = 512
NG = 4
D_FF = 2816


class Prog:
    def __init__(self, nc, n_dma_sems=24):
        self.nc = nc
        self.es = contextlib.ExitStack()
        self.q = {e: [] for e in ENGS}
        self.cnt = {}
        self.seen = {e: {} for e in ENGS}
        self.snap = {}
        self.last_w = {}
        self.readers = {}
        self.sems = {}
        self.epoch = 0
        self.n_dma = {'sp': 16, 'pool': 8, 'act': 4, 'cc': 2}
        self.dma_rr = {'sp': 0, 'pool': 0, 'act': 0, 'cc': 0}
        self.n_wait = 0
        for qn, n in self.n_dma.items():
            for i in range(n):
                self._mk_owner(('d' + qn, i))
        for e in ENGS:
            self._mk_owner((e, 0))

    def _mk_owner(self, o):
        nm = "s_" + "_".join(str(x) for x in o)
        self.sems[o] = self.es.enter_context(self.nc.semaphore(nm))
        self.cnt[o] = 0

    def new_epoch(self):
        self.epoch += 1
        for e in ENGS:
            if e != 'sp':
                self._mk_owner((e, self.epoch))

    def sbuf(self, name, shape, dt):
        return self.es.enter_context(self.nc.sbuf_tensor(name, list(shape), dt))

    def psum(self, name, shape, dt):
        return self.es.enter_context(self.nc.psum_tensor(name, list(shape), dt))

    def _wait(self, E, o, v):
        if self.seen[E].get(o, 0) >= v:
            return
        if o[0] == E:
            if E == 'pe' or E == 'sp':
                return
            if o[1] != self.epoch or v < self.cnt[o] - 1:
                return
        self.q[E].append(('wait', o, v))
        self.n_wait += 1
        self.seen[E][o] = v
        sn = self.snap.get((o, v))
        if sn:
            se = self.seen[E]
            for o2, v2 in sn.items():
                if se.get(o2, 0) < v2:
                    se[o2] = v2

    def _deps(self, reads, writes):
        deps = set()
        for k in reads:
            w = self.last_w.get(k)
            if w is not None:
                deps.add(w)
        for k in writes:
            w = self.last_w.get(k)
            if w is not None:
                deps.add(w)
            for r in self.readers.get(k, ()):
                deps.add(r)
        return deps

    def _commit(self, ident, reads, writes):
        for k in reads:
            self.readers.setdefault(k, []).append(ident)
        for k in writes:
            self.last_w[k] = ident
            self.readers[k] = []

    def op(self, E, fn, reads=(), writes=(), inc=True):
        psr = [k for k in reads if isinstance(k, tuple) and k[0] == 'ps']
        if psr:
            reads = [k for k in reads if not (isinstance(k, tuple) and k[0] == 'ps')]
            writes = list(writes) + psr
        own = (E, self.epoch)
        for (o, v) in sorted(self._deps(reads, writes), key=str):
            self._wait(E, o, v)
        v = self.cnt[own] + 1
        if inc:
            self.cnt[own] = v
            self.snap[(own, v)] = dict(self.seen[E])
        self.q[E].append(('op', fn, own if inc else None, 1))
        self._commit((own, v), reads, writes)

    def dma(self, out, in_, reads=(), writes=(), queue='sp', **kw):
        i = self.dma_rr[queue]
        self.dma_rr[queue] = (i + 1) % self.n_dma[queue]
        own = ('d' + queue, i)
        deps = self._deps(reads, writes)
        if self.cnt[own] > 0:
            deps.add((own, self.cnt[own]))
        for (o, v) in sorted(deps, key=str):
            self._wait(queue, o, v)
        v = self.cnt[own] + 16
        self.cnt[own] = v
        self.snap[(own, v)] = dict(self.seen[queue])
        self.q[queue].append(('op', lambda eng: eng.dma_start(out=out, in_=in_, **kw), own, 16))
        self._commit((own, v), reads, writes)

    def coll(self, fn, reads=(), writes=()):
        i = self.dma_rr['cc']
        self.dma_rr['cc'] = (i + 1) % self.n_dma['cc']
        own = ('dcc', i)
        deps = self._deps(reads, writes)
        if self.cnt[own] > 0:
            deps.add((own, self.cnt[own]))
        for (o, v) in sorted(deps, key=str):
            self._wait('pool', o, v)
        v = self.cnt[own] + 16
        self.cnt[own] = v
        self.snap[(own, v)] = dict(self.seen['pool'])
        self.q['pool'].append(('op', fn, own, 16))
        self._commit((own, v), reads, writes)

    def finish(self):
        for o, c in self.cnt.items():
            if c > 0 and o[0] != 'sp':
                if self.seen['sp'].get(o, 0) < c:
                    self.q['sp'].append(('wait', o, c))
                    self.seen['sp'][o] = c

    def emit(self):
        nc = self.nc
        for o, c in self.cnt.items():
            assert c < 60000, (o, c)
        with nc.Block() as block:
            def replay(E):
                def run(eng):
                    for ent in self.q[E]:
                        if ent[0] == 'wait':
                            eng.wait_ge(self.sems[ent[1]], ent[2])
                        else:
                            ins = ent[1](eng)
                            if ent[2] is not None:
                                ins.then_inc(self.sems[ent[2]], ent[3])
                return run
            block.tensor(replay('pe'))
            block.vector(replay('dve'))
            block.scalar(replay('act'))
            block.gpsimd(replay('pool'))
            block.sync(replay('sp'))
        self.es.close()

    def check(self):
        val = {o: 0 for o in self.cnt}
        pc = {e: 0 for e in ENGS}
        progress = True
        while progress:
            progress = False
            for e in ENGS:
                q = self.q[e]
                while pc[e] < len(q):
                    ent = q[pc[e]]
                    if ent[0] == 'wait':
                        if val[ent[1]] >= ent[2]:
                            pc[e] += 1
                            progress = True
                        else:
                            break
                    else:
                        if ent[2] is not None:
                            val[ent[2]] += ent[3]
                        pc[e] += 1
                        progress = True
        stuck = {e: (pc[e], len(self.q[e]), self.q[e][pc[e]][:3] if pc[e] < len(self.q[e]) else None) for e in ENGS}
        ok = all(pc[e] == len(self.q[e]) for e in ENGS)
        return ok, stuck, {o: (val[o], self.cnt[o]) for o in val if val[o] != self.cnt[o]}

    def stats(self):
        return {e: sum(1 for x in self.q[e] if x[0] == 'op') for e in ENGS}, self.n_wait


def _kmaj(w):
    K, C = w.shape
    return np.ascontiguousarray(w.reshape(K // 128, 128, C).transpose(1, 0, 2)).reshape(128, -1)


def _qreorder(w):
    w4 = w.reshape(w.shape[0], 8, 64)
    order = [h for i in range(4) for h in (i, 4 + i)]
    return w4[:, order, :].reshape(w.shape[0], 512)


def weight_blocks(inp, l):
    w_in = inp['w_in'][l]
    B = {}
    B['kvB'] = _kmaj(w_in[:, 1280:1536])
    B['kvA'] = _kmaj(w_in[:, 512:768])
    B['ws'] = np.ascontiguousarray(inp['c_ws'][l].transpose(2, 0, 1)).reshape(128, 512)
    B['xk'] = _kmaj(inp['x_wkv'][l][:, 0:512])
    B['xv'] = _kmaj(inp['x_wkv'][l][:, 512:1024])
    B['qB'] = _kmaj(_qreorder(w_in[:, 768:1280]))
    B['qA'] = _kmaj(_qreorder(w_in[:, 0:512]))
    B['wu'] = _kmaj(w_in[:, 1536:2048])
    B['wv'] = _kmaj(w_in[:, 2048:2560])
    for nb in range(3):
        for cq in range(2):
            B[f'g{nb}{cq}'] = _kmaj(w_in[:, 2560 + nb * 1024 + cq * 512: 2560 + nb * 1024 + cq * 512 + 512])
            wb = inp['w_branch'][l, nb][:, cq * 512:(cq + 1) * 512]
            if nb < 2:
                a = np.zeros((128, 8, 512), np.float32)
                a[0:64] = wb.reshape(8, 64, 512).transpose(1, 0, 2)
                B[f'br{nb}{cq}'] = a.reshape(128, 4096)
            else:
                B[f'br{nb}{cq}'] = _kmaj(wb)
    for cq in range(2):
        B[f'mo{cq}'] = _kmaj(inp['w_mix_out'][l][:, cq * 512:(cq + 1) * 512])
    B['xq'] = _kmaj(inp['x_wq'][l])
    for cq in range(2):
        B[f'xo{cq}'] = _kmaj(inp['x_wo'][l][:, cq * 512:(cq + 1) * 512])
    up = inp['f_w_up'][l]
    for jq in range(6):
        w = min(512, D_FF - jq * 512)
        B[f'ua{jq}'] = _kmaj(up[:, jq * 512: jq * 512 + w])
        B[f'ub{jq}'] = _kmaj(up[:, D_FF + jq * 512: D_FF + jq * 512 + w])
    for c in range(8):
        B[f'dn{c}'] = _kmaj(inp['f_w_down'][l][:, c * 128:(c + 1) * 128])
    return B


_WOFF = None


def weight_offsets():
    global _WOFF
    if _WOFF is None:
        sizes = [('kvB', 2048), ('kvA', 2048), ('ws', 512), ('xk', 4096), ('xv', 4096), ('qB', 4096), ('qA', 4096),
                 ('wu', 4096), ('wv', 4096)]
        for nb in range(3):
            for cq in range(2):
                sizes.append((f'g{nb}{cq}', 4096))
                sizes.append((f'br{nb}{cq}', 4096 if nb < 2 else 2048))
        sizes += [('mo0', 4096), ('mo1', 4096), ('xq', 4096), ('xo0', 2048), ('xo1', 2048)]
        for jq in range(6):
            w = min(512, D_FF - jq * 512)
            sizes += [(f'ua{jq}', 8 * w), (f'ub{jq}', 8 * w)]
        for c in range(8):
            sizes.append((f'dn{c}', 2816))
        off = 0
        d = {}
        for nm, e in sizes:
            d[nm] = (off, e)
            off += e
        d['_tot'] = off
        _WOFF = d
    return _WOFF


S_BG = 0
S_GQ = 24
S_GK = 25
S_LN = 26
S_CV = 74
S_SINK = 250
S_CLG = 258
S_CLB = 770
S_BS = 1282
NS = 1794


def small_params(inp, l):
    s = np.zeros((128, NS), np.float32)
    s[:, S_BG:S_BG + 24] = inp['b_gate'][l].reshape(24, 128).T
    s[:, S_GQ] = np.tile(inp['b_q_gain'][l], 2)
    s[:, S_GK] = np.tile(inp['b_k_gain'][l], 2)
    for i, nm in enumerate(['ln1_g', 'ln1_b', 'ln2_g', 'ln2_b', 'ln3_g', 'ln3_b']):
        s[:, S_LN + 8 * i: S_LN + 8 * i + 8] = inp[nm][l].reshape(8, 128).T
    ck = inp['f_conv_k'][l]
    for j in range(3):
        s[:, S_CV + 44 * j: S_CV + 44 * j + 44] = ck[j].reshape(44, 128).T
    s[:, S_CV + 132: S_CV + 176] = inp['f_conv_b'][l].reshape(44, 128).T
    s[:, S_SINK:S_SINK + 8] = inp['a_sink'][l][None, :]
    s[:, S_CLG:S_CLG + 512] = inp['c_ln_g'][l][None, :]
    s[:, S_CLB:S_CLB + 512] = inp['c_ln_b'][l][None, :]
    s[:, S_BS:S_BS + 512] = inp['c_bs'][l].reshape(1, 512)
    return s


C_RTA = 0
C_RTB = 128
C_BONES = 256
C_ONES = 384
C_ONESLN = 512
C_MASK = 640
C_FLAG = 1792
C_EPS6 = 1794
C_EPS5 = 1795
NC_ = 1796


def constants(hf):
    c = np.zeros((128, NC_), np.float32)
    R_A = np.zeros((128, 128), np.float32)
    R_B = np.zeros((128, 128), np.float32)
    for i in range(128):
        if i % 64 < 32:
            R_A[i, i + 32] = -1.0
        else:
            R_A[i, i - 32] = 1.0
        if i % 32 < 16:
            R_B[i, i + 16] = -1.0
        else:
            R_B[i, i - 16] = 1.0
    c[:, C_RTA:C_RTA + 128] = R_A.T
    c[:, C_RTB:C_RTB + 128] = R_B.T
    bo = np.zeros((128, 128), np.float32)
    bo[0:64, 0:64] = 1.0 / 64
    bo[64:128, 64:128] = 1.0 / 64
    c[:, C_BONES:C_BONES + 128] = bo
    c[:, C_ONES:C_ONES + 128] = 1.0
    c[:, C_ONESLN:C_ONESLN + 128] = 1.0 / 1024
    ki = np.arange(128)[:, None]
    qi = np.arange(128)[None, :]
    mL = (qi <= ki).astype(np.float32)
    mR = (ki <= qi).astype(np.float32)
    one = np.ones((128, 128), np.float32)
    zero = np.zeros((128, 128), np.float32)
    c[:, C_MASK:C_MASK + 384] = np.concatenate([mL, one, mR], 1)
    c[:, C_MASK + 384:C_MASK + 768] = np.concatenate([zero, one, mR], 1)
    c[:, C_MASK + 768:C_MASK + 1152] = np.concatenate([mL, one, zero], 1)
    c[:, C_FLAG] = 1.0 if hf == 1 else 0.0
    c[:, C_FLAG + 1] = 1.0 if hf == 0 else 0.0
    c[:, C_EPS6] = 1e-6
    c[:, C_EPS5] = 1e-5
    return c


def rope_tables(hf):
    theta = np.float32(10000.0)
    invA = (theta ** (-np.arange(32, dtype=np.float32) * np.float32(2.0 / 64))).astype(np.float32)
    invB = (theta ** (-np.arange(16, dtype=np.float32) * np.float32(2.0 / 32))).astype(np.float32)
    p = np.arange(128)
    d = p % 64

    def tabA(pos):
        ang = pos.astype(np.float32)[None, :] * invA[d % 32][:, None]
        return np.stack([np.cos(ang), np.sin(ang)], 1).astype(np.float32)

    def tabB(pos):
        row = (pos // 64).astype(np.float32)
        col = (pos % 64).astype(np.float32)
        sel = np.where((d < 32)[:, None], row[None, :], col[None, :])
        ang = sel * invB[d % 16][:, None]
        return np.stack([np.cos(ang), np.sin(ang)], 1).astype(np.float32)

    own = np.arange(hf * NTOK, (hf + 1) * NTOK)
    posA = np.concatenate([np.arange(1920, 2048), own, np.arange(2048, 2176)])
    return tabA(posA), tabB(own), tabB(np.arange(4096))


class Builder:
    def __init__(self, mode, layers):
        self.mode = mode
        self.layers = layers
        nL = len(layers)
        nc = bass.Bass("TRN2", target_bir_lowering=False)
        self.nc = nc
        self.P = P = Prog(nc)
        W = weight_offsets()
        dt = nc.dram_tensor
        if mode == 'F':
            self.d_xT32h = dt("xT32", [2, 128, 8, NTOK], F32, kind="ExternalInput").ap()
        else:
            self.d_xT32 = dt("xT32", [128, 8, NTOK], F32, kind="ExternalInput").ap()
        if mode in ('A', 'B', 'F'):
            self.d_wblk = dt("wblk", [nL, 128, W['_tot']], F32, kind="ExternalInput").ap()
            self.d_small = dt("small", [nL, 128, NS], F32, kind="ExternalInput").ap()
            self.d_cst = dt("cst", [128, NC_], F32, kind="ExternalInput").ap()
        if mode == 'F':
            self.d_tabAh = dt("tabA", [2, 128, 2, 2304], F32, kind="ExternalInput").ap()
            self.d_tabBk = dt("tabBk", [128, 2, 4096], F32, kind="ExternalInput").ap()
            self.d_memT = dt("memT", [128, 8, 256], F32, kind="ExternalInput").ap()
        if mode == 'A':
            self.d_tabA = dt("tabA", [128, 2, 2304], F32, kind="ExternalInput").ap()
            self.d_tabBq = dt("tabBq", [128, 2, NTOK], F32, kind="ExternalInput").ap()
            self.d_tabBk = dt("tabBk", [128, 2, 4096], F32, kind="ExternalInput").ap()
            self.d_memT = dt("memT", [128, 8, 256], F32, kind="ExternalInput").ap()
        if mode == 'P':
            self.d_xown = dt("xown", [128, 8, NTOK], BF16, kind="ExternalOutput").ap()
        if mode == 'A':
            self.d_xown = dt("xown", [128, 8, NTOK], BF16, kind="ExternalInput").ap()
            self.d_xall = dt("xall", [128, 8, 4096], BF16, kind="ExternalInput").ap()
            self.d_x2own = dt("x2own", [128, 8, NTOK], BF16, kind="ExternalOutput").ap()
            self.d_xT32o = dt("xT32o", [128, 8, NTOK], F32, kind="ExternalOutput").ap()
        if mode == 'F':
            self.d_xallp = dt("xall_i", [128, 8, 4096], BF16, kind="Internal").ap()
            self.d_x2all = dt("x2all_i", [128, 8, 4096], BF16, kind="Internal").ap()
            self.d_park = dt("park_i", [2, 128, 8, NTOK], F32, kind="Internal").ap()
            self.d_yh = dt("y", [2, 128, 8, NTOK], F32, kind="ExternalOutput").ap()
            self.d_wbf = dt("wbf_i", [nL, 128, W['_tot']], BF16, kind="Internal").ap()
            self.hf = 0
        if mode == 'B':
            self.d_x2own = dt("x2own", [128, 8, NTOK], BF16, kind="ExternalInput").ap()
            self.d_halo = dt("halo", [128, 8, 2], BF16, kind="ExternalInput").ap()
            self.d_xown = dt("xown", [128, 8, NTOK], BF16, kind="ExternalOutput").ap()
            self.d_xT32o = dt("xT32o", [128, 8, NTOK], F32, kind="ExternalOutput").ap()
        self.alloc()

    def alloc(self):
        P = self.P
        m = self.mode
        self.xT32 = P.sbuf("xT32s", [128, 8, NTOK], F32)
        self.ps = [P.psum(f"ps{i}", [128, 512], F32) for i in range(8)]
        self.xg = [P.sbuf(f"xg{i}", [128, 8, 514], BF16) for i in range(2)]
        if m == 'P':
            return
        self.ring = [P.sbuf(f"ring{i}", [128, 4096], BF16) for i in range(3)]
        self.small = P.sbuf("small_s", [128, NS], F32)
        self.cst = P.sbuf("cst_s", [128, NC_ - C_ONES], F32)
        self.cstb = P.sbuf("cstb", [128, 512], BF16)
        self.sinkexp = P.sbuf("sinkexp", [128, 8], F32)
        self.arena = P.sbuf("arena", [128, 28, 512], BF16)
        self.T = [P.sbuf(f"t{i}", [128, 512], F32) for i in range(8)]
        self.uT = self.arena[:, 24:28, :]
        self.wkv = self.arena[:, 0:4, :].rearrange("p a (b c) -> p (a b) c", c=256)
        self.WKV_KEYS = [('ar', i) for i in range(4)]
        self.Bt = [P.sbuf(f"b{i}", [128, 512], BF16) for i in range(4)]
        if m in ('A', 'F'):
            self.KB = P.sbuf("KB", [128, 4096], BF16)
            self.VB = P.sbuf("VB", [128, 32, 2, 65], BF16)
            self.KA = P.sbuf("KA", [128, 2304], BF16)
            self.VA = P.sbuf("VA", [128, 18, 2, 65], BF16)
            self.wsT = P.sbuf("wsT", [128, 4, 128], BF16)
            self.tabs = [P.sbuf(f"tab{i}", [128, 2, 512], F32) for i in range(1)]
            self.pt = [P.sbuf(f"pt{i}", [128, 512], BF16) for i in range(3)]
            self.qpad = [P.sbuf(f"qpad{i}", [128, 512], BF16) for i in range(2)]
            self.KmT = P.sbuf("KmT", [128, 4, 256], BF16)
            self.Vm = P.sbuf("Vm", [128, 2, 512], BF16)
            self.mv = P.sbuf("mv", [128, 16], F32)
        if m == 'B':
            self.hext = [P.sbuf(f"hext{i}", [128, 514], F32) for i in range(2)]
            self.hext_keys = [[('hext', 0)], [('hext', 1)]]
            self.halo_s = P.sbuf("halo_s", [128, 8, 2], BF16)
        if m == 'F':
            self.hext = [self.arena[:, 22:25, :].rearrange("p a b -> p (a b)").bitcast(F32),
                         self.arena[:, 25:28, :].rearrange("p a b -> p (a b)").bitcast(F32)]
            self.hext_keys = [[('ar', i) for i in (22, 23, 24)], [('ar', i) for i in (25, 26, 27)]]
            self.halo_s = P.sbuf("halo_s", [128, 8, 2], BF16)
        self.tabi = 0
        self.wnext = 0
        self.wcur = 0
        self.wplan = []

    def plan_weights(self):
        plan = []
        nh = 2 if self.mode == 'F' else 1
        for li in range(len(self.layers)):
            if self.mode in ('A', 'F'):
                plan += [(li, 'xk'), (li, 'xv')]
                for n in range(NG * nh):
                    plan += [(li, 'qB'), (li, 'qA'), (li, 'wu'), (li, 'wv')]
                    for cq in range(2):
                        for nb in range(3):
                            plan += [(li, f'g{nb}{cq}'), (li, f'br{nb}{cq}')]
                    plan += [(li, 'mo0'), (li, 'mo1'), (li, 'xq'), (li, 'xo0'), (li, 'xo1')]
            if self.mode in ('B', 'F'):
                for n in range(NG * nh):
                    for jq in range(6):
                        plan += [(li, f'ua{jq}'), (li, f'ub{jq}')]
                    for c in range(8):
                        plan.append((li, f'dn{c}'))
        self.wplan = plan

    def _issue_wload(self, k):
        li, nm = self.wplan[k]
        off, e = weight_offsets()[nm]
        slot = k % 3
        self.P.dma(self.ring[slot][:, 0:e], self.d_wbf[li, :, off:off + e], reads=[('wbf', li)],
                   writes=[('ring', slot)], queue='sp')

    def wget(self, li, nm):
        k = self.wcur
        assert self.wplan[k] == (li, nm), (self.wplan[k], li, nm)
        while self.wnext < min(len(self.wplan), k + 2):
            self._issue_wload(self.wnext)
            self.wnext += 1
        self.wcur += 1
        slot = k % 3
        e = weight_offsets()[nm][1]
        return self.ring[slot][:, 0:e], ('ring', slot)

    def mm(self, out, lhsT, rhs, start, stop, reads, writes, inc):
        self.P.op('pe', lambda e: e.matmul(out, lhsT=lhsT, rhs=rhs, start=start, stop=stop),
                  reads=reads, writes=writes, inc=inc)

    def act(self, out, in_, func, reads, writes, **kw):
        self.P.op('act', lambda e: e.activation(out=out, in_=in_, func=func, **kw), reads=reads, writes=writes)

    def tt(self, out, in0, in1, op, reads, writes, eng='dve'):
        self.P.op(eng, lambda e: e.tensor_tensor(out=out, in0=in0, in1=in1, op=op), reads=reads, writes=writes)

    def ts(self, out, in0, s1, s2, op0, op1, reads, writes, eng='dve'):
        if op1 is None:
            self.P.op(eng, lambda e: e.tensor_scalar(out=out, in0=in0, scalar1=s1, scalar2=None, op0=op0),
                      reads=reads, writes=writes)
        else:
            self.P.op(eng, lambda e: e.tensor_scalar(out=out, in0=in0, scalar1=s1, scalar2=s2, op0=op0, op1=op1),
                      reads=reads, writes=writes)

    def stt(self, out, in0, scalar, in1, op0, op1, reads, writes):
        self.P.op('dve', lambda e: e.scalar_tensor_tensor(out=out, in0=in0, scalar=scalar, in1=in1, op0=op0, op1=op1),
                  reads=reads, writes=writes)

    def powact(self, out, in_, expo, reads, writes, bias=None, extra_reads=()):
        if bias is None:
            self.act(out, in_, AF.Ln, reads=list(reads), writes=list(writes))
        else:
            self.act(out, in_, AF.Ln, reads=list(reads) + list(extra_reads), writes=list(writes), bias=bias)
        self.act(out, out, AF.Exp, reads=list(writes), writes=list(writes), scale=float(expo))

    def cb(self, col):
        return self.cstb[:, col:col + 128]

    def cf(self, col, n, rows=slice(None)):
        return self.cst[rows, col - C_ONES:col - C_ONES + n]

    def load_common(self):
        P = self.P
        P.dma(self.cst[:], self.d_cst[:, C_ONES:NC_], writes=['cst'])
        P.dma(self.cstb[:], self.d_cst[:, 0:512], writes=['cstb'], queue='pool')

    def load_layer_small(self, li):
        P = self.P
        P.dma(self.small[:], self.d_small[li, :, :], writes=['small'])
        if self.mode in ('A', 'F'):
            self.act(self.sinkexp[:], self.small[:, S_SINK:S_SINK + 8], AF.Exp, reads=['small'], writes=['sinkexp'])

    def load_state(self):
        for c in range(8):
            self.P.dma(self.xT32[:, c, :], self.d_xT32[:, c, :], writes=[('x32', c, n) for n in range(NG)])

    def store_state(self, dst):
        for c in range(8):
            self.P.dma(dst[:, c, :], self.xT32[:, c, :], reads=[('x32', c, n) for n in range(NG)], writes=[('dst32', c)])

    def rope_evac(self, psb, N, kind, tab, tabkey, out_ap, out_key, gcol=None):
        ps = self.ps
        T, Bt = self.T, self.Bt
        src = ps[psb][:, 0:N]
        pk = ('ps', psb)
        RT = self.cb(C_RTA if kind == 'A' else C_RTB)
        R = DEBUG_R
        if kind == 'A':
            self.act(Bt[0][:, 0:N], src, AF.Copy, reads=[pk], writes=['b0'])
        else:
            g = self.small[:, gcol:gcol + 1]
            self.act(Bt[0][:, 0:N], src, AF.Identity, reads=[pk, 'small'], writes=['b0'], scale=g)
            if R >= 2:
                self.act(Bt[1][:, 0:N], src, AF.Square, reads=[pk], writes=['b1'])
        if R < 3:
            return
        rb = 6
        self.mm(ps[rb][:, 0:N], lhsT=RT, rhs=Bt[0][:, 0:N], start=True, stop=True,
                reads=['b0', 'cstb'], writes=[('ps', rb)], inc=True)
        if kind == 'B':
            mb = 7
            self.mm(ps[mb][:, 0:N], lhsT=self.cb(C_BONES), rhs=Bt[1][:, 0:N], start=True, stop=True,
                    reads=['b1', 'cstb'], writes=[('ps', mb)], inc=True)
        if R < 4:
            return
        cos = tab[:, 0, 0:N]
        sin = tab[:, 1, 0:N]
        if kind == 'A':
            self.tt(T[0][:, 0:N], src, cos, ALU.mult, reads=[pk, tabkey], writes=['t0'])
        else:
            self.stt(T[0][:, 0:N], src, g, cos, ALU.mult, ALU.mult, reads=[pk, tabkey, 'small'], writes=['t0'])
        if R < 5:
            return
        self.tt(T[1][:, 0:N], ps[rb][:, 0:N], sin, ALU.mult, reads=[('ps', rb), tabkey], writes=['t1'])
        if kind == 'A':
            self.tt(out_ap, T[0][:, 0:N], T[1][:, 0:N], ALU.add, reads=['t0', 't1'], writes=[out_key])
        else:
            self.tt(T[2][:, 0:N], T[0][:, 0:N], T[1][:, 0:N], ALU.add, reads=['t0', 't1'], writes=['t2'])
            if R < 6:
                return
            self.powact(T[3][:, 0:N], ps[7][:, 0:N], -0.5, [('ps', 7)], ['t3'], bias=self.cf(C_EPS6, 1), extra_reads=['cst'])
            if R < 7:
                return
            self.tt(out_ap, T[2][:, 0:N], T[3][:, 0:N], ALU.mult, reads=['t2', 't3'], writes=[out_key])

    def load_tab(self, src_ap):
        i = self.tabi
        n = src_ap.shape[-1]
        self.P.dma(self.tabs[i][:, :, 0:n], src_ap, writes=[('tab', i)])
        return self.tabs[i], ('tab', i)

    def xall(self, t0, n):
        if self.mode == 'F':
            return self.d_xallp[:, :, t0:t0 + n]
        return self.d_xall[:, :, t0:t0 + n]

    def xall_keys(self, t0, n):
        if self.mode == 'F':
            return [('xown', t0 // NTOK, (t0 % NTOK) // GS)]
        return []

    def xown_ap(self, n):
        if self.mode == 'F':
            return self.d_xallp[:, :, self.hf * NTOK + n * GS: self.hf * NTOK + (n + 1) * GS]
        return self.d_xown[:, :, n * GS:(n + 1) * GS]

    def xown_key(self, n):
        return ('xown', self.hf, n) if self.mode == 'F' else ('xown', n)

    def tabA_ap(self, c0, n):
        if self.mode == 'F':
            return self.d_tabAh[self.hf][:, :, c0:c0 + n]
        return self.d_tabA[:, :, c0:c0 + n]

    def tabBq_ap(self, n):
        if self.mode == 'F':
            t0 = self.hf * NTOK + n * GS
            return self.d_tabBk[:, :, t0:t0 + GS]
        return self.d_tabBq[:, :, n * GS:(n + 1) * GS]

    def kv_block(self, xsrc, xkey, N, kind, tab_ap, kdst, kkey, V, vt0, vkeyf):
        ps = self.ps
        tab, tkey = self.load_tab(tab_ap)
        for kc in range(8):
            self.mm(ps[0][:, 0:N], lhsT=self.wkv[:, kc, 0:128], rhs=xsrc(kc), start=(kc == 0), stop=(kc == 7),
                    reads=self.WKV_KEYS + [xkey], writes=[('ps', 0)], inc=(kc == 7))
        if DEBUG_SUB >= 2:
            self.rope_evac(0, N, kind, tab, tkey, kdst, kkey, gcol=S_GK)
        for t in range(N // 128 if DEBUG_SUB >= 3 else 0):
            pb = 1 + (t % 2)
            for kc in range(8):
                self.mm(ps[pb][:, 0:128], lhsT=xsrc(kc)[:, t * 128:(t + 1) * 128], rhs=self.wkv[:, kc, 128:256],
                        start=(kc == 0), stop=(kc == 7), reads=self.WKV_KEYS + [xkey], writes=[('ps', pb)], inc=(kc == 7))
            vt = vt0 + t
            self.P.op('act', lambda e, pb=pb, vt=vt: e.copy(
                out=V[:, vt, :, 0:64], in_=ps[pb][:, 0:128].rearrange("p (g d) -> p g d", g=2)),
                reads=[('ps', pb)], writes=[vkeyf(vt)])

    def kv_passes(self, li, do_b=True, do_a=True):
        P = self.P
        W = weight_offsets()
        if do_b:
            self.kv_pass_b(li)
        if do_a:
            self.kv_pass_a(li)

    def kv_pass_b(self, li):
        P = self.P
        W = weight_offsets()
        off, e = W['kvB']
        P.dma(self.arena[:, 0:4, :].rearrange("p a b -> p (a b)"), self.d_wbf[li, :, off:off + e], reads=[('wbf', li)], writes=self.WKV_KEYS)
        for tg in range(8):
            xb = self.xg[tg % 2]
            xk = ('xg', tg % 2)
            P.dma(xb[:, :, 0:512], self.xall(tg * 512, 512), reads=self.xall_keys(tg * 512, 512), writes=[xk])
            self.kv_block(lambda kc, xb=xb: xb[:, kc, 0:512], xk, 512, 'B',
                          self.d_tabBk[:, :, tg * 512:(tg + 1) * 512],
                          self.KB[:, tg * 512:(tg + 1) * 512], ('KB', tg), self.VB, tg * 4, lambda vt: ('VB', vt // 4))

    def kv_pass_a(self, li):
        P = self.P
        W = weight_offsets()
        off, e = W['kvA']
        P.dma(self.arena[:, 0:4, :].rearrange("p a b -> p (a b)"), self.d_wbf[li, :, off:off + e], reads=[('wbf', li)], writes=self.WKV_KEYS)
        for n in range(NG):
            xb = self.xg[n % 2]
            xk = ('xg', n % 2)
            P.dma(xb[:, :, 0:512], self.xown_ap(n), reads=[self.xown_key(n)], writes=[xk])
            self.kv_block(lambda kc, xb=xb: xb[:, kc, 0:512], xk, 512, 'A',
                          self.tabA_ap(128 + n * 512, 512),
                          self.KA[:, 128 + n * 512:128 + (n + 1) * 512], ('KA', 1 + n), self.VA, 1 + n * 4,
                          lambda vt: ('VA', vt))
        for side in range(2):
            xb = self.xg[side]
            xk = ('xg', side)
            g0 = 1920 if side == 0 else 2048
            P.dma(xb[:, :, 0:128], self.xall(g0, 128), reads=self.xall_keys(g0, 128), writes=[xk])
            tcol = 0 if side == 0 else 2176
            self.kv_block(lambda kc, xb=xb: xb[:, kc, 0:128], xk, 128, 'A',
                          self.tabA_ap(tcol, 128),
                          self.KA[:, tcol:tcol + 128], ('KA', 0 if side == 0 else 5), self.VA,
                          0 if side == 0 else 17, lambda vt: ('VA', vt))

    def ka_keys(self, J):
        if J == 0:
            return ('KA', 0)
        if J == 17:
            return ('KA', 5)
        return ('KA', 1 + (J - 1) // 4)

    def mem_kv(self, li):
        ps = self.ps
        self.P.dma(self.xg[1][:, :, 0:256], self.d_memT[:, :, :], writes=[('xg', 1)], queue='pool')
        wk, wkk = self.wget(li, 'xk')
        wk = wk.rearrange("p (k c) -> p k c", k=8)
        for h in range(4):
            pb = h % 2
            for kc in range(8):
                self.mm(ps[pb][:, 0:256], lhsT=wk[:, kc, h * 128:(h + 1) * 128], rhs=self.xg[1][:, kc, 0:256],
                        start=(kc == 0), stop=(kc == 7), reads=[wkk, ('xg', 1)], writes=[('ps', pb)], inc=(kc == 7))
            self.act(self.KmT[:, h, :], ps[pb][:, 0:256], AF.Copy, reads=[('ps', pb)], writes=['KmT'])
        wv, wvk = self.wget(li, 'xv')
        wv = wv.rearrange("p (k c) -> p k c", k=8)
        for mt in range(2):
            pb = 2 + mt
            for kc in range(8):
                self.mm(ps[pb][:, 0:512], lhsT=self.xg[1][:, kc, mt * 128:(mt + 1) * 128], rhs=wv[:, kc, :],
                        start=(kc == 0), stop=(kc == 7), reads=[wvk, ('xg', 1)], writes=[('ps', pb)], inc=(kc == 7))
            self.act(self.Vm[:, mt, :], ps[pb][:, 0:512], AF.Copy, reads=[('ps', pb)], writes=['Vm'])

    def attn_norm(self, ob, slot, sink_h=None):
        ps, T = self.ps, self.T
        ok = ('ps', ob)
        rd = T[4]
        if sink_h is None:
            self.powact(rd[64:65, :], ps[ob][64:65, :], -1.0, [ok], ['t4'])
        else:
            self.powact(rd[64:65, :], ps[ob][64:65, :], -1.0, [ok], ['t4'],
                        bias=self.sinkexp[64:65, sink_h:sink_h + 1], extra_reads=['sinkexp'])
        bb = 6
        self.mm(ps[bb][0:64, :], lhsT=self.cf(C_ONES, 64, slice(64, 65)), rhs=rd[64:65, :], start=True, stop=True,
                reads=['t4', 'cst'], writes=[('ps', bb)], inc=True)
        self.act(T[5][0:64, :], ps[ob][0:64, :], AF.Copy, reads=[ok], writes=['t5'])
        self.tt(self.arena[0:64, slot, :], T[5][0:64, :], ps[bb][0:64, :], ALU.mult,
                reads=['t5', ('ps', bb)], writes=[('ar', slot)])

    def ln_accum(self, c, n, hb):
        ps, T = self.ps, self.T
        zs = self.xT32[:, c, n * GS:(n + 1) * GS]
        zk = ('x32', c, n)
        self.stt(zs, zs, ALPHA, ps[hb][:, :], ALU.mult, ALU.add, reads=[('ps', hb), zk], writes=[zk])
        sq = T[6 + (c % 2)]
        sqk = 't6' if c % 2 == 0 else 't7'
        self.act(sq[:], zs, AF.Square, reads=[zk], writes=[sqk])
        ones = self.cf(C_ONESLN, 128)
        self.mm(ps[6][:, :], lhsT=ones, rhs=zs, start=(c == 0), stop=(c == 7), reads=[zk, 'cst'],
                writes=[('ps', 6)], inc=False)
        self.mm(ps[7][:, :], lhsT=ones, rhs=sq[:], start=(c == 0), stop=(c == 7), reads=[sqk, 'cst'],
                writes=[('ps', 7)], inc=True)

    def ln_finish(self, n, lncol, dst_fn):
        ps, T = self.ps, self.T
        self.act(T[0][:], ps[6][:, :], AF.Square, reads=[('ps', 6)], writes=['t0'])
        self.stt(T[1][:], ps[7][:, :], LN_EPS, T[0][:], ALU.add, ALU.subtract, reads=[('ps', 7), 't0'], writes=['t1'])
        self.powact(T[2][:], T[1][:], -0.5, ['t1'], ['t2'])
        self.stt(T[3][:], ps[6][:, :], -1.0, T[2][:], ALU.mult, ALU.mult, reads=[('ps', 6), 't2'], writes=['t3'])
        for c in range(8):
            zs = self.xT32[:, c, n * GS:(n + 1) * GS]
            zk = ('x32', c, n)
            a = T[4 + (c % 2)]
            ak = 't4' if c % 2 == 0 else 't5'
            self.tt(a[:], zs, T[2][:], ALU.mult, reads=[zk, 't2'], writes=[ak])
            self.tt(a[:], a[:], T[3][:], ALU.add, reads=[ak, 't3'], writes=[ak])
            g = self.small[:, lncol + c:lncol + c + 1]
            b = self.small[:, lncol + 8 + c:lncol + 8 + c + 1]
            self.act(zs, a[:], AF.Identity, reads=[ak, 'small'], writes=[zk], scale=g, bias=b)
            dst, dk = dst_fn(c)
            self.ts(dst, a[:], g, b, ALU.mult, ALU.add, reads=[ak, 'small'], writes=[dk])

    def mixer_group(self, li, n):
        P, ps, T, Bt = self.P, self.ps, self.T, self.Bt
        ar = self.arena
        xb = self.xg[0]
        xk = ('xg', 0)
        P.dma(xb[:, :, 0:512], self.xown_ap(n), reads=[self.xown_key(n)], writes=[xk])
        xs = lambda kc: xb[:, kc, 0:512]
        wq, wqk = self.wget(li, 'qB')
        wq = wq.rearrange("p (k c) -> p k c", k=8)
        tab, tkey = self.load_tab(self.tabBq_ap(n))
        SB = [1, 2, 3]
        pending = None
        QB = [(Bt[2], 'b2'), (Bt[3], 'b3')]

        def prep_b(i):
            for kc in range(8):
                self.mm(ps[0][:, :], lhsT=wq[:, kc, i * 128:(i + 1) * 128], rhs=xs(kc), start=(kc == 0), stop=(kc == 7),
                        reads=[wqk, xk], writes=[('ps', 0)], inc=(kc == 7))
            self.rope_evac(0, 512, 'B', tab, tkey, QB[i % 2][0][:], QB[i % 2][1], gcol=S_GQ)
        prep_b(0)
        for i in range(4):
            qb, qbk = QB[i % 2]

            for gi in range(2):
                P.op('pool', lambda e, gi=gi, qb=qb: e.tensor_copy(out=self.qpad[gi][gi * 64:(gi + 1) * 64, :],
                                                                   in_=qb[gi * 64:(gi + 1) * 64, :]),
                     reads=[qbk], writes=[('qp', gi)])

            def st_b(g, kt, qb=qb, qbk=qbk):
                sb = SB[kt % 3]
                self.mm(ps[sb][:, :], lhsT=self.KB[:, kt * 128:(kt + 1) * 128],
                        rhs=self.qpad[g][:, :], start=True, stop=True,
                        reads=[('KB', kt // 4), ('qp', g)], writes=[('ps', sb)], inc=True)
            for g in range(2):
                ob = 4 if g == 0 else 5
                st_b(g, 0)
                st_b(g, 1)
                for kt in range(32):
                    sb = SB[kt % 3]
                    pt = self.pt[kt % 3]
                    pk = ('pt', kt % 3)
                    self.act(pt[:], ps[sb][:, :], AF.Exp, reads=[('ps', sb)], writes=[pk], scale=0.125)
                    if kt + 2 < 32:
                        st_b(g, kt + 2)
                    if g == 0:
                        self.mm(ps[ob][:, :], lhsT=self.VB[:, kt].rearrange("p g c -> p (g c)")[:, 0:128], rhs=pt[:],
                                start=(kt == 0), stop=(kt == 31),
                                reads=[('VB', kt // 4), pk], writes=[('ps', ob)], inc=(kt == 31))
                    else:
                        self.mm(ps[ob][0:65, :], lhsT=self.VB[:, kt, g, :], rhs=pt[:], start=(kt == 0), stop=(kt == 31),
                                reads=[('VB', kt // 4), pk], writes=[('ps', ob)], inc=(kt == 31))
                    if kt == 6 and pending is not None:
                        self.attn_norm(*pending)
                        pending = None
                    if g == 1 and kt == 12 and i + 1 < 4:
                        prep_b(i + 1)
                if pending is not None:
                    self.attn_norm(*pending)
                pending = (ob, 8 + g * 4 + i, None)
        if pending is not None:
            self.attn_norm(*pending)
            pending = None
        wq, wqk = self.wget(li, 'qA')
        wq = wq.rearrange("p (k c) -> p k c", k=8)
        tab, tkey = self.load_tab(self.tabA_ap(128 + n * GS, GS))
        def prep_a(i):
            for kc in range(8):
                self.mm(ps[0][:, :], lhsT=wq[:, kc, i * 128:(i + 1) * 128], rhs=xs(kc), start=(kc == 0), stop=(kc == 7),
                        reads=[wqk, xk], writes=[('ps', 0)], inc=(kc == 7))
            self.rope_evac(0, 512, 'A', tab, tkey, QB[i % 2][0][:], QB[i % 2][1])
        prep_a(0)
        for i in range(4):
            qa, qak = QB[i % 2]

            for gi in range(2):
                P.op('pool', lambda e, gi=gi, qa=qa: e.tensor_copy(out=self.qpad[gi][gi * 64:(gi + 1) * 64, :],
                                                                   in_=qa[gi * 64:(gi + 1) * 64, :]),
                     reads=[qak], writes=[('qp', gi)])

            def st_a(it, qa=qa, qak=qak):
                g, jb = it // 4, it % 4
                J = n * 4 + jb
                sb = SB[it % 3]
                for r in range(3):
                    self.mm(ps[sb][:, r * 128:(r + 1) * 128],
                            lhsT=self.KA[:, (J + r) * 128:(J + r + 1) * 128],
                            rhs=self.qpad[g][:, jb * 128:(jb + 1) * 128], start=True, stop=True,
                            reads=[self.ka_keys(J + r), ('qp', g)], writes=[('ps', sb)], inc=(r == 2))
            st_a(0)
            st_a(1)
            for it in range(8):
                g, jb = it // 4, it % 4
                J = n * 4 + jb
                ob = 4 if g == 0 else 5
                sb = SB[it % 3]
                e32 = T[6 + (it % 2)]
                ek = 't6' if it % 2 == 0 else 't7'
                self.act(e32[:, 0:384], ps[sb][:, 0:384], AF.Exp, reads=[('ps', sb)], writes=[ek], scale=0.125)
                first_blk = (n == 0 and jb == 0)
                last_blk = (n == NG - 1 and jb == 3)
                if first_blk and (self.mode != 'F' or self.hf == 0):
                    mcol = C_MASK + 384
                elif last_blk and (self.mode != 'F' or self.hf == 1):
                    mcol = C_MASK + 768
                else:
                    mcol = C_MASK
                pt = self.pt[it % 3]
                pk = ('pt', it % 3)
                self.tt(pt[:, 0:384], e32[:, 0:384], self.cf(mcol, 384), ALU.mult, reads=[ek, 'cst'], writes=[pk])
                if it + 2 < 8:
                    st_a(it + 2)
                for r in range(3):
                    if g == 0:
                        self.mm(ps[ob][:, jb * 128:(jb + 1) * 128],
                                lhsT=self.VA[:, J + r].rearrange("p g c -> p (g c)")[:, 0:128],
                                rhs=pt[:, r * 128:(r + 1) * 128], start=(r == 0), stop=(r == 2),
                                reads=[('VA', J + r), pk], writes=[('ps', ob)], inc=(r == 2))
                    else:
                        self.mm(ps[ob][0:65, jb * 128:(jb + 1) * 128], lhsT=self.VA[:, J + r, g, :],
                                rhs=pt[:, r * 128:(r + 1) * 128], start=(r == 0), stop=(r == 2),
                                reads=[('VA', J + r), pk], writes=[('ps', ob)], inc=(r == 2))
                if it == 1 and pending is not None:
                    self.attn_norm(*pending)
                    pending = None
                if it == 3 and i + 1 < 4:
                    prep_a(i + 1)
                if jb == 3:
                    if pending is not None:
                        self.attn_norm(*pending)
                    pending = (ob, g * 4 + i, g * 4 + i)
        if pending is not None:
            self.attn_norm(*pending)
            pending = None
        wu, wuk = self.wget(li, 'wu')
        wu = wu.rearrange("p (k c) -> p k c", k=8)
        for c in range(4):
            pb = c % 2
            for kc in range(8):
                self.mm(ps[pb][:, :], lhsT=wu[:, kc, c * 128:(c + 1) * 128], rhs=xs(kc), start=(kc == 0), stop=(kc == 7),
                        reads=[wuk, xk], writes=[('ps', pb)], inc=(kc == 7))
            self.act(self.uT[:, c, :], ps[pb][:, :], AF.Gelu, reads=[('ps', pb)], writes=[('ar', 24 + c)])
        wv, wvk = self.wget(li, 'wv')
        wv = wv.rearrange("p (k c) -> p k c", k=8)
        for t in range(4):
            pb = 2 + (t % 2)
            for kc in range(8):
                self.mm(ps[pb][:, :], lhsT=xb[:, kc, t * 128:(t + 1) * 128], rhs=wv[:, kc, :], start=(kc == 0),
                        stop=(kc == 7), reads=[wvk, xk], writes=[('ps', pb)], inc=(kc == 7))
            v32 = T[0 + (t % 2)]
            vk = 't0' if t % 2 == 0 else 't1'
            self.act(v32[:], ps[pb][:, :], AF.Gelu, reads=[('ps', pb)], writes=[vk])
            st = self.mv[:, 0:6]
            P.op('dve', lambda e, v32=v32: e.bn_stats(out=self.mv[:, 0:6], in_=v32[:]), reads=[vk], writes=['mv6'])
            P.op('dve', lambda e: e.bn_aggr(out=self.mv[:, 8:10], in_=self.mv[:, 0:6]), reads=['mv6'], writes=['mv2'])
            self.powact(self.mv[:, 10:11], self.mv[:, 9:10], -0.5, ['mv2'], ['mvr'], bias=self.cf(C_EPS5, 1), extra_reads=['cst'])
            self.ts(T[2][:], v32[:], self.mv[:, 8:9], self.mv[:, 10:11], ALU.subtract, ALU.mult,
                    reads=[vk, 'mv2', 'mvr'], writes=['t2'])
            self.tt(T[3][:], T[2][:], self.small[:, S_CLG:S_CLG + 512], ALU.mult, reads=['t2', 'small'], writes=['t3'])
            vc = Bt[3]
            self.tt(vc[:], T[3][:], self.small[:, S_CLB:S_CLB + 512], ALU.add, reads=['t3', 'small'], writes=['b3'])
            for g4 in range(4):
                self.mm(ps[4 + g4][:, t * 128:(t + 1) * 128], lhsT=vc[:, g4 * 128:(g4 + 1) * 128], rhs=self.wsT[:, g4, :],
                        start=True, stop=True, reads=['b3', 'wsT'], writes=[('ps', 4 + g4)], inc=(g4 == 3))
        for g4 in range(4):
            bsv = self.small[:, S_BS + g4 * 128:S_BS + (g4 + 1) * 128]
            a = T[4 + (g4 % 2)]
            ak = 't4' if g4 % 2 == 0 else 't5'
            for t in range(4):
                self.tt(a[:, t * 128:(t + 1) * 128], ps[4 + g4][:, t * 128:(t + 1) * 128], bsv, ALU.add,
                        reads=[('ps', 4 + g4), 'small'], writes=[ak])
            self.tt(ar[:, 16 + g4, :], a[:], self.uT[:, g4, :], ALU.mult, reads=[ak, ('ar', 24 + g4)], writes=[('ar', 16 + g4)])
        if DEBUG_M < 4:
            return
        acc = [T[0], T[1], T[2], T[3]]
        acck = ['t0', 't1', 't2', 't3']
        for cq in range(2):
            for nb in range(3):
                wg, wgk = self.wget(li, f'g{nb}{cq}')
                wg = wg.rearrange("p (k c) -> p k c", k=8)
                wb, wbk = self.wget(li, f'br{nb}{cq}')
                if nb < 2:
                    wb = wb.rearrange("p (k c) -> p k c", k=8)
                else:
                    wb = wb.rearrange("p (k c) -> p k c", k=4)
                for cc in range(4):
                    c = cq * 4 + cc
                    gb = cc % 2
                    for kc in range(8):
                        self.mm(ps[gb][:, :], lhsT=wg[:, kc, cc * 128:(cc + 1) * 128], rhs=xs(kc), start=(kc == 0),
                                stop=(kc == 7), reads=[wgk, xk], writes=[('ps', gb)], inc=(kc == 7))
                    sg = T[4 + (cc % 2)]
                    sgk = 't4' if cc % 2 == 0 else 't5'
                    self.act(sg[:], ps[gb][:, :], AF.Sigmoid, reads=[('ps', gb), 'small'], writes=[sgk],
                             bias=self.small[:, S_BG + nb * 8 + c:S_BG + nb * 8 + c + 1])
                    bb = 2 + (cc % 2)
                    if nb < 2:
                        for h in range(8):
                            self.mm(ps[bb][:, :], lhsT=wb[0:64, h, cc * 128:(cc + 1) * 128], rhs=ar[0:64, nb * 8 + h, :],
                                    start=(h == 0), stop=(h == 7), reads=[wbk, ('ar', nb * 8 + h)], writes=[('ps', bb)],
                                    inc=(h == 7))
                    else:
                        for kc in range(4):
                            self.mm(ps[bb][:, :], lhsT=wb[:, kc, cc * 128:(cc + 1) * 128], rhs=ar[:, 16 + kc, :],
                                    start=(kc == 0), stop=(kc == 3), reads=[wbk, ('ar', 16 + kc)], writes=[('ps', bb)],
                                    inc=(kc == 3))
                    mk = acck[cc]
                    ma = acc[cc][:]
                    if nb == 0:
                        self.tt(ma, ps[bb][:, :], sg[:], ALU.mult, reads=[('ps', bb), sgk], writes=[mk])
                    else:
                        pr = T[6 + (cc % 2)]
                        prk = 't6' if cc % 2 == 0 else 't7'
                        self.tt(pr[:], ps[bb][:, :], sg[:], ALU.mult, reads=[('ps', bb), sgk], writes=[prk])
                        if nb == 1:
                            self.tt(ma, ma, pr[:], ALU.add, reads=[mk, prk], writes=[mk])
                        else:
                            self.tt(ar[:, 20 + c, :], ma, pr[:], ALU.add, reads=[mk, prk], writes=[('ar', 20 + c)])
        if DEBUG_M < 5:
            return
        for cq in range(2):
            wm, wmk = self.wget(li, f'mo{cq}')
            wm = wm.rearrange("p (k c) -> p k c", k=8)
            for cc in range(4):
                c = cq * 4 + cc
                hb = c % 2
                for kc in range(8):
                    self.mm(ps[hb][:, :], lhsT=wm[:, kc, cc * 128:(cc + 1) * 128], rhs=ar[:, 20 + kc, :], start=(kc == 0),
                            stop=(kc == 7), reads=[wmk, ('ar', 20 + kc)], writes=[('ps', hb)], inc=(kc == 7))
                self.ln_accum(c, n, hb)
        x1 = self.xg[1]
        self.ln_finish(n, S_LN + 0, lambda c: (x1[:, c, 0:512], ('xg', 1)))
        if DEBUG_M < 6:
            return
        wq, wqk = self.wget(li, 'xq')
        wq = wq.rearrange("p (k c) -> p k c", k=8)
        sc = 1.0 / np.sqrt(128.0)
        for h in range(4):
            for kc in range(8):
                self.mm(ps[0][:, :], lhsT=wq[:, kc, h * 128:(h + 1) * 128], rhs=x1[:, kc, 0:512], start=(kc == 0),
                        stop=(kc == 7), reads=[wqk, ('xg', 1)], writes=[('ps', 0)], inc=(kc == 7))
            qx = Bt[0]
            self.act(qx[:], ps[0][:, :], AF.Copy, reads=[('ps', 0)], writes=['b0'])
            for mt in range(2):
                sb = 1 + mt
                self.mm(ps[sb][:, :], lhsT=self.KmT[:, h, mt * 128:(mt + 1) * 128], rhs=qx[:], start=True, stop=True,
                        reads=['KmT', 'b0'], writes=[('ps', sb)], inc=True)
                self.act(self.pt[mt][:], ps[sb][:, :], AF.Exp, reads=[('ps', sb)], writes=[('pt', mt)], scale=float(sc))
            for mt in range(2):
                self.mm(ps[3][:, :], lhsT=self.Vm[:, mt, h * 128:(h + 1) * 128], rhs=self.pt[mt][:], start=(mt == 0),
                        stop=(mt == 1), reads=['Vm', ('pt', mt)], writes=[('ps', 3)], inc=(mt == 1))
            for mt in range(2):
                self.mm(ps[4][:, :], lhsT=self.cb(C_ONES), rhs=self.pt[mt][:], start=(mt == 0),
                        stop=(mt == 1), reads=['cstb', ('pt', mt)], writes=[('ps', 4)], inc=(mt == 1))
            self.powact(T[0][:], ps[4][:, :], -1.0, [('ps', 4)], ['t0'])
            self.tt(ar[:, h, :], ps[3][:, :], T[0][:], ALU.mult, reads=[('ps', 3), 't0'], writes=[('ar', h)])
        for cq in range(2):
            wo, wok = self.wget(li, f'xo{cq}')
            wo = wo.rearrange("p (k c) -> p k c", k=4)
            for cc in range(4):
                c = cq * 4 + cc
                hb = c % 2
                for kc in range(4):
                    self.mm(ps[hb][:, :], lhsT=wo[:, kc, cc * 128:(cc + 1) * 128], rhs=ar[:, kc, :], start=(kc == 0),
                            stop=(kc == 3), reads=[wok, ('ar', kc)], writes=[('ps', hb)], inc=(kc == 3))
                self.ln_accum(c, n, hb)
        x2 = self.xg[0]
        self.ln_finish(n, S_LN + 16, lambda c: (x2[:, c, 0:512], ('xg', 0)))
        if self.mode == 'F':
            t0 = self.hf * NTOK + n * GS
            P.dma(self.d_x2all[:, :, t0:t0 + GS], x2[:, :, 0:512], reads=[('xg', 0)], writes=[('x2own', self.hf, n)])
        else:
            P.dma(self.d_x2own[:, :, n * GS:(n + 1) * GS], x2[:, :, 0:512], reads=[('xg', 0)], writes=[('x2own', n)])

    def ffn_group(self, li, n, last):
        P, ps, T = self.P, self.ps, self.T
        ar = self.arena
        xb = self.xg[n % 2]
        xk = ('xg', n % 2)
        if self.mode == 'F':
            T0 = self.hf * NTOK + n * GS
            lo, hi, c0 = T0 - 1, T0 + GS + 1, 0
            if lo < 0:
                P.op('dve', lambda e, xb=xb: e.memset(xb[:, :, 0:1], 0.0), writes=[xk])
                lo, c0 = 0, 1
            if hi > 4096:
                P.op('dve', lambda e, xb=xb: e.memset(xb[:, :, 513:514], 0.0), writes=[xk])
                hi = 4096
            rk = [('x2own', h, j) for h in range(2) for j in range(NG)]
            P.dma(xb[:, :, c0:c0 + (hi - lo)], self.d_x2all[:, :, lo:hi], reads=rk, writes=[xk])
        else:
            lo = n * GS - 1
            hi = n * GS + GS + 1
            c0 = 0
            if lo < 0:
                P.dma(xb[:, :, 0:1], self.halo_s[:, :, 0:1], reads=['halo_s'], writes=[xk], allow_slow_non_contiguous=True)
                lo = 0
                c0 = 1
            if hi > NTOK:
                P.dma(xb[:, :, 513:514], self.halo_s[:, :, 1:2], reads=['halo_s'], writes=[xk], allow_slow_non_contiguous=True)
                hi = NTOK
            P.dma(xb[:, :, c0:c0 + (hi - lo)], self.d_x2own[:, :, lo:hi], reads=[('x2own', j) for j in range(NG)], writes=[xk])
        cvk = lambda j, col: self.small[:, S_CV + 44 * j + col: S_CV + 44 * j + col + 1]
        for jq in range(6):
            nch = 4 if jq < 5 else 2
            wa, wak = self.wget(li, f'ua{jq}')
            wa = wa.rearrange("p (k c) -> p k c", k=8)
            wb, wbk = self.wget(li, f'ub{jq}')
            wb = wb.rearrange("p (k c) -> p k c", k=8)
            for cc in range(nch):
                j = jq * 4 + cc
                hc = []
                for half, (w, wk) in enumerate(((wa, wak), (wb, wbk))):
                    col = j + 22 * half
                    pb = 2 * half
                    for kc in range(8):
                        self.mm(ps[pb][:, :], lhsT=w[:, kc, cc * 128:(cc + 1) * 128], rhs=xb[:, kc, 1:513], start=(kc == 0),
                                stop=(kc == 7), reads=[wk, xk], writes=[('ps', pb)], inc=(kc == 7))
                    for kc in range(8):
                        self.mm(ps[pb + 1][:, 0:2], lhsT=w[:, kc, cc * 128:(cc + 1) * 128], rhs=xb[:, kc, 0:514:513],
                                start=(kc == 0), stop=(kc == 7), reads=[wk, xk], writes=[('ps', pb + 1)], inc=(kc == 7))
                    he = self.hext[half]
                    hk = self.hext_keys[half]
                    self.act(he[:, 1:513], ps[pb][:, :], AF.Copy, reads=[('ps', pb)], writes=hk)
                    self.act(he[:, 0:514:513], ps[pb + 1][:, 0:2], AF.Copy, reads=[('ps', pb + 1)], writes=hk)
                    a = T[2 * half]
                    ak = f't{2 * half}'
                    self.ts(a[:], he[:, 0:512], cvk(0, col), cvk(3, col), ALU.mult, ALU.add, reads=hk + ['small'], writes=[ak], eng='pool')
                    self.stt(a[:], he[:, 1:513], cvk(1, col), a[:], ALU.mult, ALU.add, reads=hk + ['small', ak], writes=[ak])
                    b2 = T[2 * half + 1]
                    bk = f't{2 * half + 1}'
                    self.stt(b2[:], he[:, 2:514], cvk(2, col), a[:], ALU.mult, ALU.add, reads=hk + ['small', ak], writes=[bk])
                    hc.append((b2, bk))
                ga = T[4 + (j % 2)]
                gk = 't4' if j % 2 == 0 else 't5'
                self.act(ga[:], hc[0][0][:], AF.Gelu, reads=[hc[0][1]], writes=[gk])
                self.tt(ar[:, j, :], ga[:], hc[1][0][:], ALU.mult, reads=[gk, hc[1][1]], writes=[('ar', j)], eng='pool')
        for c in range(8):
            wd, wdk = self.wget(li, f'dn{c}')
            wd = wd.rearrange("p (k c) -> p k c", k=22)
            hb = 4 + (c % 2)
            for kc in range(22):
                self.mm(ps[hb][:, :], lhsT=wd[:, kc, :], rhs=ar[:, kc, :], start=(kc == 0), stop=(kc == 21),
                        reads=[wdk, ('ar', kc)], writes=[('ps', hb)], inc=(kc == 21))
            self.ln_accum(c, n, hb)
        if self.mode == 'F':
            ob = self.KB[:].rearrange("p (c t) -> p c t", c=8)
            okeys = [('KB', c) for c in range(8)]
        else:
            ob = self.xo_buf[:]
            okeys = ['xo_buf'] * 8
        self.ln_finish(n, S_LN + 32, lambda c: (ob[:, c, :], okeys[c]))
        if not last:
            P.dma(self.xown_ap(n), ob, reads=list(set(okeys)), writes=[self.xown_key(n)])

    def cast_layer(self, li):
        tot = weight_offsets()['_tot']
        nch = 16
        step = (tot + nch - 1) // nch
        for j in range(nch):
            a, b = j * step, min(tot, (j + 1) * step)
            self.P.dma(self.d_wbf[li, :, a:b], self.d_wblk[li, :, a:b], writes=[('wbf', li)], queue='pool')

    def switch_half(self, hf, first_touch):
        P = self.P
        if self.resident == hf:
            self.hf = hf
            return
        if self.resident is not None:
            r = self.resident
            for c in range(8):
                P.dma(self.d_park[r, :, c, :], self.xT32[:, c, :], reads=[('x32', c, n) for n in range(NG)],
                      writes=[('park', r, c)])
        src = self.d_xT32h if first_touch else self.d_park
        for c in range(8):
            P.dma(self.xT32[:, c, :], src[hf, :, c, :], reads=([] if first_touch else [('park', hf, c)]),
                  writes=[('x32', c, n) for n in range(NG)])
        self.resident = hf
        self.hf = hf

    def build(self):
        P = self.P
        assert self.mode == 'F'
        nL = len(self.layers)
        self.xo_buf = None
        self.resident = None
        self.plan_weights()
        self.load_common()
        self.cast_layer(0)
        for hf in range(2):
            self.switch_half(hf, True)
            for n in range(NG):
                xb = self.xg[n % 2]
                for c in range(8):
                    if c % 2:
                        P.op('act', lambda e, c=c, n=n, xb=xb: e.copy(out=xb[:, c, 0:512], in_=self.xT32[:, c, n * GS:(n + 1) * GS]),
                             reads=[('x32', c, n)], writes=[('xg', n % 2)])
                    else:
                        P.op('dve', lambda e, c=c, n=n, xb=xb: e.tensor_copy(out=xb[:, c, 0:512], in_=self.xT32[:, c, n * GS:(n + 1) * GS]),
                             reads=[('x32', c, n)], writes=[('xg', n % 2)])
                P.dma(self.xown_ap(n), xb[:, :, 0:512], reads=[('xg', n % 2)], writes=[self.xown_key(n)])
        P.op('dve', lambda e: e.memset(self.VB[:].rearrange("p a b c -> p (a b c)"), 1.0),
             writes=[('VB', i) for i in range(8)])
        P.op('dve', lambda e: e.memset(self.VA[:].rearrange("p a b c -> p (a b c)"), 1.0),
             writes=[('VA', i) for i in range(18)])
        for gi in range(2):
            P.op('dve', lambda e, gi=gi: e.memset(self.qpad[gi][:], 0.0), writes=[('qp', gi)])
        for li in range(nL):
            if li > 0:
                P.new_epoch()
            self.load_layer_small(li)
            off, e = weight_offsets()['ws']
            P.dma(self.wsT[:].rearrange("p g i -> p (g i)"), self.d_wbf[li, :, off:off + e], reads=[('wbf', li)], writes=['wsT'])
            if li + 1 < nL:
                self.cast_layer(li + 1)
            self.kv_pass_b(li)
            self.mem_kv(li)
            order = [1, 0]
            for hf in order:
                self.switch_half(hf, False)
                self.kv_pass_a(li)
                for n in range(NG):
                    self.mixer_group(li, n)
            for hf in [0, 1]:
                self.switch_half(hf, False)
                for n in range(NG):
                    self.ffn_group(li, n, last=(li == nL - 1))
                if li == nL - 1:
                    for c in range(8):
                        P.dma(self.d_yh[hf, :, c, :], self.xT32[:, c, :], reads=[('x32', c, n) for n in range(NG)],
                              writes=[('y', hf, c)])
        P.finish()
        P.emit()
        return self.nc


_CACHE = {}


def _get_prog(mode, nl):
    key = (mode, nl)
    if key not in _CACHE:
        b = Builder(mode, list(range(nl)))
        _CACHE[key] = b.build()
    return _CACHE[key]


def _bf(a):
    return np.ascontiguousarray(a).view(ml_dtypes.bfloat16) if a.dtype == np.uint16 else a


def kernel(**inp):
    inp = {k: np.asarray(v, dtype=np.float32) for k, v in inp.items()}
    x = inp['x']
    mem = inp['mem']
    cores = list(range(8))
    W = weight_offsets()
    wblk = np.zeros((DEPTH, 128, W['_tot']), np.float32)
    small = np.zeros((DEPTH, 128, NS), np.float32)
    for l in range(DEPTH):
        for nm, a in weight_blocks(inp, l).items():
            off, e = W[nm]
            wblk[l, :, off:off + e] = a
        small[l] = small_params(inp, l)
    tabs = [rope_tables(hf) for hf in range(2)]
    tabA = np.ascontiguousarray(np.stack([tabs[0][0], tabs[1][0]], 0))
    tabBk = tabs[0][2]
    cst = constants(0)
    ins = []
    for c in cores:
        b = c % 4
        xT = np.ascontiguousarray(x[b].T.reshape(8, 128, 2, NTOK).transpose(2, 1, 0, 3))
        memT = np.ascontiguousarray(mem[b].T.reshape(8, 128, 256).transpose(1, 0, 2))
        ins.append({"xT32": xT, "wblk": wblk, "small": small, "cst": cst, "tabA": tabA, "tabBk": tabBk, "memT": memT})
    nc = _get_prog('F', DEPTH)
    res = run_bass_kernel_spmd(nc, ins, core_ids=cores)
    out = np.zeros((4, 4096, 1024), np.float32)
    for b in range(4):
        y = np.asarray(res.results[b]["y"])
        for hf in range(2):
            yt = y[hf].transpose(1, 0, 2).reshape(1024, NTOK)
            out[b, hf * NTOK:(hf + 1) * NTOK, :] = yt.T
    return out
```

```python
import contextlib
import numpy as np
import ml_dtypes
import concourse.bass as bass
import concourse.mybir as mybir
from concourse.bass_utils import run_bass_kernel_spmd

F32 = mybir.dt.float32
BF16 = mybir.dt.bfloat16
AF = mybir.ActivationFunctionType
ALU = mybir.AluOpType
ENGS = ('pe', 'dve', 'act', 'pool', 'sp')

DEPTH = 4
ALPHA = (2 * DEPTH) ** 0.25
LN_EPS = 1e-5
RMS_EPS = 1e-6
NTOK = 2048
DEBUG_STAGE = 9
DEBUG_SUB = 9
DEBUG_R = 9
DEBUG_M = 9
GS = 512
NG = 4
D_FF = 2816


class Prog:
    def __init__(self, nc, n_dma_sems=24):
        self.nc = nc
        self.es = contextlib.ExitStack()
        self.q = {e: [] for e in ENGS}
        self.cnt = {}
        self.seen = {e: {} for e in ENGS}
        self.snap = {}
        self.last_w = {}
        self.readers = {}
        self.sems = {}
        self.epoch = 0
        self.n_dma = {'sp': 16, 'pool': 8, 'act': 4, 'cc': 2}
        self.dma_rr = {'sp': 0, 'pool': 0, 'act': 0, 'cc': 0}
        self.n_wait = 0
        for qn, n in self.n_dma.items():
            for i in range(n):
                self._mk_owner(('d' + qn, i))
        for e in ENGS:
            self._mk_owner((e, 0))

    def _mk_owner(self, o):
        nm = "s_" + "_".join(str(x) for x in o)
        self.sems[o] = self.es.enter_context(self.nc.semaphore(nm))
        self.cnt[o] = 0

    def new_epoch(self):
        self.epoch += 1
        for e in ENGS:
            if e != 'sp':
                self._mk_owner((e, self.epoch))

    def sbuf(self, name, shape, dt):
        return self.es.enter_context(self.nc.sbuf_tensor(name, list(shape), dt))

    def psum(self, name, shape, dt):
        return self.es.enter_context(self.nc.psum_tensor(name, list(shape), dt))

    def _wait(self, E, o, v):
        if self.seen[E].get(o, 0) >= v:
            return
        if o[0] == E:
            if E == 'pe' or E == 'sp':
                return
            if o[1] != self.epoch or v < self.cnt[o] - 1:
                return
        self.q[E].append(('wait', o, v))
        self.n_wait += 1
        self.seen[E][o] = v
        sn = self.snap.get((o, v))
        if sn:
            se = self.seen[E]
            for o2, v2 in sn.items():
                if se.get(o2, 0) < v2:
                    se[o2] = v2

    def _deps(self, reads, writes):
        deps = set()
        for k in reads:
            w = self.last_w.get(k)
            if w is not None:
                deps.add(w)
        for k in writes:
            w = self.last_w.get(k)
            if w is not None:
                deps.add(w)
            for r in self.readers.get(k, ()):
                deps.add(r)
        return deps

    def _commit(self, ident, reads, writes):
        for k in reads:
            self.readers.setdefault(k, []).append(ident)
        for k in writes:
            self.last_w[k] = ident
            self.readers[k] = []

    def op(self, E, fn, reads=(), writes=(), inc=True):
        psr = [k for k in reads if isinstance(k, tuple) and k[0] == 'ps']
        if psr:
            reads = [k for k in reads if not (isinstance(k, tuple) and k[0] == 'ps')]
            writes = list(writes) + psr
        own = (E, self.epoch)
        for (o, v) in sorted(self._deps(reads, writes), key=str):
            self._wait(E, o, v)
        v = self.cnt[own] + 1
        if inc:
            self.cnt[own] = v
            self.snap[(own, v)] = dict(self.seen[E])
        self.q[E].append(('op', fn, own if inc else None, 1))
        self._commit((own, v), reads, writes)

    def dma(self, out, in_, reads=(), writes=(), queue='sp', **kw):
        i = self.dma_rr[queue]
        self.dma_rr[queue] = (i + 1) % self.n_dma[queue]
        own = ('d' + queue, i)
        deps = self._deps(reads, writes)
        if self.cnt[own] > 0:
            deps.add((own, self.cnt[own]))
        for (o, v) in sorted(deps, key=str):
            self._wait(queue, o, v)
        v = self.cnt[own] + 16
        self.cnt[own] = v
        self.snap[(own, v)] = dict(self.seen[queue])
        self.q[queue].append(('op', lambda eng: eng.dma_start(out=out, in_=in_, **kw), own, 16))
        self._commit((own, v), reads, writes)

    def coll(self, fn, reads=(), writes=()):
        i = self.dma_rr['cc']
        self.dma_rr['cc'] = (i + 1) % self.n_dma['cc']
        own = ('dcc', i)
        deps = self._deps(reads, writes)
        if self.cnt[own] > 0:
            deps.add((own, self.cnt[own]))
        for (o, v) in sorted(deps, key=str):
            self._wait('pool', o, v)
        v = self.cnt[own] + 16
        self.cnt[own] = v
        self.snap[(own, v)] = dict(self.seen['pool'])
        self.q['pool'].append(('op', fn, own, 16))
        self._commit((own, v), reads, writes)

    def finish(self):
        for o, c in self.cnt.items():
            if c > 0 and o[0] != 'sp':
                if self.seen['sp'].get(o, 0) < c:
                    self.q['sp'].append(('wait', o, c))
                    self.seen['sp'][o] = c

    def emit(self):
        nc = self.nc
        for o, c in self.cnt.items():
            assert c < 60000, (o, c)
        with nc.Block() as block:
            def replay(E):
                def run(eng):
                    for ent in self.q[E]:
                        if ent[0] == 'wait':
                            eng.wait_ge(self.sems[ent[1]], ent[2])
                        else:
                            ins = ent[1](eng)
                            if ent[2] is not None:
                                ins.then_inc(self.sems[ent[2]], ent[3])
                return run
            block.tensor(replay('pe'))
            block.vector(replay('dve'))
            block.scalar(replay('act'))
            block.gpsimd(replay('pool'))
            block.sync(replay('sp'))
        self.es.close()

    def check(self):
        val = {o: 0 for o in self.cnt}
        pc = {e: 0 for e in ENGS}
        progress = True
        while progress:
            progress = False
            for e in ENGS:
                q = self.q[e]
                while pc[e] < len(q):
                    ent = q[pc[e]]
                    if ent[0] == 'wait':
                        if val[ent[1]] >= ent[2]:
                            pc[e] += 1
                            progress = True
                        else:
                            break
                    else:
                        if ent[2] is not None:
                            val[ent[2]] += ent[3]
                        pc[e] += 1
                        progress = True
        stuck = {e: (pc[e], len(self.q[e]), self.q[e][pc[e]][:3] if pc[e] < len(self.q[e]) else None) for e in ENGS}
        ok = all(pc[e] == len(self.q[e]) for e in ENGS)
        return ok, stuck, {o: (val[o], self.cnt[o]) for o in val if val[o] != self.cnt[o]}

    def stats(self):
        return {e: sum(1 for x in self.q[e] if x[0] == 'op') for e in ENGS}, self.n_wait


def _kmaj(w):
    K, C = w.shape
    return np.ascontiguousarray(w.reshape(K // 128, 128, C).transpose(1, 0, 2)).reshape(128, -1)


def _qreorder(w):
    w4 = w.reshape(w.shape[0], 8, 64)
    order = [h for i in range(4) for h in (i, 4 + i)]
    return w4[:, order, :].reshape(w.shape[0], 512)


def weight_blocks(inp, l):
    w_in = inp['w_in'][l]
    B = {}
    B['kvB'] = _kmaj(w_in[:, 1280:1536])
    B['kvA'] = _kmaj(w_in[:, 512:768])
    B['ws'] = np.ascontiguousarray(inp['c_ws'][l].transpose(2, 0, 1)).reshape(128, 512)
    B['xk'] = _kmaj(inp['x_wkv'][l][:, 0:512])
    B['xv'] = _kmaj(inp['x_wkv'][l][:, 512:1024])
    B['qB'] = _kmaj(_qreorder(w_in[:, 768:1280]))
    B['qA'] = _kmaj(_qreorder(w_in[:, 0:512]))
    B['wu'] = _kmaj(w_in[:, 1536:2048])
    B['wv'] = _kmaj(w_in[:, 2048:2560])
    for nb in range(3):
        for cq in range(2):
            B[f'g{nb}{cq}'] = _kmaj(w_in[:, 2560 + nb * 1024 + cq * 512: 2560 + nb * 1024 + cq * 512 + 512])
            wb = inp['w_branch'][l, nb][:, cq * 512:(cq + 1) * 512]
            if nb < 2:
                a = np.zeros((128, 8, 512), np.float32)
                a[0:64] = wb.reshape(8, 64, 512).transpose(1, 0, 2)
                B[f'br{nb}{cq}'] = a.reshape(128, 4096)
            else:
                B[f'br{nb}{cq}'] = _kmaj(wb)
    for cq in range(2):
        B[f'mo{cq}'] = _kmaj(inp['w_mix_out'][l][:, cq * 512:(cq + 1) * 512])
    B['xq'] = _kmaj(inp['x_wq'][l])
    for cq in range(2):
        B[f'xo{cq}'] = _kmaj(inp['x_wo'][l][:, cq * 512:(cq + 1) * 512])
    up = inp['f_w_up'][l]
    for jq in range(6):
        w = min(512, D_FF - jq * 512)
        B[f'ua{jq}'] = _kmaj(up[:, jq * 512: jq * 512 + w])
        B[f'ub{jq}'] = _kmaj(up[:, D_FF + jq * 512: D_FF + jq * 512 + w])
    for c in range(8):
        B[f'dn{c}'] = _kmaj(inp['f_w_down'][l][:, c * 128:(c + 1) * 128])
    return B


_WOFF = None


def weight_offsets():
    global _WOFF
    if _WOFF is None:
        sizes = [('kvB', 2048), ('kvA', 2048), ('ws', 512), ('xk', 4096), ('xv', 4096), ('qB', 4096), ('qA', 4096),
                 ('wu', 4096), ('wv', 4096)]
        for nb in range(3):
            for cq in range(2):
                sizes.append((f'g{nb}{cq}', 4096))
                sizes.append((f'br{nb}{cq}', 4096 if nb < 2 else 2048))
        sizes += [('mo0', 4096), ('mo1', 4096), ('xq', 4096), ('xo0', 2048), ('xo1', 2048)]
        for jq in range(6):
            w = min(512, D_FF - jq * 512)
            sizes += [(f'ua{jq}', 8 * w), (f'ub{jq}', 8 * w)]
        for c in range(8):
            sizes.append((f'dn{c}', 2816))
        off = 0
        d = {}
        for nm, e in sizes:
            d[nm] = (off, e)
            off += e
        d['_tot'] = off
        _WOFF = d
    return _WOFF


S_BG = 0
S_GQ = 24
S_GK = 25
S_LN = 26
S_CV = 74
S_SINK = 250
S_CLG = 258
S_CLB = 770
S_BS = 1282
NS = 1794


def small_params(inp, l):
    s = np.zeros((128, NS), np.float32)
    s[:, S_BG:S_BG + 24] = inp['b_gate'][l].reshape(24, 128).T
    s[:, S_GQ] = np.tile(inp['b_q_gain'][l], 2)
    s[:, S_GK] = np.tile(inp['b_k_gain'][l], 2)
    for i, nm in enumerate(['ln1_g', 'ln1_b', 'ln2_g', 'ln2_b', 'ln3_g', 'ln3_b']):
        s[:, S_LN + 8 * i: S_LN + 8 * i + 8] = inp[nm][l].reshape(8, 128).T
    ck = inp['f_conv_k'][l]
    for j in range(3):
        s[:, S_CV + 44 * j: S_CV + 44 * j + 44] = ck[j].reshape(44, 128).T
    s[:, S_CV + 132: S_CV + 176] = inp['f_conv_b'][l].reshape(44, 128).T
    s[:, S_SINK:S_SINK + 8] = inp['a_sink'][l][None, :]
    s[:, S_CLG:S_CLG + 512] = inp['c_ln_g'][l][None, :]
    s[:, S_CLB:S_CLB + 512] = inp['c_ln_b'][l][None, :]
    s[:, S_BS:S_BS + 512] = inp['c_bs'][l].reshape(1, 512)
    return s


C_RTA = 0
C_RTB = 128
C_BONES = 256
C_ONES = 384
C_ONESLN = 512
C_MASK = 640
C_FLAG = 1792
C_EPS6 = 1794
C_EPS5 = 1795
NC_ = 1796


def constants(hf):
    c = np.zeros((128, NC_), np.float32)
    R_A = np.zeros((128, 128), np.float32)
    R_B = np.zeros((128, 128), np.float32)
    for i in range(128):
        if i % 64 < 32:
            R_A[i, i + 32] = -1.0
        else:
            R_A[i, i - 32] = 1.0
        if i % 32 < 16:
            R_B[i, i + 16] = -1.0
        else:
            R_B[i, i - 16] = 1.0
    c[:, C_RTA:C_RTA + 128] = R_A.T
    c[:, C_RTB:C_RTB + 128] = R_B.T
    bo = np.zeros((128, 128), np.float32)
    bo[0:64, 0:64] = 1.0 / 64
    bo[64:128, 64:128] = 1.0 / 64
    c[:, C_BONES:C_BONES + 128] = bo
    c[:, C_ONES:C_ONES + 128] = 1.0
    c[:, C_ONESLN:C_ONESLN + 128] = 1.0 / 1024
    ki = np.arange(128)[:, None]
    qi = np.arange(128)[None, :]
    mL = (qi <= ki).astype(np.float32)
    mR = (ki <= qi).astype(np.float32)
    one = np.ones((128, 128), np.float32)
    zero = np.zeros((128, 128), np.float32)
    c[:, C_MASK:C_MASK + 384] = np.concatenate([mL, one, mR], 1)
    c[:, C_MASK + 384:C_MASK + 768] = np.concatenate([zero, one, mR], 1)
    c[:, C_MASK + 768:C_MASK + 1152] = np.concatenate([mL, one, zero], 1)
    c[:, C_FLAG] = 1.0 if hf == 1 else 0.0
    c[:, C_FLAG + 1] = 1.0 if hf == 0 else 0.0
    c[:, C_EPS6] = 1e-6
    c[:, C_EPS5] = 1e-5
    return c


def rope_tables(hf):
    theta = np.float32(10000.0)
    invA = (theta ** (-np.arange(32, dtype=np.float32) * np.float32(2.0 / 64))).astype(np.float32)
    invB = (theta ** (-np.arange(16, dtype=np.float32) * np.float32(2.0 / 32))).astype(np.float32)
    p = np.arange(128)
    d = p % 64

    def tabA(pos):
        ang = pos.astype(np.float32)[None, :] * invA[d % 32][:, None]
        return np.stack([np.cos(ang), np.sin(ang)], 1).astype(np.float32)

    def tabB(pos):
        row = (pos // 64).astype(np.float32)
        col = (pos % 64).astype(np.float32)
        sel = np.where((d < 32)[:, None], row[None, :], col[None, :])
        ang = sel * invB[d % 16][:, None]
        return np.stack([np.cos(ang), np.sin(ang)], 1).astype(np.float32)

    own = np.arange(hf * NTOK, (hf + 1) * NTOK)
    posA = np.concatenate([np.arange(1920, 2048), own, np.arange(2048, 2176)])
    return tabA(posA), tabB(own), tabB(np.arange(4096))


class Builder:
    def __init__(self, mode, layers):
        self.mode = mode
        self.layers = layers
        nL = len(layers)
        nc = bass.Bass("TRN2", target_bir_lowering=False)
        self.nc = nc
        self.P = P = Prog(nc)
        W = weight_offsets()
        dt = nc.dram_tensor
        if mode == 'F':
            self.d_xT32h = dt("xT32", [2, 128, 8, NTOK], F32, kind="ExternalInput").ap()
        else:
            self.d_xT32 = dt("xT32", [128, 8, NTOK], F32, kind="ExternalInput").ap()
        if mode in ('A', 'B', 'F'):
            self.d_wblk = dt("wblk", [nL, 128, W['_tot']], F32, kind="ExternalInput").ap()
            self.d_small = dt("small", [nL, 128, NS], F32, kind="ExternalInput").ap()
            self.d_cst = dt("cst", [128, NC_], F32, kind="ExternalInput").ap()
        if mode == 'F':
            self.d_tabAh = dt("tabA", [2, 128, 2, 2304], F32, kind="ExternalInput").ap()
            self.d_tabBk = dt("tabBk", [128, 2, 4096], F32, kind="ExternalInput").ap()
            self.d_memT = dt("memT", [128, 8, 256], F32, kind="ExternalInput").ap()
        if mode == 'A':
            self.d_tabA = dt("tabA", [128, 2, 2304], F32, kind="ExternalInput").ap()
            self.d_tabBq = dt("tabBq", [128, 2, NTOK], F32, kind="ExternalInput").ap()
            self.d_tabBk = dt("tabBk", [128, 2, 4096], F32, kind="ExternalInput").ap()
            self.d_memT = dt("memT", [128, 8, 256], F32, kind="ExternalInput").ap()
        if mode == 'P':
            self.d_xown = dt("xown", [128, 8, NTOK], BF16, kind="ExternalOutput").ap()
        if mode == 'A':
            self.d_xown = dt("xown", [128, 8, NTOK], BF16, kind="ExternalInput").ap()
            self.d_xall = dt("xall", [128, 8, 4096], BF16, kind="ExternalInput").ap()
            self.d_x2own = dt("x2own", [128, 8, NTOK], BF16, kind="ExternalOutput").ap()
            self.d_xT32o = dt("xT32o", [128, 8, NTOK], F32, kind="ExternalOutput").ap()
        if mode == 'F':
            self.d_xallp = dt("xall_i", [128, 8, 4096], BF16, kind="Internal").ap()
            self.d_x2all = dt("x2all_i", [128, 8, 4096], BF16, kind="Internal").ap()
            self.d_park = dt("park_i", [2, 128, 8, NTOK], F32, kind="Internal").ap()
            self.d_yh = dt("y", [2, 128, 8, NTOK], F32, kind="ExternalOutput").ap()
            self.d_wbf = dt("wbf_i", [nL, 128, W['_tot']], BF16, kind="Internal").ap()
            self.hf = 0
        if mode == 'B':
            self.d_x2own = dt("x2own", [128, 8, NTOK], BF16, kind="ExternalInput").ap()
            self.d_halo = dt("halo", [128, 8, 2], BF16, kind="ExternalInput").ap()
            self.d_xown = dt("xown", [128, 8, NTOK], BF16, kind="ExternalOutput").ap()
            self.d_xT32o = dt("xT32o", [128, 8, NTOK], F32, kind="ExternalOutput").ap()
        self.alloc()

    def alloc(self):
        P = self.P
        m = self.mode
        self.xT32 = P.sbuf("xT32s", [128, 8, NTOK], F32)
        self.ps = [P.psum(f"ps{i}", [128, 512], F32) for i in range(8)]
        self.xg = [P.sbuf(f"xg{i}", [128, 8, 514], BF16) for i in range(2)]
        if m == 'P':
            return
        self.ring = [P.sbuf(f"ring{i}", [128, 4096], BF16) for i in range(3)]
        self.small = P.sbuf("small_s", [128, NS], F32)
        self.cst = P.sbuf("cst_s", [128, NC_ - C_ONES], F32)
        self.cstb = P.sbuf("cstb", [128, 512], BF16)
        self.sinkexp = P.sbuf("sinkexp", [128, 8], F32)
        self.arena = P.sbuf("arena", [128, 28, 512], BF16)
        self.T = [P.sbuf(f"t{i}", [128, 512], F32) for i in range(8)]
        self.uT = self.arena[:, 24:28, :]
        self.wkv = self.arena[:, 0:4, :].rearrange("p a (b c) -> p (a b) c", c=256)
        self.WKV_KEYS = [('ar', i) for i in range(4)]
        self.Bt = [P.sbuf(f"b{i}", [128, 512], BF16) for i in range(4)]
        if m in ('A', 'F'):
            self.KB = P.sbuf("KB", [128, 4096], BF16)
            self.VB = P.sbuf("VB", [128, 32, 2, 65], BF16)
            self.KA = P.sbuf("KA", [128, 2304], BF16)
            self.VA = P.sbuf("VA", [128, 18, 2, 65], BF16)
            self.wsT = P.sbuf("wsT", [128, 4, 128], BF16)
            self.tabs = [P.sbuf(f"tab{i}", [128, 2, 512], F32) for i in range(1)]
            self.pt = [P.sbuf(f"pt{i}", [128, 512], BF16) for i in range(3)]
            self.qpad = [P.sbuf(f"qpad{i}", [128, 512], BF16) for i in range(2)]
            self.KmT = P.sbuf("KmT", [128, 4, 256], BF16)
            self.Vm = P.sbuf("Vm", [128, 2, 512], BF16)
            self.mv = P.sbuf("mv", [128, 16], F32)
        if m == 'B':
            self.hext = [P.sbuf(f"hext{i}", [128, 514], F32) for i in range(2)]
            self.hext_keys = [[('hext', 0)], [('hext', 1)]]
            self.halo_s = P.sbuf("halo_s", [128, 8, 2], BF16)
        if m == 'F':
            self.hext = [self.arena[:, 22:25, :].rearrange("p a b -> p (a b)").bitcast(F32),
                         self.arena[:, 25:28, :].rearrange("p a b -> p (a b)").bitcast(F32)]
            self.hext_keys = [[('ar', i) for i in (22, 23, 24)], [('ar', i) for i in (25, 26, 27)]]
            self.halo_s = P.sbuf("halo_s", [128, 8, 2], BF16)
        self.tabi = 0
        self.wnext = 0
        self.wcur = 0
        self.wplan = []

    def plan_weights(self):
        plan = []
        nh = 2 if self.mode == 'F' else 1
        for li in range(len(self.layers)):
            if self.mode in ('A', 'F'):
                plan += [(li, 'xk'), (li, 'xv')]
                for n in range(NG * nh):
                    plan += [(li, 'qB'), (li, 'qA'), (li, 'wu'), (li, 'wv')]
                    for cq in range(2):
                        for nb in range(3):
                            plan += [(li, f'g{nb}{cq}'), (li, f'br{nb}{cq}')]
                    plan += [(li, 'mo0'), (li, 'mo1'), (li, 'xq'), (li, 'xo0'), (li, 'xo1')]
            if self.mode in ('B', 'F'):
                for n in range(NG * nh):
                    for jq in range(6):
                        plan += [(li, f'ua{jq}'), (li, f'ub{jq}')]
                    for c in range(8):
                        plan.append((li, f'dn{c}'))
        self.wplan = plan

    def _issue_wload(self, k):
        li, nm = self.wplan[k]
        off, e = weight_offsets()[nm]
        slot = k % 3
        self.P.dma(self.ring[slot][:, 0:e], self.d_wbf[li, :, off:off + e], reads=[('wbf', li)],
                   writes=[('ring', slot)], queue='sp')

    def wget(self, li, nm):
        k = self.wcur
        assert self.wplan[k] == (li, nm), (self.wplan[k], li, nm)
        while self.wnext < min(len(self.wplan), k + 2):
            self._issue_wload(self.wnext)
            self.wnext += 1
        self.wcur += 1
        slot = k % 3
        e = weight_offsets()[nm][1]
        return self.ring[slot][:, 0:e], ('ring', slot)

    def mm(self, out, lhsT, rhs, start, stop, reads, writes, inc):
        self.P.op('pe', lambda e: e.matmul(out, lhsT=lhsT, rhs=rhs, start=start, stop=stop),
                  reads=reads, writes=writes, inc=inc)

    def act(self, out, in_, func, reads, writes, **kw):
        self.P.op('act', lambda e: e.activation(out=out, in_=in_, func=func, **kw), reads=reads, writes=writes)

    def tt(self, out, in0, in1, op, reads, writes, eng='dve'):
        self.P.op(eng, lambda e: e.tensor_tensor(out=out, in0=in0, in1=in1, op=op), reads=reads, writes=writes)

    def ts(self, out, in0, s1, s2, op0, op1, reads, writes, eng='dve'):
        if op1 is None:
            self.P.op(eng, lambda e: e.tensor_scalar(out=out, in0=in0, scalar1=s1, scalar2=None, op0=op0),
                      reads=reads, writes=writes)
        else:
            self.P.op(eng, lambda e: e.tensor_scalar(out=out, in0=in0, scalar1=s1, scalar2=s2, op0=op0, op1=op1),
                      reads=reads, writes=writes)

    def stt(self, out, in0, scalar, in1, op0, op1, reads, writes):
        self.P.op('dve', lambda e: e.scalar_tensor_tensor(out=out, in0=in0, scalar=scalar, in1=in1, op0=op0, op1=op1),
                  reads=reads, writes=writes)

    def powact(self, out, in_, expo, reads, writes, bias=None, extra_reads=()):
        if bias is None:
            self.act(out, in_, AF.Ln, reads=list(reads), writes=list(writes))
        else:
            self.act(out, in_, AF.Ln, reads=list(reads) + list(extra_reads), writes=list(writes), bias=bias)
        self.act(out, out, AF.Exp, reads=list(writes), writes=list(writes), scale=float(expo))

    def cb(self, col):
        return self.cstb[:, col:col + 128]

    def cf(self, col, n, rows=slice(None)):
        return self.cst[rows, col - C_ONES:col - C_ONES + n]

    def load_common(self):
        P = self.P
        P.dma(self.cst[:], self.d_cst[:, C_ONES:NC_], writes=['cst'])
        P.dma(self.cstb[:], self.d_cst[:, 0:512], writes=['cstb'], queue='pool')

    def load_layer_small(self, li):
        P = self.P
        P.dma(self.small[:], self.d_small[li, :, :], writes=['small'])
        if self.mode in ('A', 'F'):
            self.act(self.sinkexp[:], self.small[:, S_SINK:S_SINK + 8], AF.Exp, reads=['small'], writes=['sinkexp'])

    def load_state(self):
        for c in range(8):
            self.P.dma(self.xT32[:, c, :], self.d_xT32[:, c, :], writes=[('x32', c, n) for n in range(NG)])

    def store_state(self, dst):
        for c in range(8):
            self.P.dma(dst[:, c, :], self.xT32[:, c, :], reads=[('x32', c, n) for n in range(NG)], writes=[('dst32', c)])

    def rope_evac(self, psb, N, kind, tab, tabkey, out_ap, out_key, gcol=None):
        ps = self.ps
        T, Bt = self.T, self.Bt
        src = ps[psb][:, 0:N]
        pk = ('ps', psb)
        RT = self.cb(C_RTA if kind == 'A' else C_RTB)
        R = DEBUG_R
        if kind == 'A':
            self.act(Bt[0][:, 0:N], src, AF.Copy, reads=[pk], writes=['b0'])
        else:
            g = self.small[:, gcol:gcol + 1]
            self.act(Bt[0][:, 0:N], src, AF.Identity, reads=[pk, 'small'], writes=['b0'], scale=g)
            if R >= 2:
                self.act(Bt[1][:, 0:N], src, AF.Square, reads=[pk], writes=['b1'])
        if R < 3:
            return
        rb = 6
        self.mm(ps[rb][:, 0:N], lhsT=RT, rhs=Bt[0][:, 0:N], start=True, stop=True,
                reads=['b0', 'cstb'], writes=[('ps', rb)], inc=True)
        if kind == 'B':
            mb = 7
            self.mm(ps[mb][:, 0:N], lhsT=self.cb(C_BONES), rhs=Bt[1][:, 0:N], start=True, stop=True,
                    reads=['b1', 'cstb'], writes=[('ps', mb)], inc=True)
        if R < 4:
            return
        cos = tab[:, 0, 0:N]
        sin = tab[:, 1, 0:N]
        if kind == 'A':
            self.tt(T[0][:, 0:N], src, cos, ALU.mult, reads=[pk, tabkey], writes=['t0'])
        else:
            self.stt(T[0][:, 0:N], src, g, cos, ALU.mult, ALU.mult, reads=[pk, tabkey, 'small'], writes=['t0'])
        if R < 5:
            return
        self.tt(T[1][:, 0:N], ps[rb][:, 0:N], sin, ALU.mult, reads=[('ps', rb), tabkey], writes=['t1'])
        if kind == 'A':
            self.tt(out_ap, T[0][:, 0:N], T[1][:, 0:N], ALU.add, reads=['t0', 't1'], writes=[out_key])
        else:
            self.tt(T[2][:, 0:N], T[0][:, 0:N], T[1][:, 0:N], ALU.add, reads=['t0', 't1'], writes=['t2'])
            if R < 6:
                return
            self.powact(T[3][:, 0:N], ps[7][:, 0:N], -0.5, [('ps', 7)], ['t3'], bias=self.cf(C_EPS6, 1), extra_reads=['cst'])
            if R < 7:
                return
            self.tt(out_ap, T[2][:, 0:N], T[3][:, 0:N], ALU.mult, reads=['t2', 't3'], writes=[out_key])

    def load_tab(self, src_ap):
        i = self.tabi
        n = src_ap.shape[-1]
        self.P.dma(self.tabs[i][:, :, 0:n], src_ap, writes=[('tab', i)])
        return self.tabs[i], ('tab', i)

    def xall(self, t0, n):
        if self.mode == 'F':
            return self.d_xallp[:, :, t0:t0 + n]
        return self.d_xall[:, :, t0:t0 + n]

    def xall_keys(self, t0, n):
        if self.mode == 'F':
            return [('xown', t0 // NTOK, (t0 % NTOK) // GS)]
        return []

    def xown_ap(self, n):
        if self.mode == 'F':
            return self.d_xallp[:, :, self.hf * NTOK + n * GS: self.hf * NTOK + (n + 1) * GS]
        return self.d_xown[:, :, n * GS:(n + 1) * GS]

    def xown_key(self, n):
        return ('xown', self.hf, n) if self.mode == 'F' else ('xown', n)

    def tabA_ap(self, c0, n):
        if self.mode == 'F':
            return self.d_tabAh[self.hf][:, :, c0:c0 + n]
        return self.d_tabA[:, :, c0:c0 + n]

    def tabBq_ap(self, n):
        if self.mode == 'F':
            t0 = self.hf * NTOK + n * GS
            return self.d_tabBk[:, :, t0:t0 + GS]
        return self.d_tabBq[:, :, n * GS:(n + 1) * GS]

    def kv_block(self, xsrc, xkey, N, kind, tab_ap, kdst, kkey, V, vt0, vkeyf):
        ps = self.ps
        tab, tkey = self.load_tab(tab_ap)
        for kc in range(8):
            self.mm(ps[0][:, 0:N], lhsT=self.wkv[:, kc, 0:128], rhs=xsrc(kc), start=(kc == 0), stop=(kc == 7),
                    reads=self.WKV_KEYS + [xkey], writes=[('ps', 0)], inc=(kc == 7))
        if DEBUG_SUB >= 2:
            self.rope_evac(0, N, kind, tab, tkey, kdst, kkey, gcol=S_GK)
        for t in range(N // 128 if DEBUG_SUB >= 3 else 0):
            pb = 1 + (t % 2)
            for kc in range(8):
                self.mm(ps[pb][:, 0:128], lhsT=xsrc(kc)[:, t * 128:(t + 1) * 128], rhs=self.wkv[:, kc, 128:256],
                        start=(kc == 0), stop=(kc == 7), reads=self.WKV_KEYS + [xkey], writes=[('ps', pb)], inc=(kc == 7))
            vt = vt0 + t
            self.P.op('act', lambda e, pb=pb, vt=vt: e.copy(
                out=V[:, vt, :, 0:64], in_=ps[pb][:, 0:128].rearrange("p (g d) -> p g d", g=2)),
                reads=[('ps', pb)], writes=[vkeyf(vt)])

    def kv_passes(self, li, do_b=True, do_a=True):
        P = self.P
        W = weight_offsets()
        if do_b:
            self.kv_pass_b(li)
        if do_a:
            self.kv_pass_a(li)

    def kv_pass_b(self, li):
        P = self.P
        W = weight_offsets()
        off, e = W['kvB']
        P.dma(self.arena[:, 0:4, :].rearrange("p a b -> p (a b)"), self.d_wbf[li, :, off:off + e], reads=[('wbf', li)], writes=self.WKV_KEYS)
        for tg in range(8):
            xb = self.xg[tg % 2]
            xk = ('xg', tg % 2)
            P.dma(xb[:, :, 0:512], self.xall(tg * 512, 512), reads=self.xall_keys(tg * 512, 512), writes=[xk])
            self.kv_block(lambda kc, xb=xb: xb[:, kc, 0:512], xk, 512, 'B',
                          self.d_tabBk[:, :, tg * 512:(tg + 1) * 512],
                          self.KB[:, tg * 512:(tg + 1) * 512], ('KB', tg), self.VB, tg * 4, lambda vt: ('VB', vt // 4))

    def kv_pass_a(self, li):
        P = self.P
        W = weight_offsets()
        off, e = W['kvA']
        P.dma(self.arena[:, 0:4, :].rearrange("p a b -> p (a b)"), self.d_wbf[li, :, off:off + e], reads=[('wbf', li)], writes=self.WKV_KEYS)
        for n in range(NG):
            xb = self.xg[n % 2]
            xk = ('xg', n % 2)
            P.dma(xb[:, :, 0:512], self.xown_ap(n), reads=[self.xown_key(n)], writes=[xk])
            self.kv_block(lambda kc, xb=xb: xb[:, kc, 0:512], xk, 512, 'A',
                          self.tabA_ap(128 + n * 512, 512),
                          self.KA[:, 128 + n * 512:128 + (n + 1) * 512], ('KA', 1 + n), self.VA, 1 + n * 4,
                          lambda vt: ('VA', vt))
        for side in range(2):
            xb = self.xg[side]
            xk = ('xg', side)
            g0 = 1920 if side == 0 else 2048
            P.dma(xb[:, :, 0:128], self.xall(g0, 128), reads=self.xall_keys(g0, 128), writes=[xk])
            tcol = 0 if side == 0 else 2176
            self.kv_block(lambda kc, xb=xb: xb[:, kc, 0:128], xk, 128, 'A',
                          self.tabA_ap(tcol, 128),
                          self.KA[:, tcol:tcol + 128], ('KA', 0 if side == 0 else 5), self.VA,
                          0 if side == 0 else 17, lambda vt: ('VA', vt))

    def ka_keys(self, J):
        if J == 0:
            return ('KA', 0)
        if J == 17:
            return ('KA', 5)
        return ('KA', 1 + (J - 1) // 4)

    def mem_kv(self, li):
        ps = self.ps
        self.P.dma(self.xg[1][:, :, 0:256], self.d_memT[:, :, :], writes=[('xg', 1)], queue='pool')
        wk, wkk = self.wget(li, 'xk')
        wk = wk.rearrange("p (k c) -> p k c", k=8)
        for h in range(4):
            pb = h % 2
            for kc in range(8):
                self.mm(ps[pb][:, 0:256], lhsT=wk[:, kc, h * 128:(h + 1) * 128], rhs=self.xg[1][:, kc, 0:256],
                        start=(kc == 0), stop=(kc == 7), reads=[wkk, ('xg', 1)], writes=[('ps', pb)], inc=(kc == 7))
            self.act(self.KmT[:, h, :], ps[pb][:, 0:256], AF.Copy, reads=[('ps', pb)], writes=['KmT'])
        wv, wvk = self.wget(li, 'xv')
        wv = wv.rearrange("p (k c) -> p k c", k=8)
        for mt in range(2):
            pb = 2 + mt
            for kc in range(8):
                self.mm(ps[pb][:, 0:512], lhsT=self.xg[1][:, kc, mt * 128:(mt + 1) * 128], rhs=wv[:, kc, :],
                        start=(kc == 0), stop=(kc == 7), reads=[wvk, ('xg', 1)], writes=[('ps', pb)], inc=(kc == 7))
            self.act(self.Vm[:, mt, :], ps[pb][:, 0:512], AF.Copy, reads=[('ps', pb)], writes=['Vm'])

    def attn_norm(self, ob, slot, sink_h=None):
        ps, T = self.ps, self.T
        ok = ('ps', ob)
        rd = T[4]
        if sink_h is None:
            self.powact(rd[64:65, :], ps[ob][64:65, :], -1.0, [ok], ['t4'])
        else:
            self.powact(rd[64:65, :], ps[ob][64:65, :], -1.0, [ok], ['t4'],
                        bias=self.sinkexp[64:65, sink_h:sink_h + 1], extra_reads=['sinkexp'])
        bb = 6
        self.mm(ps[bb][0:64, :], lhsT=self.cf(C_ONES, 64, slice(64, 65)), rhs=rd[64:65, :], start=True, stop=True,
                reads=['t4', 'cst'], writes=[('ps', bb)], inc=True)
        self.act(T[5][0:64, :], ps[ob][0:64, :], AF.Copy, reads=[ok], writes=['t5'])
        self.tt(self.arena[0:64, slot, :], T[5][0:64, :], ps[bb][0:64, :], ALU.mult,
                reads=['t5', ('ps', bb)], writes=[('ar', slot)])

    def ln_accum(self, c, n, hb):
        ps, T = self.ps, self.T
        zs = self.xT32[:, c, n * GS:(n + 1) * GS]
        zk = ('x32', c, n)
        self.stt(zs, zs, ALPHA, ps[hb][:, :], ALU.mult, ALU.add, reads=[('ps', hb), zk], writes=[zk])
        sq = T[6 + (c % 2)]
        sqk = 't6' if c % 2 == 0 else 't7'
        self.act(sq[:], zs, AF.Square, reads=[zk], writes=[sqk])
        ones = self.cf(C_ONESLN, 128)
        self.mm(ps[6][:, :], lhsT=ones, rhs=zs, start=(c == 0), stop=(c == 7), reads=[zk, 'cst'],
                writes=[('ps', 6)], inc=False)
        self.mm(ps[7][:, :], lhsT=ones, rhs=sq[:], start=(c == 0), stop=(c == 7), reads=[sqk, 'cst'],
                writes=[('ps', 7)], inc=True)

    def ln_finish(self, n, lncol, dst_fn):
        ps, T = self.ps, self.T
        self.act(T[0][:], ps[6][:, :], AF.Square, reads=[('ps', 6)], writes=['t0'])
        self.stt(T[1][:], ps[7][:, :], LN_EPS, T[0][:], ALU.add, ALU.subtract, reads=[('ps', 7), 't0'], writes=['t1'])
        self.powact(T[2][:], T[1][:], -0.5, ['t1'], ['t2'])
        self.stt(T[3][:], ps[6][:, :], -1.0, T[2][:], ALU.mult, ALU.mult, reads=[('ps', 6), 't2'], writes=['t3'])
        for c in range(8):
            zs = self.xT32[:, c, n * GS:(n + 1) * GS]
            zk = ('x32', c, n)
            a = T[4 + (c % 2)]
            ak = 't4' if c % 2 == 0 else 't5'
            self.tt(a[:], zs, T[2][:], ALU.mult, reads=[zk, 't2'], writes=[ak])
            self.tt(a[:], a[:], T[3][:], ALU.add, reads=[ak, 't3'], writes=[ak])
            g = self.small[:, lncol + c:lncol + c + 1]
            b = self.small[:, lncol + 8 + c:lncol + 8 + c + 1]
            self.act(zs, a[:], AF.Identity, reads=[ak, 'small'], writes=[zk], scale=g, bias=b)
            dst, dk = dst_fn(c)
            self.ts(dst, a[:], g, b, ALU.mult, ALU.add, reads=[ak, 'small'], writes=[dk])

    def mixer_group(self, li, n):
        P, ps, T, Bt = self.P, self.ps, self.T, self.Bt
        ar = self.arena
        xb = self.xg[0]
        xk = ('xg', 0)
        P.dma(xb[:, :, 0:512], self.xown_ap(n), reads=[self.xown_key(n)], writes=[xk])
        xs = lambda kc: xb[:, kc, 0:512]
        wq, wqk = self.wget(li, 'qB')
        wq = wq.rearrange("p (k c) -> p k c", k=8)
        tab, tkey = self.load_tab(self.tabBq_ap(n))
        SB = [1, 2, 3]
        pending = None
        QB = [(Bt[2], 'b2'), (Bt[3], 'b3')]

        def prep_b(i):
            for kc in range(8):
                self.mm(ps[0][:, :], lhsT=wq[:, kc, i * 128:(i + 1) * 128], rhs=xs(kc), start=(kc == 0), stop=(kc == 7),
                        reads=[wqk, xk], writes=[('ps', 0)], inc=(kc == 7))
            self.rope_evac(0, 512, 'B', tab, tkey, QB[i % 2][0][:], QB[i % 2][1], gcol=S_GQ)
        prep_b(0)
        for i in range(4):
            qb, qbk = QB[i % 2]

            for gi in range(2):
                P.op('pool', lambda e, gi=gi, qb=qb: e.tensor_copy(out=self.qpad[gi][gi * 64:(gi + 1) * 64, :],
                                                                   in_=qb[gi * 64:(gi + 1) * 64, :]),
                     reads=[qbk], writes=[('qp', gi)])

            def st_b(g, kt, qb=qb, qbk=qbk):
                sb = SB[kt % 3]
                self.mm(ps[sb][:, :], lhsT=self.KB[:, kt * 128:(kt + 1) * 128],
                        rhs=self.qpad[g][:, :], start=True, stop=True,
                        reads=[('KB', kt // 4), ('qp', g)], writes=[('ps', sb)], inc=True)
            for g in range(2):
                ob = 4 if g == 0 else 5
                st_b(g, 0)
                st_b(g, 1)
                for kt in range(32):
                    sb = SB[kt % 3]
                    pt = self.pt[kt % 3]
                    pk = ('pt', kt % 3)
                    self.act(pt[:], ps[sb][:, :], AF.Exp, reads=[('ps', sb)], writes=[pk], scale=0.125)
                    if kt + 2 < 32:
                        st_b(g, kt + 2)
                    self.mm(ps[ob][0:65, :], lhsT=self.VB[:, kt, g, :], rhs=pt[:], start=(kt == 0), stop=(kt == 31),
                            reads=[('VB', kt // 4), pk], writes=[('ps', ob)], inc=(kt == 31))
                    if kt == 6 and pending is not None:
                        self.attn_norm(*pending)
                        pending = None
                    if g == 1 and kt == 12 and i + 1 < 4:
                        prep_b(i + 1)
                if pending is not None:
                    self.attn_norm(*pending)
                pending = (ob, 8 + g * 4 + i, None)
        if pending is not None:
            self.attn_norm(*pending)
            pending = None
        wq, wqk = self.wget(li, 'qA')
        wq = wq.rearrange("p (k c) -> p k c", k=8)
        tab, tkey = self.load_tab(self.tabA_ap(128 + n * GS, GS))
        def prep_a(i):
            for kc in range(8):
                self.mm(ps[0][:, :], lhsT=wq[:, kc, i * 128:(i + 1) * 128], rhs=xs(kc), start=(kc == 0), stop=(kc == 7),
                        reads=[wqk, xk], writes=[('ps', 0)], inc=(kc == 7))
            self.rope_evac(0, 512, 'A', tab, tkey, QB[i % 2][0][:], QB[i % 2][1])
        prep_a(0)
        for i in range(4):
            qa, qak = QB[i % 2]

            def st_a(it, qa=qa, qak=qak):
                g, jb = it // 4, it % 4
                J = n * 4 + jb
                sb = SB[it % 3]
                for r in range(3):
                    self.mm(ps[sb][:, r * 128:(r + 1) * 128],
                            lhsT=self.KA[g * 64:(g + 1) * 64, (J + r) * 128:(J + r + 1) * 128],
                            rhs=qa[g * 64:(g + 1) * 64, jb * 128:(jb + 1) * 128], start=True, stop=True,
                            reads=[self.ka_keys(J + r), qak], writes=[('ps', sb)], inc=(r == 2))
            st_a(0)
            st_a(1)
            for it in range(8):
                g, jb = it // 4, it % 4
                J = n * 4 + jb
                ob = 4 if g == 0 else 5
                sb = SB[it % 3]
                e32 = T[6 + (it % 2)]
                ek = 't6' if it % 2 == 0 else 't7'
                self.act(e32[:, 0:384], ps[sb][:, 0:384], AF.Exp, reads=[('ps', sb)], writes=[ek], scale=0.125)
                first_blk = (n == 0 and jb == 0)
                last_blk = (n == NG - 1 and jb == 3)
                if first_blk and (self.mode != 'F' or self.hf == 0):
                    mcol = C_MASK + 384
                elif last_blk and (self.mode != 'F' or self.hf == 1):
                    mcol = C_MASK + 768
                else:
                    mcol = C_MASK
                pt = self.pt[it % 3]
                pk = ('pt', it % 3)
                self.tt(pt[:, 0:384], e32[:, 0:384], self.cf(mcol, 384), ALU.mult, reads=[ek, 'cst'], writes=[pk])
                if it + 2 < 8:
                    st_a(it + 2)
                for r in range(3):
                    self.mm(ps[ob][0:65, jb * 128:(jb + 1) * 128], lhsT=self.VA[:, J + r, g, :],
                            rhs=pt[:, r * 128:(r + 1) * 128], start=(r == 0), stop=(r == 2),
                            reads=[('VA', J + r), pk], writes=[('ps', ob)], inc=(r == 2))
                if it == 1 and pending is not None:
                    self.attn_norm(*pending)
                    pending = None
                if it == 3 and i + 1 < 4:
                    prep_a(i + 1)
                if jb == 3:
                    if pending is not None:
                        self.attn_norm(*pending)
                    pending = (ob, g * 4 + i, g * 4 + i)
        if pending is not None:
            self.attn_norm(*pending)
            pending = None
        wu, wuk = self.wget(li, 'wu')
        wu = wu.rearrange("p (k c) -> p k c", k=8)
        for c in range(4):
            pb = c % 2
            for kc in range(8):
                self.mm(ps[pb][:, :], lhsT=wu[:, kc, c * 128:(c + 1) * 128], rhs=xs(kc), start=(kc == 0), stop=(kc == 7),
                        reads=[wuk, xk], writes=[('ps', pb)], inc=(kc == 7))
            self.act(self.uT[:, c, :], ps[pb][:, :], AF.Gelu, reads=[('ps', pb)], writes=[('ar', 24 + c)])
        wv, wvk = self.wget(li, 'wv')
        wv = wv.rearrange("p (k c) -> p k c", k=8)
        for t in range(4):
            pb = 2 + (t % 2)
            for kc in range(8):
                self.mm(ps[pb][:, :], lhsT=xb[:, kc, t * 128:(t + 1) * 128], rhs=wv[:, kc, :], start=(kc == 0),
                        stop=(kc == 7), reads=[wvk, xk], writes=[('ps', pb)], inc=(kc == 7))
            v32 = T[0 + (t % 2)]
            vk = 't0' if t % 2 == 0 else 't1'
            self.act(v32[:], ps[pb][:, :], AF.Gelu, reads=[('ps', pb)], writes=[vk])
            st = self.mv[:, 0:6]
            P.op('dve', lambda e, v32=v32: e.bn_stats(out=self.mv[:, 0:6], in_=v32[:]), reads=[vk], writes=['mv6'])
            P.op('dve', lambda e: e.bn_aggr(out=self.mv[:, 8:10], in_=self.mv[:, 0:6]), reads=['mv6'], writes=['mv2'])
            self.powact(self.mv[:, 10:11], self.mv[:, 9:10], -0.5, ['mv2'], ['mvr'], bias=self.cf(C_EPS5, 1), extra_reads=['cst'])
            self.ts(T[2][:], v32[:], self.mv[:, 8:9], self.mv[:, 10:11], ALU.subtract, ALU.mult,
                    reads=[vk, 'mv2', 'mvr'], writes=['t2'])
            self.tt(T[3][:], T[2][:], self.small[:, S_CLG:S_CLG + 512], ALU.mult, reads=['t2', 'small'], writes=['t3'])
            vc = Bt[3]
            self.tt(vc[:], T[3][:], self.small[:, S_CLB:S_CLB + 512], ALU.add, reads=['t3', 'small'], writes=['b3'])
            for g4 in range(4):
                self.mm(ps[4 + g4][:, t * 128:(t + 1) * 128], lhsT=vc[:, g4 * 128:(g4 + 1) * 128], rhs=self.wsT[:, g4, :],
                        start=True, stop=True, reads=['b3', 'wsT'], writes=[('ps', 4 + g4)], inc=(g4 == 3))
        for g4 in range(4):
            bsv = self.small[:, S_BS + g4 * 128:S_BS + (g4 + 1) * 128]
            a = T[4 + (g4 % 2)]
            ak = 't4' if g4 % 2 == 0 else 't5'
            for t in range(4):
                self.tt(a[:, t * 128:(t + 1) * 128], ps[4 + g4][:, t * 128:(t + 1) * 128], bsv, ALU.add,
                        reads=[('ps', 4 + g4), 'small'], writes=[ak])
            self.tt(ar[:, 16 + g4, :], a[:], self.uT[:, g4, :], ALU.mult, reads=[ak, ('ar', 24 + g4)], writes=[('ar', 16 + g4)])
        if DEBUG_M < 4:
            return
        acc = [T[0], T[1], T[2], T[3]]
        acck = ['t0', 't1', 't2', 't3']
        for cq in range(2):
            for nb in range(3):
                wg, wgk = self.wget(li, f'g{nb}{cq}')
                wg = wg.rearrange("p (k c) -> p k c", k=8)
                wb, wbk = self.wget(li, f'br{nb}{cq}')
                if nb < 2:
                    wb = wb.rearrange("p (k c) -> p k c", k=8)
                else:
                    wb = wb.rearrange("p (k c) -> p k c", k=4)
                for cc in range(4):
                    c = cq * 4 + cc
                    gb = cc % 2
                    for kc in range(8):
                        self.mm(ps[gb][:, :], lhsT=wg[:, kc, cc * 128:(cc + 1) * 128], rhs=xs(kc), start=(kc == 0),
                                stop=(kc == 7), reads=[wgk, xk], writes=[('ps', gb)], inc=(kc == 7))
                    sg = T[4 + (cc % 2)]
                    sgk = 't4' if cc % 2 == 0 else 't5'
                    self.act(sg[:], ps[gb][:, :], AF.Sigmoid, reads=[('ps', gb), 'small'], writes=[sgk],
                             bias=self.small[:, S_BG + nb * 8 + c:S_BG + nb * 8 + c + 1])
                    bb = 2 + (cc % 2)
                    if nb < 2:
                        for h in range(8):
                            self.mm(ps[bb][:, :], lhsT=wb[0:64, h, cc * 128:(cc + 1) * 128], rhs=ar[0:64, nb * 8 + h, :],
                                    start=(h == 0), stop=(h == 7), reads=[wbk, ('ar', nb * 8 + h)], writes=[('ps', bb)],
                                    inc=(h == 7))
                    else:
                        for kc in range(4):
                            self.mm(ps[bb][:, :], lhsT=wb[:, kc, cc * 128:(cc + 1) * 128], rhs=ar[:, 16 + kc, :],
                                    start=(kc == 0), stop=(kc == 3), reads=[wbk, ('ar', 16 + kc)], writes=[('ps', bb)],
                                    inc=(kc == 3))
                    mk = acck[cc]
                    ma = acc[cc][:]
                    if nb == 0:
                        self.tt(ma, ps[bb][:, :], sg[:], ALU.mult, reads=[('ps', bb), sgk], writes=[mk])
                    else:
                        pr = T[6 + (cc % 2)]
                        prk = 't6' if cc % 2 == 0 else 't7'
                        self.tt(pr[:], ps[bb][:, :], sg[:], ALU.mult, reads=[('ps', bb), sgk], writes=[prk])
                        if nb == 1:
                            self.tt(ma, ma, pr[:], ALU.add, reads=[mk, prk], writes=[mk])
                        else:
                            self.tt(ar[:, 20 + c, :], ma, pr[:], ALU.add, reads=[mk, prk], writes=[('ar', 20 + c)])
        if DEBUG_M < 5:
            return
        for cq in range(2):
            wm, wmk = self.wget(li, f'mo{cq}')
            wm = wm.rearrange("p (k c) -> p k c", k=8)
            for cc in range(4):
                c = cq * 4 + cc
                hb = c % 2
                for kc in range(8):
                    self.mm(ps[hb][:, :], lhsT=wm[:, kc, cc * 128:(cc + 1) * 128], rhs=ar[:, 20 + kc, :], start=(kc == 0),
                            stop=(kc == 7), reads=[wmk, ('ar', 20 + kc)], writes=[('ps', hb)], inc=(kc == 7))
                self.ln_accum(c, n, hb)
        x1 = self.xg[1]
        self.ln_finish(n, S_LN + 0, lambda c: (x1[:, c, 0:512], ('xg', 1)))
        if DEBUG_M < 6:
            return
        wq, wqk = self.wget(li, 'xq')
        wq = wq.rearrange("p (k c) -> p k c", k=8)
        sc = 1.0 / np.sqrt(128.0)
        for h in range(4):
            for kc in range(8):
                self.mm(ps[0][:, :], lhsT=wq[:, kc, h * 128:(h + 1) * 128], rhs=x1[:, kc, 0:512], start=(kc == 0),
                        stop=(kc == 7), reads=[wqk, ('xg', 1)], writes=[('ps', 0)], inc=(kc == 7))
            qx = Bt[0]
            self.act(qx[:], ps[0][:, :], AF.Copy, reads=[('ps', 0)], writes=['b0'])
            for mt in range(2):
                sb = 1 + mt
                self.mm(ps[sb][:, :], lhsT=self.KmT[:, h, mt * 128:(mt + 1) * 128], rhs=qx[:], start=True, stop=True,
                        reads=['KmT', 'b0'], writes=[('ps', sb)], inc=True)
                self.act(self.pt[mt][:], ps[sb][:, :], AF.Exp, reads=[('ps', sb)], writes=[('pt', mt)], scale=float(sc))
            for mt in range(2):
                self.mm(ps[3][:, :], lhsT=self.Vm[:, mt, h * 128:(h + 1) * 128], rhs=self.pt[mt][:], start=(mt == 0),
                        stop=(mt == 1), reads=['Vm', ('pt', mt)], writes=[('ps', 3)], inc=(mt == 1))
            for mt in range(2):
                self.mm(ps[4][:, :], lhsT=self.cb(C_ONES), rhs=self.pt[mt][:], start=(mt == 0),
                        stop=(mt == 1), reads=['cstb', ('pt', mt)], writes=[('ps', 4)], inc=(mt == 1))
            self.powact(T[0][:], ps[4][:, :], -1.0, [('ps', 4)], ['t0'])
            self.tt(ar[:, h, :], ps[3][:, :], T[0][:], ALU.mult, reads=[('ps', 3), 't0'], writes=[('ar', h)])
        for cq in range(2):
            wo, wok = self.wget(li, f'xo{cq}')
            wo = wo.rearrange("p (k c) -> p k c", k=4)
            for cc in range(4):
                c = cq * 4 + cc
                hb = c % 2
                for kc in range(4):
                    self.mm(ps[hb][:, :], lhsT=wo[:, kc, cc * 128:(cc + 1) * 128], rhs=ar[:, kc, :], start=(kc == 0),
                            stop=(kc == 3), reads=[wok, ('ar', kc)], writes=[('ps', hb)], inc=(kc == 3))
                self.ln_accum(c, n, hb)
        x2 = self.xg[0]
        self.ln_finish(n, S_LN + 16, lambda c: (x2[:, c, 0:512], ('xg', 0)))
        if self.mode == 'F':
            t0 = self.hf * NTOK + n * GS
            P.dma(self.d_x2all[:, :, t0:t0 + GS], x2[:, :, 0:512], reads=[('xg', 0)], writes=[('x2own', self.hf, n)])
        else:
            P.dma(self.d_x2own[:, :, n * GS:(n + 1) * GS], x2[:, :, 0:512], reads=[('xg', 0)], writes=[('x2own', n)])

    def ffn_group(self, li, n, last):
        P, ps, T = self.P, self.ps, self.T
        ar = self.arena
        xb = self.xg[n % 2]
        xk = ('xg', n % 2)
        if self.mode == 'F':
            T0 = self.hf * NTOK + n * GS
            lo, hi, c0 = T0 - 1, T0 + GS + 1, 0
            if lo < 0:
                P.op('dve', lambda e, xb=xb: e.memset(xb[:, :, 0:1], 0.0), writes=[xk])
                lo, c0 = 0, 1
            if hi > 4096:
                P.op('dve', lambda e, xb=xb: e.memset(xb[:, :, 513:514], 0.0), writes=[xk])
                hi = 4096
            rk = [('x2own', h, j) for h in range(2) for j in range(NG)]
            P.dma(xb[:, :, c0:c0 + (hi - lo)], self.d_x2all[:, :, lo:hi], reads=rk, writes=[xk])
        else:
            lo = n * GS - 1
            hi = n * GS + GS + 1
            c0 = 0
            if lo < 0:
                P.dma(xb[:, :, 0:1], self.halo_s[:, :, 0:1], reads=['halo_s'], writes=[xk], allow_slow_non_contiguous=True)
                lo = 0
                c0 = 1
            if hi > NTOK:
                P.dma(xb[:, :, 513:514], self.halo_s[:, :, 1:2], reads=['halo_s'], writes=[xk], allow_slow_non_contiguous=True)
                hi = NTOK
            P.dma(xb[:, :, c0:c0 + (hi - lo)], self.d_x2own[:, :, lo:hi], reads=[('x2own', j) for j in range(NG)], writes=[xk])
        cvk = lambda j, col: self.small[:, S_CV + 44 * j + col: S_CV + 44 * j + col + 1]
        for jq in range(6):
            nch = 4 if jq < 5 else 2
            wa, wak = self.wget(li, f'ua{jq}')
            wa = wa.rearrange("p (k c) -> p k c", k=8)
            wb, wbk = self.wget(li, f'ub{jq}')
            wb = wb.rearrange("p (k c) -> p k c", k=8)
            for cc in range(nch):
                j = jq * 4 + cc
                hc = []
                for half, (w, wk) in enumerate(((wa, wak), (wb, wbk))):
                    col = j + 22 * half
                    pb = 2 * half
                    for kc in range(8):
                        self.mm(ps[pb][:, :], lhsT=w[:, kc, cc * 128:(cc + 1) * 128], rhs=xb[:, kc, 1:513], start=(kc == 0),
                                stop=(kc == 7), reads=[wk, xk], writes=[('ps', pb)], inc=(kc == 7))
                    for kc in range(8):
                        self.mm(ps[pb + 1][:, 0:2], lhsT=w[:, kc, cc * 128:(cc + 1) * 128], rhs=xb[:, kc, 0:514:513],
                                start=(kc == 0), stop=(kc == 7), reads=[wk, xk], writes=[('ps', pb + 1)], inc=(kc == 7))
                    he = self.hext[half]
                    hk = self.hext_keys[half]
                    self.act(he[:, 1:513], ps[pb][:, :], AF.Copy, reads=[('ps', pb)], writes=hk)
                    self.act(he[:, 0:514:513], ps[pb + 1][:, 0:2], AF.Copy, reads=[('ps', pb + 1)], writes=hk)
                    a = T[2 * half]
                    ak = f't{2 * half}'
                    self.ts(a[:], he[:, 0:512], cvk(0, col), cvk(3, col), ALU.mult, ALU.add, reads=hk + ['small'], writes=[ak])
                    self.stt(a[:], he[:, 1:513], cvk(1, col), a[:], ALU.mult, ALU.add, reads=hk + ['small', ak], writes=[ak])
                    b2 = T[2 * half + 1]
                    bk = f't{2 * half + 1}'
                    self.stt(b2[:], he[:, 2:514], cvk(2, col), a[:], ALU.mult, ALU.add, reads=hk + ['small', ak], writes=[bk])
                    hc.append((b2, bk))
                ga = T[4 + (j % 2)]
                gk = 't4' if j % 2 == 0 else 't5'
                self.act(ga[:], hc[0][0][:], AF.Gelu, reads=[hc[0][1]], writes=[gk])
                self.tt(ar[:, j, :], ga[:], hc[1][0][:], ALU.mult, reads=[gk, hc[1][1]], writes=[('ar', j)])
        for c in range(8):
            wd, wdk = self.wget(li, f'dn{c}')
            wd = wd.rearrange("p (k c) -> p k c", k=22)
            hb = 4 + (c % 2)
            for kc in range(22):
                self.mm(ps[hb][:, :], lhsT=wd[:, kc, :], rhs=ar[:, kc, :], start=(kc == 0), stop=(kc == 21),
                        reads=[wdk, ('ar', kc)], writes=[('ps', hb)], inc=(kc == 21))
            self.ln_accum(c, n, hb)
        if self.mode == 'F':
            ob = self.KB[:].rearrange("p (c t) -> p c t", c=8)
            okeys = [('KB', c) for c in range(8)]
        else:
            ob = self.xo_buf[:]
            okeys = ['xo_buf'] * 8
        self.ln_finish(n, S_LN + 32, lambda c: (ob[:, c, :], okeys[c]))
        if not last:
            P.dma(self.xown_ap(n), ob, reads=list(set(okeys)), writes=[self.xown_key(n)])

    def cast_layer(self, li):
        tot = weight_offsets()['_tot']
        nch = 16
        step = (tot + nch - 1) // nch
        for j in range(nch):
            a, b = j * step, min(tot, (j + 1) * step)
            self.P.dma(self.d_wbf[li, :, a:b], self.d_wblk[li, :, a:b], writes=[('wbf', li)], queue='pool')

    def switch_half(self, hf, first_touch):
        P = self.P
        if self.resident == hf:
            self.hf = hf
            return
        if self.resident is not None:
            r = self.resident
            for c in range(8):
                P.dma(self.d_park[r, :, c, :], self.xT32[:, c, :], reads=[('x32', c, n) for n in range(NG)],
                      writes=[('park', r, c)])
        src = self.d_xT32h if first_touch else self.d_park
        for c in range(8):
            P.dma(self.xT32[:, c, :], src[hf, :, c, :], reads=([] if first_touch else [('park', hf, c)]),
                  writes=[('x32', c, n) for n in range(NG)])
        self.resident = hf
        self.hf = hf

    def build(self):
        P = self.P
        assert self.mode == 'F'
        nL = len(self.layers)
        self.xo_buf = None
        self.resident = None
        self.plan_weights()
        self.load_common()
        self.cast_layer(0)
        for hf in range(2):
            self.switch_half(hf, True)
            for n in range(NG):
                xb = self.xg[n % 2]
                for c in range(8):
                    if c % 2:
                        P.op('act', lambda e, c=c, n=n, xb=xb: e.copy(out=xb[:, c, 0:512], in_=self.xT32[:, c, n * GS:(n + 1) * GS]),
                             reads=[('x32', c, n)], writes=[('xg', n % 2)])
                    else:
                        P.op('dve', lambda e, c=c, n=n, xb=xb: e.tensor_copy(out=xb[:, c, 0:512], in_=self.xT32[:, c, n * GS:(n + 1) * GS]),
                             reads=[('x32', c, n)], writes=[('xg', n % 2)])
                P.dma(self.xown_ap(n), xb[:, :, 0:512], reads=[('xg', n % 2)], writes=[self.xown_key(n)])
        P.op('dve', lambda e: e.memset(self.VB[:].rearrange("p a b c -> p (a b c)"), 1.0),
             writes=[('VB', i) for i in range(8)])
        P.op('dve', lambda e: e.memset(self.VA[:].rearrange("p a b c -> p (a b c)"), 1.0),
             writes=[('VA', i) for i in range(18)])
        for gi in range(2):
            P.op('dve', lambda e, gi=gi: e.memset(self.qpad[gi][:], 0.0), writes=[('qp', gi)])
        for li in range(nL):
            if li > 0:
                P.new_epoch()
            self.load_layer_small(li)
            off, e = weight_offsets()['ws']
            P.dma(self.wsT[:].rearrange("p g i -> p (g i)"), self.d_wbf[li, :, off:off + e], reads=[('wbf', li)], writes=['wsT'])
            if li + 1 < nL:
                self.cast_layer(li + 1)
            self.kv_pass_b(li)
            self.mem_kv(li)
            order = [1, 0]
            for hf in order:
                self.switch_half(hf, False)
                self.kv_pass_a(li)
                for n in range(NG):
                    self.mixer_group(li, n)
            for hf in [0, 1]:
                self.switch_half(hf, False)
                for n in range(NG):
                    self.ffn_group(li, n, last=(li == nL - 1))
                if li == nL - 1:
                    for c in range(8):
                        P.dma(self.d_yh[hf, :, c, :], self.xT32[:, c, :], reads=[('x32', c, n) for n in range(NG)],
                              writes=[('y', hf, c)])
        P.finish()
        P.emit()
        return self.nc


_CACHE = {}


def _get_prog(mode, nl):
    key = (mode, nl)
    if key not in _CACHE:
        b = Builder(mode, list(range(nl)))
        _CACHE[key] = b.build()
    return _CACHE[key]


def _bf(a):
    return np.ascontiguousarray(a).view(ml_dtypes.bfloat16) if a.dtype == np.uint16 else a


def kernel(**inp):
    inp = {k: np.asarray(v, dtype=np.float32) for k, v in inp.items()}
    x = inp['x']
    mem = inp['mem']
    cores = list(range(8))
    W = weight_offsets()
    wblk = np.zeros((DEPTH, 128, W['_tot']), np.float32)
    small = np.zeros((DEPTH, 128, NS), np.float32)
    for l in range(DEPTH):
        for nm, a in weight_blocks(inp, l).items():
            off, e = W[nm]
            wblk[l, :, off:off + e] = a
        small[l] = small_params(inp, l)
    tabs = [rope_tables(hf) for hf in range(2)]
    tabA = np.ascontiguousarray(np.stack([tabs[0][0], tabs[1][0]], 0))
    tabBk = tabs[0][2]
    cst = constants(0)
    ins = []
    for c in cores:
        b = c % 4
        xT = np.ascontiguousarray(x[b].T.reshape(8, 128, 2, NTOK).transpose(2, 1, 0, 3))
        memT = np.ascontiguousarray(mem[b].T.reshape(8, 128, 256).transpose(1, 0, 2))
        ins.append({"xT32": xT, "wblk": wblk, "small": small, "cst": cst, "tabA": tabA, "tabBk": tabBk, "memT": memT})
    nc = _get_prog('F', DEPTH)
    res = run_bass_kernel_spmd(nc, ins, core_ids=cores)
    out = np.zeros((4, 4096, 1024), np.float32)
    for b in range(4):
        y = np.asarray(res.results[b]["y"])
        for hf in range(2):
            yt = y[hf].transpose(1, 0, 2).reshape(1024, NTOK)
            out[b, hf * NTOK:(hf + 1) * NTOK, :] = yt.T
    return out
```

```python
import contextlib
import numpy as np
import ml_dtypes
import concourse.bass as bass
import concourse.mybir as mybir
from concourse.bass_utils import run_bass_kernel_spmd

F32 = mybir.dt.float32
BF16 = mybir.dt.bfloat16
AF = mybir.ActivationFunctionType
ALU = mybir.AluOpType
ENGS = ('pe', 'dve', 'act', 'pool', 'sp')

DEPTH = 4
ALPHA = (2 * DEPTH) ** 0.25
LN_EPS = 1e-5
RMS_EPS = 1e-6
NTOK = 2048
DEBUG_STAGE = 9
DEBUG_SUB = 9
DEBUG_R = 9
DEBUG_M = 9
GS = 512
NG = 4
D_FF = 2816


class Prog:
    def __init__(self, nc, n_dma_sems=24):
        self.nc = nc
        self.es = contextlib.ExitStack()
        self.q = {e: [] for e in ENGS}
        self.cnt = {}
        self.seen = {e: {} for e in ENGS}
        self.snap = {}
        self.last_w = {}
        self.readers = {}
        self.sems = {}
        self.epoch = 0
        self.n_dma = {'sp': 16, 'pool': 8, 'act': 4, 'cc': 2}
        self.dma_rr = {'sp': 0, 'pool': 0, 'act': 0, 'cc': 0}
        self.n_wait = 0
        for qn, n in self.n_dma.items():
            for i in range(n):
                self._mk_owner(('d' + qn, i))
        for e in ENGS:
            self._mk_owner((e, 0))

    def _mk_owner(self, o):
        nm = "s_" + "_".join(str(x) for x in o)
        self.sems[o] = self.es.enter_context(self.nc.semaphore(nm))
        self.cnt[o] = 0

    def new_epoch(self):
        self.epoch += 1
        for e in ENGS:
            if e != 'sp':
                self._mk_owner((e, self.epoch))

    def sbuf(self, name, shape, dt):
        return self.es.enter_context(self.nc.sbuf_tensor(name, list(shape), dt))

    def psum(self, name, shape, dt):
        return self.es.enter_context(self.nc.psum_tensor(name, list(shape), dt))

    def _wait(self, E, o, v):
        if self.seen[E].get(o, 0) >= v:
            return
        if o[0] == E:
            if E == 'pe' or E == 'sp':
                return
            if o[1] != self.epoch or v < self.cnt[o] - 1:
                return
        self.q[E].append(('wait', o, v))
        self.n_wait += 1
        self.seen[E][o] = v
        sn = self.snap.get((o, v))
        if sn:
            se = self.seen[E]
            for o2, v2 in sn.items():
                if se.get(o2, 0) < v2:
                    se[o2] = v2

    def _deps(self, reads, writes):
        deps = set()
        for k in reads:
            w = self.last_w.get(k)
            if w is not None:
                deps.add(w)
        for k in writes:
            w = self.last_w.get(k)
            if w is not None:
                deps.add(w)
            for r in self.readers.get(k, ()):
                deps.add(r)
        return deps

    def _commit(self, ident, reads, writes):
        for k in reads:
            self.readers.setdefault(k, []).append(ident)
        for k in writes:
            self.last_w[k] = ident
            self.readers[k] = []

    def op(self, E, fn, reads=(), writes=(), inc=True):
        psr = [k for k in reads if isinstance(k, tuple) and k[0] == 'ps']
        if psr:
            reads = [k for k in reads if not (isinstance(k, tuple) and k[0] == 'ps')]
            writes = list(writes) + psr
        own = (E, self.epoch)
        for (o, v) in sorted(self._deps(reads, writes), key=str):
            self._wait(E, o, v)
        v = self.cnt[own] + 1
        if inc:
            self.cnt[own] = v
            self.snap[(own, v)] = dict(self.seen[E])
        self.q[E].append(('op', fn, own if inc else None, 1))
        self._commit((own, v), reads, writes)

    def dma(self, out, in_, reads=(), writes=(), queue='sp', **kw):
        i = self.dma_rr[queue]
        self.dma_rr[queue] = (i + 1) % self.n_dma[queue]
        own = ('d' + queue, i)
        deps = self._deps(reads, writes)
        if self.cnt[own] > 0:
            deps.add((own, self.cnt[own]))
        for (o, v) in sorted(deps, key=str):
            self._wait(queue, o, v)
        v = self.cnt[own] + 16
        self.cnt[own] = v
        self.snap[(own, v)] = dict(self.seen[queue])
        self.q[queue].append(('op', lambda eng: eng.dma_start(out=out, in_=in_, **kw), own, 16))
        self._commit((own, v), reads, writes)

    def coll(self, fn, reads=(), writes=()):
        i = self.dma_rr['cc']
        self.dma_rr['cc'] = (i + 1) % self.n_dma['cc']
        own = ('dcc', i)
        deps = self._deps(reads, writes)
        if self.cnt[own] > 0:
            deps.add((own, self.cnt[own]))
        for (o, v) in sorted(deps, key=str):
            self._wait('pool', o, v)
        v = self.cnt[own] + 16
        self.cnt[own] = v
        self.snap[(own, v)] = dict(self.seen['pool'])
        self.q['pool'].append(('op', fn, own, 16))
        self._commit((own, v), reads, writes)

    def finish(self):
        for o, c in self.cnt.items():
            if c > 0 and o[0] != 'sp':
                if self.seen['sp'].get(o, 0) < c:
                    self.q['sp'].append(('wait', o, c))
                    self.seen['sp'][o] = c

    def emit(self):
        nc = self.nc
        for o, c in self.cnt.items():
            assert c < 60000, (o, c)
        with nc.Block() as block:
            def replay(E):
                def run(eng):
                    for ent in self.q[E]:
                        if ent[0] == 'wait':
                            eng.wait_ge(self.sems[ent[1]], ent[2])
                        else:
                            ins = ent[1](eng)
                            if ent[2] is not None:
                                ins.then_inc(self.sems[ent[2]], ent[3])
                return run
            block.tensor(replay('pe'))
            block.vector(replay('dve'))
            block.scalar(replay('act'))
            block.gpsimd(replay('pool'))
            block.sync(replay('sp'))
        self.es.close()

    def check(self):
        val = {o: 0 for o in self.cnt}
        pc = {e: 0 for e in ENGS}
        progress = True
        while progress:
            progress = False
            for e in ENGS:
                q = self.q[e]
                while pc[e] < len(q):
                    ent = q[pc[e]]
                    if ent[0] == 'wait':
                        if val[ent[1]] >= ent[2]:
                            pc[e] += 1
                            progress = True
                        else:
                            break
                    else:
                        if ent[2] is not None:
                            val[ent[2]] += ent[3]
                        pc[e] += 1
                        progress = True
        stuck = {e: (pc[e], len(self.q[e]), self.q[e][pc[e]][:3] if pc[e] < len(self.q[e]) else None) for e in ENGS}
        ok = all(pc[e] == len(self.q[e]) for e in ENGS)
        return ok, stuck, {o: (val[o], self.cnt[o]) for o in val if val[o] != self.cnt[o]}

    def stats(self):
        return {e: sum(1 for x in self.q[e] if x[0] == 'op') for e in ENGS}, self.n_wait


def _kmaj(w):
    K, C = w.shape
    return np.ascontiguousarray(w.reshape(K // 128, 128, C).transpose(1, 0, 2)).reshape(128, -1)


def _qreorder(w):
    w4 = w.reshape(w.shape[0], 8, 64)
    order = [h for i in range(4) for h in (i, 4 + i)]
    return w4[:, order, :].reshape(w.shape[0], 512)


def weight_blocks(inp, l):
    w_in = inp['w_in'][l]
    B = {}
    B['kvB'] = _kmaj(w_in[:, 1280:1536])
    B['kvA'] = _kmaj(w_in[:, 512:768])
    B['ws'] = np.ascontiguousarray(inp['c_ws'][l].transpose(2, 0, 1)).reshape(128, 512)
    B['xk'] = _kmaj(inp['x_wkv'][l][:, 0:512])
    B['xv'] = _kmaj(inp['x_wkv'][l][:, 512:1024])
    B['qB'] = _kmaj(_qreorder(w_in[:, 768:1280]))
    B['qA'] = _kmaj(_qreorder(w_in[:, 0:512]))
    B['wu'] = _kmaj(w_in[:, 1536:2048])
    B['wv'] = _kmaj(w_in[:, 2048:2560])
    for nb in range(3):
        for cq in range(2):
            B[f'g{nb}{cq}'] = _kmaj(w_in[:, 2560 + nb * 1024 + cq * 512: 2560 + nb * 1024 + cq * 512 + 512])
            wb = inp['w_branch'][l, nb][:, cq * 512:(cq + 1) * 512]
            if nb < 2:
                a = np.zeros((128, 8, 512), np.float32)
                a[0:64] = wb.reshape(8, 64, 512).transpose(1, 0, 2)
                B[f'br{nb}{cq}'] = a.reshape(128, 4096)
            else:
                B[f'br{nb}{cq}'] = _kmaj(wb)
    for cq in range(2):
        B[f'mo{cq}'] = _kmaj(inp['w_mix_out'][l][:, cq * 512:(cq + 1) * 512])
    B['xq'] = _kmaj(inp['x_wq'][l])
    for cq in range(2):
        B[f'xo{cq}'] = _kmaj(inp['x_wo'][l][:, cq * 512:(cq + 1) * 512])
    up = inp['f_w_up'][l]
    for jq in range(6):
        w = min(512, D_FF - jq * 512)
        B[f'ua{jq}'] = _kmaj(up[:, jq * 512: jq * 512 + w])
        B[f'ub{jq}'] = _kmaj(up[:, D_FF + jq * 512: D_FF + jq * 512 + w])
    for c in range(8):
        B[f'dn{c}'] = _kmaj(inp['f_w_down'][l][:, c * 128:(c + 1) * 128])
    return B


_WOFF = None


def weight_offsets():
    global _WOFF
    if _WOFF is None:
        sizes = [('kvB', 2048), ('kvA', 2048), ('ws', 512), ('xk', 4096), ('xv', 4096), ('qB', 4096), ('qA', 4096),
                 ('wu', 4096), ('wv', 4096)]
        for nb in range(3):
            for cq in range(2):
                sizes.append((f'g{nb}{cq}', 4096))
                sizes.append((f'br{nb}{cq}', 4096 if nb < 2 else 2048))
        sizes += [('mo0', 4096), ('mo1', 4096), ('xq', 4096), ('xo0', 2048), ('xo1', 2048)]
        for jq in range(6):
            w = min(512, D_FF - jq * 512)
            sizes += [(f'ua{jq}', 8 * w), (f'ub{jq}', 8 * w)]
        for c in range(8):
            sizes.append((f'dn{c}', 2816))
        off = 0
        d = {}
        for nm, e in sizes:
            d[nm] = (off, e)
            off += e
        d['_tot'] = off
        _WOFF = d
    return _WOFF


S_BG = 0
S_GQ = 24
S_GK = 25
S_LN = 26
S_CV = 74
S_SINK = 250
S_CLG = 258
S_CLB = 770
S_BS = 1282
NS = 1794


def small_params(inp, l):
    s = np.zeros((128, NS), np.float32)
    s[:, S_BG:S_BG + 24] = inp['b_gate'][l].reshape(24, 128).T
    s[:, S_GQ] = np.tile(inp['b_q_gain'][l], 2)
    s[:, S_GK] = np.tile(inp['b_k_gain'][l], 2)
    for i, nm in enumerate(['ln1_g', 'ln1_b', 'ln2_g', 'ln2_b', 'ln3_g', 'ln3_b']):
        s[:, S_LN + 8 * i: S_LN + 8 * i + 8] = inp[nm][l].reshape(8, 128).T
    ck = inp['f_conv_k'][l]
    for j in range(3):
        s[:, S_CV + 44 * j: S_CV + 44 * j + 44] = ck[j].reshape(44, 128).T
    s[:, S_CV + 132: S_CV + 176] = inp['f_conv_b'][l].reshape(44, 128).T
    s[:, S_SINK:S_SINK + 8] = inp['a_sink'][l][None, :]
    s[:, S_CLG:S_CLG + 512] = inp['c_ln_g'][l][None, :]
    s[:, S_CLB:S_CLB + 512] = inp['c_ln_b'][l][None, :]
    s[:, S_BS:S_BS + 512] = inp['c_bs'][l].reshape(1, 512)
    return s


C_RTA = 0
C_RTB = 128
C_BONES = 256
C_ONES = 384
C_ONESLN = 512
C_MASK = 640
C_FLAG = 1792
C_EPS6 = 1794
C_EPS5 = 1795
NC_ = 1796


def constants(hf):
    c = np.zeros((128, NC_), np.float32)
    R_A = np.zeros((128, 128), np.float32)
    R_B = np.zeros((128, 128), np.float32)
    for i in range(128):
        if i % 64 < 32:
            R_A[i, i + 32] = -1.0
        else:
            R_A[i, i - 32] = 1.0
        if i % 32 < 16:
            R_B[i, i + 16] = -1.0
        else:
            R_B[i, i - 16] = 1.0
    c[:, C_RTA:C_RTA + 128] = R_A.T
    c[:, C_RTB:C_RTB + 128] = R_B.T
    bo = np.zeros((128, 128), np.float32)
    bo[0:64, 0:64] = 1.0 / 64
    bo[64:128, 64:128] = 1.0 / 64
    c[:, C_BONES:C_BONES + 128] = bo
    c[:, C_ONES:C_ONES + 128] = 1.0
    c[:, C_ONESLN:C_ONESLN + 128] = 1.0 / 1024
    ki = np.arange(128)[:, None]
    qi = np.arange(128)[None, :]
    mL = (qi <= ki).astype(np.float32)
    mR = (ki <= qi).astype(np.float32)
    one = np.ones((128, 128), np.float32)
    zero = np.zeros((128, 128), np.float32)
    c[:, C_MASK:C_MASK + 384] = np.concatenate([mL, one, mR], 1)
    c[:, C_MASK + 384:C_MASK + 768] = np.concatenate([zero, one, mR], 1)
    c[:, C_MASK + 768:C_MASK + 1152] = np.concatenate([mL, one, zero], 1)
    c[:, C_FLAG] = 1.0 if hf == 1 else 0.0
    c[:, C_FLAG + 1] = 1.0 if hf == 0 else 0.0
    c[:, C_EPS6] = 1e-6
    c[:, C_EPS5] = 1e-5
    return c


def rope_tables(hf):
    theta = np.float32(10000.0)
    invA = (theta ** (-np.arange(32, dtype=np.float32) * np.float32(2.0 / 64))).astype(np.float32)
    invB = (theta ** (-np.arange(16, dtype=np.float32) * np.float32(2.0 / 32))).astype(np.float32)
    p = np.arange(128)
    d = p % 64

    def tabA(pos):
        ang = pos.astype(np.float32)[None, :] * invA[d % 32][:, None]
        return np.stack([np.cos(ang), np.sin(ang)], 1).astype(np.float32)

    def tabB(pos):
        row = (pos // 64).astype(np.float32)
        col = (pos % 64).astype(np.float32)
        sel = np.where((d < 32)[:, None], row[None, :], col[None, :])
        ang = sel * invB[d % 16][:, None]
        return np.stack([np.cos(ang), np.sin(ang)], 1).astype(np.float32)

    own = np.arange(hf * NTOK, (hf + 1) * NTOK)
    posA = np.concatenate([np.arange(1920, 2048), own, np.arange(2048, 2176)])
    return tabA(posA), tabB(own), tabB(np.arange(4096))


class Builder:
    def __init__(self, mode, layers):
        self.mode = mode
        self.layers = layers
        nL = len(layers)
        nc = bass.Bass("TRN2", target_bir_lowering=False)
        self.nc = nc
        self.P = P = Prog(nc)
        W = weight_offsets()
        dt = nc.dram_tensor
        if mode == 'F':
            self.d_xT32h = dt("xT32", [2, 128, 8, NTOK], F32, kind="ExternalInput").ap()
        else:
            self.d_xT32 = dt("xT32", [128, 8, NTOK], F32, kind="ExternalInput").ap()
        if mode in ('A', 'B', 'F'):
            self.d_wblk = dt("wblk", [nL, 128, W['_tot']], F32, kind="ExternalInput").ap()
            self.d_small = dt("small", [nL, 128, NS], F32, kind="ExternalInput").ap()
            self.d_cst = dt("cst", [128, NC_], F32, kind="ExternalInput").ap()
        if mode == 'F':
            self.d_tabAh = dt("tabA", [2, 128, 2, 2304], F32, kind="ExternalInput").ap()
            self.d_tabBk = dt("tabBk", [128, 2, 4096], F32, kind="ExternalInput").ap()
            self.d_memT = dt("memT", [128, 8, 256], F32, kind="ExternalInput").ap()
        if mode == 'A':
            self.d_tabA = dt("tabA", [128, 2, 2304], F32, kind="ExternalInput").ap()
            self.d_tabBq = dt("tabBq", [128, 2, NTOK], F32, kind="ExternalInput").ap()
            self.d_tabBk = dt("tabBk", [128, 2, 4096], F32, kind="ExternalInput").ap()
            self.d_memT = dt("memT", [128, 8, 256], F32, kind="ExternalInput").ap()
        if mode == 'P':
            self.d_xown = dt("xown", [128, 8, NTOK], BF16, kind="ExternalOutput").ap()
        if mode == 'A':
            self.d_xown = dt("xown", [128, 8, NTOK], BF16, kind="ExternalInput").ap()
            self.d_xall = dt("xall", [128, 8, 4096], BF16, kind="ExternalInput").ap()
            self.d_x2own = dt("x2own", [128, 8, NTOK], BF16, kind="ExternalOutput").ap()
            self.d_xT32o = dt("xT32o", [128, 8, NTOK], F32, kind="ExternalOutput").ap()
        if mode == 'F':
            self.d_xallp = dt("xall_i", [128, 8, 4096], BF16, kind="Internal").ap()
            self.d_x2all = dt("x2all_i", [128, 8, 4096], BF16, kind="Internal").ap()
            self.d_park = dt("park_i", [2, 128, 8, NTOK], F32, kind="Internal").ap()
            self.d_yh = dt("y", [2, 128, 8, NTOK], F32, kind="ExternalOutput").ap()
            self.d_wbf = dt("wbf_i", [nL, 128, W['_tot']], BF16, kind="Internal").ap()
            self.hf = 0
        if mode == 'B':
            self.d_x2own = dt("x2own", [128, 8, NTOK], BF16, kind="ExternalInput").ap()
            self.d_halo = dt("halo", [128, 8, 2], BF16, kind="ExternalInput").ap()
            self.d_xown = dt("xown", [128, 8, NTOK], BF16, kind="ExternalOutput").ap()
            self.d_xT32o = dt("xT32o", [128, 8, NTOK], F32, kind="ExternalOutput").ap()
        self.alloc()

    def alloc(self):
        P = self.P
        m = self.mode
        self.xT32 = P.sbuf("xT32s", [128, 8, NTOK], F32)
        self.ps = [P.psum(f"ps{i}", [128, 512], F32) for i in range(8)]
        self.xg = [P.sbuf(f"xg{i}", [128, 8, 514], BF16) for i in range(2)]
        if m == 'P':
            return
        self.ring = [P.sbuf(f"ring{i}", [128, 4096], BF16) for i in range(3)]
        self.small = P.sbuf("small_s", [128, NS], F32)
        self.cst = P.sbuf("cst_s", [128, NC_ - C_ONES], F32)
        self.cstb = P.sbuf("cstb", [128, 512], BF16)
        self.sinkexp = P.sbuf("sinkexp", [128, 8], F32)
        self.arena = P.sbuf("arena", [128, 28, 512], BF16)
        self.T = [P.sbuf(f"t{i}", [128, 512], F32) for i in range(8)]
        self.uT = self.arena[:, 24:28, :]
        self.wkv = self.arena[:, 0:4, :].rearrange("p a (b c) -> p (a b) c", c=256)
        self.WKV_KEYS = [('ar', i) for i in range(4)]
        self.Bt = [P.sbuf(f"b{i}", [128, 512], BF16) for i in range(4)]
        if m in ('A', 'F'):
            self.KB = P.sbuf("KB", [128, 4096], BF16)
            self.VB = P.sbuf("VB", [128, 32, 2, 65], BF16)
            self.KA = P.sbuf("KA", [128, 2304], BF16)
            self.VA = P.sbuf("VA", [128, 18, 2, 65], BF16)
            self.wsT = P.sbuf("wsT", [128, 4, 128], BF16)
            self.tabs = [P.sbuf(f"tab{i}", [128, 2, 512], F32) for i in range(1)]
            self.pt = [P.sbuf(f"pt{i}", [128, 512], BF16) for i in range(3)]
            self.qpad = [P.sbuf(f"qpad{i}", [128, 512], BF16) for i in range(2)]
            self.KmT = P.sbuf("KmT", [128, 4, 256], BF16)
            self.Vm = P.sbuf("Vm", [128, 2, 512], BF16)
            self.mv = P.sbuf("mv", [128, 16], F32)
        if m == 'B':
            self.hext = [P.sbuf(f"hext{i}", [128, 514], F32) for i in range(2)]
            self.hext_keys = [[('hext', 0)], [('hext', 1)]]
            self.halo_s = P.sbuf("halo_s", [128, 8, 2], BF16)
        if m == 'F':
            self.hext = [self.arena[:, 22:25, :].rearrange("p a b -> p (a b)").bitcast(F32),
                         self.arena[:, 25:28, :].rearrange("p a b -> p (a b)").bitcast(F32)]
            self.hext_keys = [[('ar', i) for i in (22, 23, 24)], [('ar', i) for i in (25, 26, 27)]]
            self.halo_s = P.sbuf("halo_s", [128, 8, 2], BF16)
        self.tabi = 0
        self.wnext = 0
        self.wcur = 0
        self.wplan = []

    def plan_weights(self):
        plan = []
        nh = 2 if self.mode == 'F' else 1
        for li in range(len(self.layers)):
            if self.mode in ('A', 'F'):
                plan += [(li, 'xk'), (li, 'xv')]
                for n in range(NG * nh):
                    plan += [(li, 'qB'), (li, 'qA'), (li, 'wu'), (li, 'wv')]
                    for cq in range(2):
                        for nb in range(3):
                            plan += [(li, f'g{nb}{cq}'), (li, f'br{nb}{cq}')]
                    plan += [(li, 'mo0'), (li, 'mo1'), (li, 'xq'), (li, 'xo0'), (li, 'xo1')]
            if self.mode in ('B', 'F'):
                for n in range(NG * nh):
                    for jq in range(6):
                        plan += [(li, f'ua{jq}'), (li, f'ub{jq}')]
                    for c in range(8):
                        plan.append((li, f'dn{c}'))
        self.wplan = plan

    def _issue_wload(self, k):
        li, nm = self.wplan[k]
        off, e = weight_offsets()[nm]
        slot = k % 3
        self.P.dma(self.ring[slot][:, 0:e], self.d_wbf[li, :, off:off + e], reads=[('wbf', li)],
                   writes=[('ring', slot)], queue='sp')

    def wget(self, li, nm):
        k = self.wcur
        assert self.wplan[k] == (li, nm), (self.wplan[k], li, nm)
        while self.wnext < min(len(self.wplan), k + 2):
            self._issue_wload(self.wnext)
            self.wnext += 1
        self.wcur += 1
        slot = k % 3
        e = weight_offsets()[nm][1]
        return self.ring[slot][:, 0:e], ('ring', slot)

    def mm(self, out, lhsT, rhs, start, stop, reads, writes, inc):
        self.P.op('pe', lambda e: e.matmul(out, lhsT=lhsT, rhs=rhs, start=start, stop=stop),
                  reads=reads, writes=writes, inc=inc)

    def act(self, out, in_, func, reads, writes, **kw):
        self.P.op('act', lambda e: e.activation(out=out, in_=in_, func=func, **kw), reads=reads, writes=writes)

    def tt(self, out, in0, in1, op, reads, writes, eng='dve'):
        self.P.op(eng, lambda e: e.tensor_tensor(out=out, in0=in0, in1=in1, op=op), reads=reads, writes=writes)

    def ts(self, out, in0, s1, s2, op0, op1, reads, writes, eng='dve'):
        if op1 is None:
            self.P.op(eng, lambda e: e.tensor_scalar(out=out, in0=in0, scalar1=s1, scalar2=None, op0=op0),
                      reads=reads, writes=writes)
        else:
            self.P.op(eng, lambda e: e.tensor_scalar(out=out, in0=in0, scalar1=s1, scalar2=s2, op0=op0, op1=op1),
                      reads=reads, writes=writes)

    def stt(self, out, in0, scalar, in1, op0, op1, reads, writes):
        self.P.op('dve', lambda e: e.scalar_tensor_tensor(out=out, in0=in0, scalar=scalar, in1=in1, op0=op0, op1=op1),
                  reads=reads, writes=writes)

    def powact(self, out, in_, expo, reads, writes, bias=None, extra_reads=()):
        if bias is None:
            self.act(out, in_, AF.Ln, reads=list(reads), writes=list(writes))
        else:
            self.act(out, in_, AF.Ln, reads=list(reads) + list(extra_reads), writes=list(writes), bias=bias)
        self.act(out, out, AF.Exp, reads=list(writes), writes=list(writes), scale=float(expo))

    def cb(self, col):
        return self.cstb[:, col:col + 128]

    def cf(self, col, n, rows=slice(None)):
        return self.cst[rows, col - C_ONES:col - C_ONES + n]

    def load_common(self):
        P = self.P
        P.dma(self.cst[:], self.d_cst[:, C_ONES:NC_], writes=['cst'])
        P.dma(self.cstb[:], self.d_cst[:, 0:512], writes=['cstb'], queue='pool')

    def load_layer_small(self, li):
        P = self.P
        P.dma(self.small[:], self.d_small[li, :, :], writes=['small'])
        if self.mode in ('A', 'F'):
            self.act(self.sinkexp[:], self.small[:, S_SINK:S_SINK + 8], AF.Exp, reads=['small'], writes=['sinkexp'])

    def load_state(self):
        for c in range(8):
            self.P.dma(self.xT32[:, c, :], self.d_xT32[:, c, :], writes=[('x32', c, n) for n in range(NG)])

    def store_state(self, dst):
        for c in range(8):
            self.P.dma(dst[:, c, :], self.xT32[:, c, :], reads=[('x32', c, n) for n in range(NG)], writes=[('dst32', c)])

    def rope_evac(self, psb, N, kind, tab, tabkey, out_ap, out_key, gcol=None):
        ps = self.ps
        T, Bt = self.T, self.Bt
        src = ps[psb][:, 0:N]
        pk = ('ps', psb)
        RT = self.cb(C_RTA if kind == 'A' else C_RTB)
        R = DEBUG_R
        if kind == 'A':
            self.act(Bt[0][:, 0:N], src, AF.Copy, reads=[pk], writes=['b0'])
        else:
            g = self.small[:, gcol:gcol + 1]
            self.act(Bt[0][:, 0:N], src, AF.Identity, reads=[pk, 'small'], writes=['b0'], scale=g)
            if R >= 2:
                self.act(Bt[1][:, 0:N], src, AF.Square, reads=[pk], writes=['b1'])
        if R < 3:
            return
        rb = 6
        self.mm(ps[rb][:, 0:N], lhsT=RT, rhs=Bt[0][:, 0:N], start=True, stop=True,
                reads=['b0', 'cstb'], writes=[('ps', rb)], inc=True)
        if kind == 'B':
            mb = 7
            self.mm(ps[mb][:, 0:N], lhsT=self.cb(C_BONES), rhs=Bt[1][:, 0:N], start=True, stop=True,
                    reads=['b1', 'cstb'], writes=[('ps', mb)], inc=True)
        if R < 4:
            return
        cos = tab[:, 0, 0:N]
        sin = tab[:, 1, 0:N]
        if kind == 'A':
            self.tt(T[0][:, 0:N], src, cos, ALU.mult, reads=[pk, tabkey], writes=['t0'])
        else:
            self.stt(T[0][:, 0:N], src, g, cos, ALU.mult, ALU.mult, reads=[pk, tabkey, 'small'], writes=['t0'])
        if R < 5:
            return
        self.tt(T[1][:, 0:N], ps[rb][:, 0:N], sin, ALU.mult, reads=[('ps', rb), tabkey], writes=['t1'])
        if kind == 'A':
            self.tt(out_ap, T[0][:, 0:N], T[1][:, 0:N], ALU.add, reads=['t0', 't1'], writes=[out_key])
        else:
            self.tt(T[2][:, 0:N], T[0][:, 0:N], T[1][:, 0:N], ALU.add, reads=['t0', 't1'], writes=['t2'])
            if R < 6:
                return
            self.powact(T[3][:, 0:N], ps[7][:, 0:N], -0.5, [('ps', 7)], ['t3'], bias=self.cf(C_EPS6, 1), extra_reads=['cst'])
            if R < 7:
                return
            self.tt(out_ap, T[2][:, 0:N], T[3][:, 0:N], ALU.mult, reads=['t2', 't3'], writes=[out_key])

    def load_tab(self, src_ap):
        i = self.tabi
        n = src_ap.shape[-1]
        self.P.dma(self.tabs[i][:, :, 0:n], src_ap, writes=[('tab', i)])
        return self.tabs[i], ('tab', i)

    def xall(self, t0, n):
        if self.mode == 'F':
            return self.d_xallp[:, :, t0:t0 + n]
        return self.d_xall[:, :, t0:t0 + n]

    def xall_keys(self, t0, n):
        if self.mode == 'F':
            return [('xown', t0 // NTOK, (t0 % NTOK) // GS)]
        return []

    def xown_ap(self, n):
        if self.mode == 'F':
            return self.d_xallp[:, :, self.hf * NTOK + n * GS: self.hf * NTOK + (n + 1) * GS]
        return self.d_xown[:, :, n * GS:(n + 1) * GS]

    def xown_key(self, n):
        return ('xown', self.hf, n) if self.mode == 'F' else ('xown', n)

    def tabA_ap(self, c0, n):
        if self.mode == 'F':
            return self.d_tabAh[self.hf][:, :, c0:c0 + n]
        return self.d_tabA[:, :, c0:c0 + n]

    def tabBq_ap(self, n):
        if self.mode == 'F':
            t0 = self.hf * NTOK + n * GS
            return self.d_tabBk[:, :, t0:t0 + GS]
        return self.d_tabBq[:, :, n * GS:(n + 1) * GS]

    def kv_block(self, xsrc, xkey, N, kind, tab_ap, kdst, kkey, V, vt0, vkeyf):
        ps = self.ps
        tab, tkey = self.load_tab(tab_ap)
        for kc in range(8):
            self.mm(ps[0][:, 0:N], lhsT=self.wkv[:, kc, 0:128], rhs=xsrc(kc), start=(kc == 0), stop=(kc == 7),
                    reads=self.WKV_KEYS + [xkey], writes=[('ps', 0)], inc=(kc == 7))
        if DEBUG_SUB >= 2:
            self.rope_evac(0, N, kind, tab, tkey, kdst, kkey, gcol=S_GK)
        for t in range(N // 128 if DEBUG_SUB >= 3 else 0):
            pb = 1 + (t % 2)
            for kc in range(8):
                self.mm(ps[pb][:, 0:128], lhsT=xsrc(kc)[:, t * 128:(t + 1) * 128], rhs=self.wkv[:, kc, 128:256],
                        start=(kc == 0), stop=(kc == 7), reads=self.WKV_KEYS + [xkey], writes=[('ps', pb)], inc=(kc == 7))
            vt = vt0 + t
            self.P.op('act', lambda e, pb=pb, vt=vt: e.copy(
                out=V[:, vt, :, 0:64], in_=ps[pb][:, 0:128].rearrange("p (g d) -> p g d", g=2)),
                reads=[('ps', pb)], writes=[vkeyf(vt)])

    def kv_passes(self, li, do_b=True, do_a=True):
        P = self.P
        W = weight_offsets()
        if do_b:
            self.kv_pass_b(li)
        if do_a:
            self.kv_pass_a(li)

    def kv_pass_b(self, li):
        P = self.P
        W = weight_offsets()
        off, e = W['kvB']
        P.dma(self.arena[:, 0:4, :].rearrange("p a b -> p (a b)"), self.d_wbf[li, :, off:off + e], reads=[('wbf', li)], writes=self.WKV_KEYS)
        for tg in range(8):
            xb = self.xg[tg % 2]
            xk = ('xg', tg % 2)
            P.dma(xb[:, :, 0:512], self.xall(tg * 512, 512), reads=self.xall_keys(tg * 512, 512), writes=[xk])
            self.kv_block(lambda kc, xb=xb: xb[:, kc, 0:512], xk, 512, 'B',
                          self.d_tabBk[:, :, tg * 512:(tg + 1) * 512],
                          self.KB[:, tg * 512:(tg + 1) * 512], ('KB', tg), self.VB, tg * 4, lambda vt: ('VB', vt // 4))

    def kv_pass_a(self, li):
        P = self.P
        W = weight_offsets()
        off, e = W['kvA']
        P.dma(self.arena[:, 0:4, :].rearrange("p a b -> p (a b)"), self.d_wbf[li, :, off:off + e], reads=[('wbf', li)], writes=self.WKV_KEYS)
        for n in range(NG):
            xb = self.xg[n % 2]
            xk = ('xg', n % 2)
            P.dma(xb[:, :, 0:512], self.xown_ap(n), reads=[self.xown_key(n)], writes=[xk])
            self.kv_block(lambda kc, xb=xb: xb[:, kc, 0:512], xk, 512, 'A',
                          self.tabA_ap(128 + n * 512, 512),
                          self.KA[:, 128 + n * 512:128 + (n + 1) * 512], ('KA', 1 + n), self.VA, 1 + n * 4,
                          lambda vt: ('VA', vt))
        for side in range(2):
            xb = self.xg[side]
            xk = ('xg', side)
            g0 = 1920 if side == 0 else 2048
            P.dma(xb[:, :, 0:128], self.xall(g0, 128), reads=self.xall_keys(g0, 128), writes=[xk])
            tcol = 0 if side == 0 else 2176
            self.kv_block(lambda kc, xb=xb: xb[:, kc, 0:128], xk, 128, 'A',
                          self.tabA_ap(tcol, 128),
                          self.KA[:, tcol:tcol + 128], ('KA', 0 if side == 0 else 5), self.VA,
                          0 if side == 0 else 17, lambda vt: ('VA', vt))

    def ka_keys(self, J):
        if J == 0:
            return ('KA', 0)
        if J == 17:
            return ('KA', 5)
        return ('KA', 1 + (J - 1) // 4)

    def mem_kv(self, li):
        ps = self.ps
        self.P.dma(self.xg[1][:, :, 0:256], self.d_memT[:, :, :], writes=[('xg', 1)], queue='pool')
        wk, wkk = self.wget(li, 'xk')
        wk = wk.rearrange("p (k c) -> p k c", k=8)
        for h in range(4):
            pb = h % 2
            for kc in range(8):
                self.mm(ps[pb][:, 0:256], lhsT=wk[:, kc, h * 128:(h + 1) * 128], rhs=self.xg[1][:, kc, 0:256],
                        start=(kc == 0), stop=(kc == 7), reads=[wkk, ('xg', 1)], writes=[('ps', pb)], inc=(kc == 7))
            self.act(self.KmT[:, h, :], ps[pb][:, 0:256], AF.Copy, reads=[('ps', pb)], writes=['KmT'])
        wv, wvk = self.wget(li, 'xv')
        wv = wv.rearrange("p (k c) -> p k c", k=8)
        for mt in range(2):
            pb = 2 + mt
            for kc in range(8):
                self.mm(ps[pb][:, 0:512], lhsT=self.xg[1][:, kc, mt * 128:(mt + 1) * 128], rhs=wv[:, kc, :],
                        start=(kc == 0), stop=(kc == 7), reads=[wvk, ('xg', 1)], writes=[('ps', pb)], inc=(kc == 7))
            self.act(self.Vm[:, mt, :], ps[pb][:, 0:512], AF.Copy, reads=[('ps', pb)], writes=['Vm'])

    def attn_norm(self, ob, slot, sink_h=None):
        ps, T = self.ps, self.T
        ok = ('ps', ob)
        rd = T[4]
        if sink_h is None:
            self.powact(rd[64:65, :], ps[ob][64:65, :], -1.0, [ok], ['t4'])
        else:
            self.powact(rd[64:65, :], ps[ob][64:65, :], -1.0, [ok], ['t4'],
                        bias=self.sinkexp[64:65, sink_h:sink_h + 1], extra_reads=['sinkexp'])
        bb = 6
        self.mm(ps[bb][0:64, :], lhsT=self.cf(C_ONES, 64, slice(64, 65)), rhs=rd[64:65, :], start=True, stop=True,
                reads=['t4', 'cst'], writes=[('ps', bb)], inc=True)
        self.act(T[5][0:64, :], ps[ob][0:64, :], AF.Copy, reads=[ok], writes=['t5'])
        self.tt(self.arena[0:64, slot, :], T[5][0:64, :], ps[bb][0:64, :], ALU.mult,
                reads=['t5', ('ps', bb)], writes=[('ar', slot)])

    def ln_accum(self, c, n, hb):
        ps, T = self.ps, self.T
        zs = self.xT32[:, c, n * GS:(n + 1) * GS]
        zk = ('x32', c, n)
        self.stt(zs, zs, ALPHA, ps[hb][:, :], ALU.mult, ALU.add, reads=[('ps', hb), zk], writes=[zk])
        sq = T[6 + (c % 2)]
        sqk = 't6' if c % 2 == 0 else 't7'
        self.act(sq[:], zs, AF.Square, reads=[zk], writes=[sqk])
        ones = self.cf(C_ONESLN, 128)
        self.mm(ps[6][:, :], lhsT=ones, rhs=zs, start=(c == 0), stop=(c == 7), reads=[zk, 'cst'],
                writes=[('ps', 6)], inc=False)
        self.mm(ps[7][:, :], lhsT=ones, rhs=sq[:], start=(c == 0), stop=(c == 7), reads=[sqk, 'cst'],
                writes=[('ps', 7)], inc=True)

    def ln_finish(self, n, lncol, dst_fn):
        ps, T = self.ps, self.T
        self.act(T[0][:], ps[6][:, :], AF.Square, reads=[('ps', 6)], writes=['t0'])
        self.stt(T[1][:], ps[7][:, :], LN_EPS, T[0][:], ALU.add, ALU.subtract, reads=[('ps', 7), 't0'], writes=['t1'])
        self.powact(T[2][:], T[1][:], -0.5, ['t1'], ['t2'])
        self.stt(T[3][:], ps[6][:, :], -1.0, T[2][:], ALU.mult, ALU.mult, reads=[('ps', 6), 't2'], writes=['t3'])
        for c in range(8):
            zs = self.xT32[:, c, n * GS:(n + 1) * GS]
            zk = ('x32', c, n)
            a = T[4 + (c % 2)]
            ak = 't4' if c % 2 == 0 else 't5'
            self.tt(a[:], zs, T[2][:], ALU.mult, reads=[zk, 't2'], writes=[ak])
            self.tt(a[:], a[:], T[3][:], ALU.add, reads=[ak, 't3'], writes=[ak])
            g = self.small[:, lncol + c:lncol + c + 1]
            b = self.small[:, lncol + 8 + c:lncol + 8 + c + 1]
            self.act(zs, a[:], AF.Identity, reads=[ak, 'small'], writes=[zk], scale=g, bias=b)
            dst, dk = dst_fn(c)
            self.ts(dst, a[:], g, b, ALU.mult, ALU.add, reads=[ak, 'small'], writes=[dk])

    def mixer_group(self, li, n):
        P, ps, T, Bt = self.P, self.ps, self.T, self.Bt
        ar = self.arena
        xb = self.xg[0]
        xk = ('xg', 0)
        P.dma(xb[:, :, 0:512], self.xown_ap(n), reads=[self.xown_key(n)], writes=[xk])
        xs = lambda kc: xb[:, kc, 0:512]
        wq, wqk = self.wget(li, 'qB')
        wq = wq.rearrange("p (k c) -> p k c", k=8)
        tab, tkey = self.load_tab(self.tabBq_ap(n))
        SB = [1, 2, 3]
        pending = None
        QB = [(Bt[2], 'b2'), (Bt[3], 'b3')]

        def prep_b(i):
            for kc in range(8):
                self.mm(ps[0][:, :], lhsT=wq[:, kc, i * 128:(i + 1) * 128], rhs=xs(kc), start=(kc == 0), stop=(kc == 7),
                        reads=[wqk, xk], writes=[('ps', 0)], inc=(kc == 7))
            self.rope_evac(0, 512, 'B', tab, tkey, QB[i % 2][0][:], QB[i % 2][1], gcol=S_GQ)
        prep_b(0)
        for i in range(4):
            qb, qbk = QB[i % 2]

            for gi in range(2):
                P.op('dve', lambda e, gi=gi, qb=qb: e.tensor_copy(out=self.qpad[gi][gi * 64:(gi + 1) * 64, :],
                                                                   in_=qb[gi * 64:(gi + 1) * 64, :]),
                     reads=[qbk], writes=[('qp', gi)])

            def st_b(g, kt, qb=qb, qbk=qbk):
                sb = SB[kt % 3]
                self.mm(ps[sb][:, :], lhsT=self.KB[:, kt * 128:(kt + 1) * 128],
                        rhs=self.qpad[g][:, :], start=True, stop=True,
                        reads=[('KB', kt // 4), ('qp', g)], writes=[('ps', sb)], inc=True)
            for g in range(2):
                ob = 4 if g == 0 else 5
                st_b(g, 0)
                st_b(g, 1)
                for kt in range(32):
                    sb = SB[kt % 3]
                    pt = self.pt[kt % 3]
                    pk = ('pt', kt % 3)
                    self.act(pt[:], ps[sb][:, :], AF.Exp, reads=[('ps', sb)], writes=[pk], scale=0.125)
                    if kt + 2 < 32:
                        st_b(g, kt + 2)
                    self.mm(ps[ob][0:65, :], lhsT=self.VB[:, kt, g, :], rhs=pt[:], start=(kt == 0), stop=(kt == 31),
                            reads=[('VB', kt // 4), pk], writes=[('ps', ob)], inc=(kt == 31))
                    if kt == 6 and pending is not None:
                        self.attn_norm(*pending)
                        pending = None
                    if g == 1 and kt == 12 and i + 1 < 4:
                        prep_b(i + 1)
                if pending is not None:
                    self.attn_norm(*pending)
                pending = (ob, 8 + g * 4 + i, None)
        if pending is not None:
            self.attn_norm(*pending)
            pending = None
        wq, wqk = self.wget(li, 'qA')
        wq = wq.rearrange("p (k c) -> p k c", k=8)
        tab, tkey = self.load_tab(self.tabA_ap(128 + n * GS, GS))
        def prep_a(i):
            for kc in range(8):
                self.mm(ps[0][:, :], lhsT=wq[:, kc, i * 128:(i + 1) * 128], rhs=xs(kc), start=(kc == 0), stop=(kc == 7),
                        reads=[wqk, xk], writes=[('ps', 0)], inc=(kc == 7))
            self.rope_evac(0, 512, 'A', tab, tkey, QB[i % 2][0][:], QB[i % 2][1])
        prep_a(0)
        for i in range(4):
            qa, qak = QB[i % 2]

            def st_a(it, qa=qa, qak=qak):
                g, jb = it // 4, it % 4
                J = n * 4 + jb
                sb = SB[it % 3]
                for r in range(3):
                    self.mm(ps[sb][:, r * 128:(r + 1) * 128],
                            lhsT=self.KA[g * 64:(g + 1) * 64, (J + r) * 128:(J + r + 1) * 128],
                            rhs=qa[g * 64:(g + 1) * 64, jb * 128:(jb + 1) * 128], start=True, stop=True,
                            reads=[self.ka_keys(J + r), qak], writes=[('ps', sb)], inc=(r == 2))
            st_a(0)
            st_a(1)
            for it in range(8):
                g, jb = it // 4, it % 4
                J = n * 4 + jb
                ob = 4 if g == 0 else 5
                sb = SB[it % 3]
                e32 = T[6 + (it % 2)]
                ek = 't6' if it % 2 == 0 else 't7'
                self.act(e32[:, 0:384], ps[sb][:, 0:384], AF.Exp, reads=[('ps', sb)], writes=[ek], scale=0.125)
                first_blk = (n == 0 and jb == 0)
                last_blk = (n == NG - 1 and jb == 3)
                if first_blk and (self.mode != 'F' or self.hf == 0):
                    mcol = C_MASK + 384
                elif last_blk and (self.mode != 'F' or self.hf == 1):
                    mcol = C_MASK + 768
                else:
                    mcol = C_MASK
                pt = self.pt[it % 3]
                pk = ('pt', it % 3)
                self.tt(pt[:, 0:384], e32[:, 0:384], self.cf(mcol, 384), ALU.mult, reads=[ek, 'cst'], writes=[pk])
                if it + 2 < 8:
                    st_a(it + 2)
                for r in range(3):
                    self.mm(ps[ob][0:65, jb * 128:(jb + 1) * 128], lhsT=self.VA[:, J + r, g, :],
                            rhs=pt[:, r * 128:(r + 1) * 128], start=(r == 0), stop=(r == 2),
                            reads=[('VA', J + r), pk], writes=[('ps', ob)], inc=(r == 2))
                if it == 1 and pending is not None:
                    self.attn_norm(*pending)
                    pending = None
                if it == 3 and i + 1 < 4:
                    prep_a(i + 1)
                if jb == 3:
                    if pending is not None:
                        self.attn_norm(*pending)
                    pending = (ob, g * 4 + i, g * 4 + i)
        if pending is not None:
            self.attn_norm(*pending)
            pending = None
        wu, wuk = self.wget(li, 'wu')
        wu = wu.rearrange("p (k c) -> p k c", k=8)
        for c in range(4):
            pb = c % 2
            for kc in range(8):
                self.mm(ps[pb][:, :], lhsT=wu[:, kc, c * 128:(c + 1) * 128], rhs=xs(kc), start=(kc == 0), stop=(kc == 7),
                        reads=[wuk, xk], writes=[('ps', pb)], inc=(kc == 7))
            self.act(self.uT[:, c, :], ps[pb][:, :], AF.Gelu, reads=[('ps', pb)], writes=[('ar', 24 + c)])
        wv, wvk = self.wget(li, 'wv')
        wv = wv.rearrange("p (k c) -> p k c", k=8)
        for t in range(4):
            pb = 2 + (t % 2)
            for kc in range(8):
                self.mm(ps[pb][:, :], lhsT=xb[:, kc, t * 128:(t + 1) * 128], rhs=wv[:, kc, :], start=(kc == 0),
                        stop=(kc == 7), reads=[wvk, xk], writes=[('ps', pb)], inc=(kc == 7))
            v32 = T[0 + (t % 2)]
            vk = 't0' if t % 2 == 0 else 't1'
            self.act(v32[:], ps[pb][:, :], AF.Gelu, reads=[('ps', pb)], writes=[vk])
            st = self.mv[:, 0:6]
            P.op('dve', lambda e, v32=v32: e.bn_stats(out=self.mv[:, 0:6], in_=v32[:]), reads=[vk], writes=['mv6'])
            P.op('dve', lambda e: e.bn_aggr(out=self.mv[:, 8:10], in_=self.mv[:, 0:6]), reads=['mv6'], writes=['mv2'])
            self.powact(self.mv[:, 10:11], self.mv[:, 9:10], -0.5, ['mv2'], ['mvr'], bias=self.cf(C_EPS5, 1), extra_reads=['cst'])
            self.ts(T[2][:], v32[:], self.mv[:, 8:9], self.mv[:, 10:11], ALU.subtract, ALU.mult,
                    reads=[vk, 'mv2', 'mvr'], writes=['t2'])
            self.tt(T[3][:], T[2][:], self.small[:, S_CLG:S_CLG + 512], ALU.mult, reads=['t2', 'small'], writes=['t3'])
            vc = Bt[3]
            self.tt(vc[:], T[3][:], self.small[:, S_CLB:S_CLB + 512], ALU.add, reads=['t3', 'small'], writes=['b3'])
            for g4 in range(4):
                self.mm(ps[4 + g4][:, t * 128:(t + 1) * 128], lhsT=vc[:, g4 * 128:(g4 + 1) * 128], rhs=self.wsT[:, g4, :],
                        start=True, stop=True, reads=['b3', 'wsT'], writes=[('ps', 4 + g4)], inc=(g4 == 3))
        for g4 in range(4):
            bsv = self.small[:, S_BS + g4 * 128:S_BS + (g4 + 1) * 128]
            a = T[4 + (g4 % 2)]
            ak = 't4' if g4 % 2 == 0 else 't5'
            for t in range(4):
                self.tt(a[:, t * 128:(t + 1) * 128], ps[4 + g4][:, t * 128:(t + 1) * 128], bsv, ALU.add,
                        reads=[('ps', 4 + g4), 'small'], writes=[ak])
            self.tt(ar[:, 16 + g4, :], a[:], self.uT[:, g4, :], ALU.mult, reads=[ak, ('ar', 24 + g4)], writes=[('ar', 16 + g4)])
        if DEBUG_M < 4:
            return
        acc = [T[0], T[1], T[2], T[3]]
        acck = ['t0', 't1', 't2', 't3']
        for cq in range(2):
            for nb in range(3):
                wg, wgk = self.wget(li, f'g{nb}{cq}')
                wg = wg.rearrange("p (k c) -> p k c", k=8)
                wb, wbk = self.wget(li, f'br{nb}{cq}')
                if nb < 2:
                    wb = wb.rearrange("p (k c) -> p k c", k=8)
                else:
                    wb = wb.rearrange("p (k c) -> p k c", k=4)
                for cc in range(4):
                    c = cq * 4 + cc
                    gb = cc % 2
                    for kc in range(8):
                        self.mm(ps[gb][:, :], lhsT=wg[:, kc, cc * 128:(cc + 1) * 128], rhs=xs(kc), start=(kc == 0),
                                stop=(kc == 7), reads=[wgk, xk], writes=[('ps', gb)], inc=(kc == 7))
                    sg = T[4 + (cc % 2)]
                    sgk = 't4' if cc % 2 == 0 else 't5'
                    self.act(sg[:], ps[gb][:, :], AF.Sigmoid, reads=[('ps', gb), 'small'], writes=[sgk],
                             bias=self.small[:, S_BG + nb * 8 + c:S_BG + nb * 8 + c + 1])
                    bb = 2 + (cc % 2)
                    if nb < 2:
                        for h in range(8):
                            self.mm(ps[bb][:, :], lhsT=wb[0:64, h, cc * 128:(cc + 1) * 128], rhs=ar[0:64, nb * 8 + h, :],
                                    start=(h == 0), stop=(h == 7), reads=[wbk, ('ar', nb * 8 + h)], writes=[('ps', bb)],
                                    inc=(h == 7))
                    else:
                        for kc in range(4):
                            self.mm(ps[bb][:, :], lhsT=wb[:, kc, cc * 128:(cc + 1) * 128], rhs=ar[:, 16 + kc, :],
                                    start=(kc == 0), stop=(kc == 3), reads=[wbk, ('ar', 16 + kc)], writes=[('ps', bb)],
                                    inc=(kc == 3))
                    mk = acck[cc]
                    ma = acc[cc][:]
                    if nb == 0:
                        self.tt(ma, ps[bb][:, :], sg[:], ALU.mult, reads=[('ps', bb), sgk], writes=[mk])
                    else:
                        pr = T[6 + (cc % 2)]
                        prk = 't6' if cc % 2 == 0 else 't7'
                        self.tt(pr[:], ps[bb][:, :], sg[:], ALU.mult, reads=[('ps', bb), sgk], writes=[prk])
                        if nb == 1:
                            self.tt(ma, ma, pr[:], ALU.add, reads=[mk, prk], writes=[mk])
                        else:
                            self.tt(ar[:, 20 + c, :], ma, pr[:], ALU.add, reads=[mk, prk], writes=[('ar', 20 + c)])
        if DEBUG_M < 5:
            return
        for cq in range(2):
            wm, wmk = self.wget(li, f'mo{cq}')
            wm = wm.rearrange("p (k c) -> p k c", k=8)
            for cc in range(4):
                c = cq * 4 + cc
                hb = c % 2
                for kc in range(8):
                    self.mm(ps[hb][:, :], lhsT=wm[:, kc, cc * 128:(cc + 1) * 128], rhs=ar[:, 20 + kc, :], start=(kc == 0),
                            stop=(kc == 7), reads=[wmk, ('ar', 20 + kc)], writes=[('ps', hb)], inc=(kc == 7))
                self.ln_accum(c, n, hb)
        x1 = self.xg[1]
        self.ln_finish(n, S_LN + 0, lambda c: (x1[:, c, 0:512], ('xg', 1)))
        if DEBUG_M < 6:
            return
        wq, wqk = self.wget(li, 'xq')
        wq = wq.rearrange("p (k c) -> p k c", k=8)
        sc = 1.0 / np.sqrt(128.0)
        for h in range(4):
            for kc in range(8):
                self.mm(ps[0][:, :], lhsT=wq[:, kc, h * 128:(h + 1) * 128], rhs=x1[:, kc, 0:512], start=(kc == 0),
                        stop=(kc == 7), reads=[wqk, ('xg', 1)], writes=[('ps', 0)], inc=(kc == 7))
            qx = Bt[0]
            self.act(qx[:], ps[0][:, :], AF.Copy, reads=[('ps', 0)], writes=['b0'])
            for mt in range(2):
                sb = 1 + mt
                self.mm(ps[sb][:, :], lhsT=self.KmT[:, h, mt * 128:(mt + 1) * 128], rhs=qx[:], start=True, stop=True,
                        reads=['KmT', 'b0'], writes=[('ps', sb)], inc=True)
                self.act(self.pt[mt][:], ps[sb][:, :], AF.Exp, reads=[('ps', sb)], writes=[('pt', mt)], scale=float(sc))
            for mt in range(2):
                self.mm(ps[3][:, :], lhsT=self.Vm[:, mt, h * 128:(h + 1) * 128], rhs=self.pt[mt][:], start=(mt == 0),
                        stop=(mt == 1), reads=['Vm', ('pt', mt)], writes=[('ps', 3)], inc=(mt == 1))
            for mt in range(2):
                self.mm(ps[4][:, :], lhsT=self.cb(C_ONES), rhs=self.pt[mt][:], start=(mt == 0),
                        stop=(mt == 1), reads=['cstb', ('pt', mt)], writes=[('ps', 4)], inc=(mt == 1))
            self.powact(T[0][:], ps[4][:, :], -1.0, [('ps', 4)], ['t0'])
            self.tt(ar[:, h, :], ps[3][:, :], T[0][:], ALU.mult, reads=[('ps', 3), 't0'], writes=[('ar', h)])
        for cq in range(2):
            wo, wok = self.wget(li, f'xo{cq}')
            wo = wo.rearrange("p (k c) -> p k c", k=4)
            for cc in range(4):
                c = cq * 4 + cc
                hb = c % 2
                for kc in range(4):
                    self.mm(ps[hb][:, :], lhsT=wo[:, kc, cc * 128:(cc + 1) * 128], rhs=ar[:, kc, :], start=(kc == 0),
                            stop=(kc == 3), reads=[wok, ('ar', kc)], writes=[('ps', hb)], inc=(kc == 3))
                self.ln_accum(c, n, hb)
        x2 = self.xg[0]
        self.ln_finish(n, S_LN + 16, lambda c: (x2[:, c, 0:512], ('xg', 0)))
        if self.mode == 'F':
            t0 = self.hf * NTOK + n * GS
            P.dma(self.d_x2all[:, :, t0:t0 + GS], x2[:, :, 0:512], reads=[('xg', 0)], writes=[('x2own', self.hf, n)])
        else:
            P.dma(self.d_x2own[:, :, n * GS:(n + 1) * GS], x2[:, :, 0:512], reads=[('xg', 0)], writes=[('x2own', n)])

    def ffn_group(self, li, n, last):
        P, ps, T = self.P, self.ps, self.T
        ar = self.arena
        xb = self.xg[n % 2]
        xk = ('xg', n % 2)
        if self.mode == 'F':
            T0 = self.hf * NTOK + n * GS
            lo, hi, c0 = T0 - 1, T0 + GS + 1, 0
            if lo < 0:
                P.op('dve', lambda e, xb=xb: e.memset(xb[:, :, 0:1], 0.0), writes=[xk])
                lo, c0 = 0, 1
            if hi > 4096:
                P.op('dve', lambda e, xb=xb: e.memset(xb[:, :, 513:514], 0.0), writes=[xk])
                hi = 4096
            rk = [('x2own', h, j) for h in range(2) for j in range(NG)]
            P.dma(xb[:, :, c0:c0 + (hi - lo)], self.d_x2all[:, :, lo:hi], reads=rk, writes=[xk])
        else:
            lo = n * GS - 1
            hi = n * GS + GS + 1
            c0 = 0
            if lo < 0:
                P.dma(xb[:, :, 0:1], self.halo_s[:, :, 0:1], reads=['halo_s'], writes=[xk], allow_slow_non_contiguous=True)
                lo = 0
                c0 = 1
            if hi > NTOK:
                P.dma(xb[:, :, 513:514], self.halo_s[:, :, 1:2], reads=['halo_s'], writes=[xk], allow_slow_non_contiguous=True)
                hi = NTOK
            P.dma(xb[:, :, c0:c0 + (hi - lo)], self.d_x2own[:, :, lo:hi], reads=[('x2own', j) for j in range(NG)], writes=[xk])
        cvk = lambda j, col: self.small[:, S_CV + 44 * j + col: S_CV + 44 * j + col + 1]
        for jq in range(6):
            nch = 4 if jq < 5 else 2
            wa, wak = self.wget(li, f'ua{jq}')
            wa = wa.rearrange("p (k c) -> p k c", k=8)
            wb, wbk = self.wget(li, f'ub{jq}')
            wb = wb.rearrange("p (k c) -> p k c", k=8)
            for cc in range(nch):
                j = jq * 4 + cc
                hc = []
                for half, (w, wk) in enumerate(((wa, wak), (wb, wbk))):
                    col = j + 22 * half
                    pb = 2 * half
                    for kc in range(8):
                        self.mm(ps[pb][:, :], lhsT=w[:, kc, cc * 128:(cc + 1) * 128], rhs=xb[:, kc, 1:513], start=(kc == 0),
                                stop=(kc == 7), reads=[wk, xk], writes=[('ps', pb)], inc=(kc == 7))
                    for kc in range(8):
                        self.mm(ps[pb + 1][:, 0:2], lhsT=w[:, kc, cc * 128:(cc + 1) * 128], rhs=xb[:, kc, 0:514:513],
                                start=(kc == 0), stop=(kc == 7), reads=[wk, xk], writes=[('ps', pb + 1)], inc=(kc == 7))
                    he = self.hext[half]
                    hk = self.hext_keys[half]
                    self.act(he[:, 1:513], ps[pb][:, :], AF.Copy, reads=[('ps', pb)], writes=hk)
                    self.act(he[:, 0:514:513], ps[pb + 1][:, 0:2], AF.Copy, reads=[('ps', pb + 1)], writes=hk)
                    a = T[2 * half]
                    ak = f't{2 * half}'
                    self.ts(a[:], he[:, 0:512], cvk(0, col), cvk(3, col), ALU.mult, ALU.add, reads=hk + ['small'], writes=[ak])
                    self.stt(a[:], he[:, 1:513], cvk(1, col), a[:], ALU.mult, ALU.add, reads=hk + ['small', ak], writes=[ak])
                    b2 = T[2 * half + 1]
                    bk = f't{2 * half + 1}'
                    self.stt(b2[:], he[:, 2:514], cvk(2, col), a[:], ALU.mult, ALU.add, reads=hk + ['small', ak], writes=[bk])
                    hc.append((b2, bk))
                ga = T[4 + (j % 2)]
                gk = 't4' if j % 2 == 0 else 't5'
                self.act(ga[:], hc[0][0][:], AF.Gelu, reads=[hc[0][1]], writes=[gk])
                self.tt(ar[:, j, :], ga[:], hc[1][0][:], ALU.mult, reads=[gk, hc[1][1]], writes=[('ar', j)])
        for c in range(8):
            wd, wdk = self.wget(li, f'dn{c}')
            wd = wd.rearrange("p (k c) -> p k c", k=22)
            hb = 4 + (c % 2)
            for kc in range(22):
                self.mm(ps[hb][:, :], lhsT=wd[:, kc, :], rhs=ar[:, kc, :], start=(kc == 0), stop=(kc == 21),
                        reads=[wdk, ('ar', kc)], writes=[('ps', hb)], inc=(kc == 21))
            self.ln_accum(c, n, hb)
        if self.mode == 'F':
            ob = self.KB[:].rearrange("p (c t) -> p c t", c=8)
            okeys = [('KB', c) for c in range(8)]
        else:
            ob = self.xo_buf[:]
            okeys = ['xo_buf'] * 8
        self.ln_finish(n, S_LN + 32, lambda c: (ob[:, c, :], okeys[c]))
        if not last:
            P.dma(self.xown_ap(n), ob, reads=list(set(okeys)), writes=[self.xown_key(n)])

    def cast_layer(self, li):
        tot = weight_offsets()['_tot']
        nch = 16
        step = (tot + nch - 1) // nch
        for j in range(nch):
            a, b = j * step, min(tot, (j + 1) * step)
            self.P.dma(self.d_wbf[li, :, a:b], self.d_wblk[li, :, a:b], writes=[('wbf', li)], queue='pool')

    def switch_half(self, hf, first_touch):
        P = self.P
        if self.resident == hf:
            self.hf = hf
            return
        if self.resident is not None:
            r = self.resident
            for c in range(8):
                P.dma(self.d_park[r, :, c, :], self.xT32[:, c, :], reads=[('x32', c, n) for n in range(NG)],
                      writes=[('park', r, c)])
        src = self.d_xT32h if first_touch else self.d_park
        for c in range(8):
            P.dma(self.xT32[:, c, :], src[hf, :, c, :], reads=([] if first_touch else [('park', hf, c)]),
                  writes=[('x32', c, n) for n in range(NG)])
        self.resident = hf
        self.hf = hf

    def build(self):
        P = self.P
        assert self.mode == 'F'
        nL = len(self.layers)
        self.xo_buf = None
        self.resident = None
        self.plan_weights()
        self.load_common()
        self.cast_layer(0)
        for hf in range(2):
            self.switch_half(hf, True)
            for n in range(NG):
                xb = self.xg[n % 2]
                for c in range(8):
                    if c % 2:
                        P.op('act', lambda e, c=c, n=n, xb=xb: e.copy(out=xb[:, c, 0:512], in_=self.xT32[:, c, n * GS:(n + 1) * GS]),
                             reads=[('x32', c, n)], writes=[('xg', n % 2)])
                    else:
                        P.op('dve', lambda e, c=c, n=n, xb=xb: e.tensor_copy(out=xb[:, c, 0:512], in_=self.xT32[:, c, n * GS:(n + 1) * GS]),
                             reads=[('x32', c, n)], writes=[('xg', n % 2)])
                P.dma(self.xown_ap(n), xb[:, :, 0:512], reads=[('xg', n % 2)], writes=[self.xown_key(n)])
        P.op('dve', lambda e: e.memset(self.VB[:].rearrange("p a b c -> p (a b c)"), 1.0),
             writes=[('VB', i) for i in range(8)])
        P.op('dve', lambda e: e.memset(self.VA[:].rearrange("p a b c -> p (a b c)"), 1.0),
             writes=[('VA', i) for i in range(18)])
        for gi in range(2):
            P.op('dve', lambda e, gi=gi: e.memset(self.qpad[gi][:], 0.0), writes=[('qp', gi)])
        for li in range(nL):
            if li > 0:
                P.new_epoch()
            self.load_layer_small(li)
            off, e = weight_offsets()['ws']
            P.dma(self.wsT[:].rearrange("p g i -> p (g i)"), self.d_wbf[li, :, off:off + e], reads=[('wbf', li)], writes=['wsT'])
            if li + 1 < nL:
                self.cast_layer(li + 1)
            self.kv_pass_b(li)
            self.mem_kv(li)
            order = [1, 0]
            for hf in order:
                self.switch_half(hf, False)
                self.kv_pass_a(li)
                for n in range(NG):
                    self.mixer_group(li, n)
            for hf in [0, 1]:
                self.switch_half(hf, False)
                for n in range(NG):
                    self.ffn_group(li, n, last=(li == nL - 1))
                if li == nL - 1:
                    for c in range(8):
                        P.dma(self.d_yh[hf, :, c, :], self.xT32[:, c, :], reads=[('x32', c, n) for n in range(NG)],
                              writes=[('y', hf, c)])
        P.finish()
        P.emit()
        return self.nc


_CACHE = {}


def _get_prog(mode, nl):
    key = (mode, nl)
    if key not in _CACHE:
        b = Builder(mode, list(range(nl)))
        _CACHE[key] = b.build()
    return _CACHE[key]


def _bf(a):
    return np.ascontiguousarray(a).view(ml_dtypes.bfloat16) if a.dtype == np.uint16 else a


def kernel(**inp):
    inp = {k: np.asarray(v, dtype=np.float32) for k, v in inp.items()}
    x = inp['x']
    mem = inp['mem']
    cores = list(range(8))
    W = weight_offsets()
    wblk = np.zeros((DEPTH, 128, W['_tot']), np.float32)
    small = np.zeros((DEPTH, 128, NS), np.float32)
    for l in range(DEPTH):
        for nm, a in weight_blocks(inp, l).items():
            off, e = W[nm]
            wblk[l, :, off:off + e] = a
        small[l] = small_params(inp, l)
    tabs = [rope_tables(hf) for hf in range(2)]
    tabA = np.ascontiguousarray(np.stack([tabs[0][0], tabs[1][0]], 0))
    tabBk = tabs[0][2]
    cst = constants(0)
    ins = []
    for c in cores:
        b = c % 4
        xT = np.ascontiguousarray(x[b].T.reshape(8, 128, 2, NTOK).transpose(2, 1, 0, 3))
        memT = np.ascontiguousarray(mem[b].T.reshape(8, 128, 256).transpose(1, 0, 2))
        ins.append({"xT32": xT, "wblk": wblk, "small": small, "cst": cst, "tabA": tabA, "tabBk": tabBk, "memT": memT})
    nc = _get_prog('F', DEPTH)
    res = run_bass_kernel_spmd(nc, ins, core_ids=cores)
    out = np.zeros((4, 4096, 1024), np.float32)
    for b in range(4):
        y = np.asarray(res.results[b]["y"])
        for hf in range(2):
            yt = y[hf].transpose(1, 0, 2).reshape(1024, NTOK)
            out[b, hf * NTOK:(hf + 1) * NTOK, :] = yt.T
    return out
```
